# Optimizing a Trainium2 kernel written in Bass

```python
import math
import jax, jax.numpy as jnp
from jax import lax
import numpy as np

D_MODEL = 1024
BATCH = 4
SEQ = 8192
DEPTH = 2

MEM_LEN = 256
MIX_W = D_MODEL
GROUP_W = MIX_W // 4
N_FOX_HEADS = 4
N_SB_HEADS = 4
N_MLA_HEADS = 4
N_MEM_HEADS = 4
HEAD_DIM = GROUP_W // 4
MLA_Q_RANK = 256
MLA_KV_RANK = 128
MLA_NOPE = HEAD_DIM
MLA_ROPE = 32
MLA_V = GROUP_W // N_MLA_HEADS
ROPE_THETA = 10000.0
BLOCK_Q = 128
LN_EPS = 1e-5
RMS_EPS = 1e-6
FOX_FORGET_BIAS_INIT = 3.0
DEEPNORM_ALPHA = (2 * DEPTH) ** 0.25
DEEPNORM_BETA = (8 * DEPTH) ** -0.25
SPLIT_SIZES = (GROUP_W, GROUP_W, GROUP_W, N_FOX_HEADS,
               GROUP_W, GROUP_W, GROUP_W,
               MLA_Q_RANK, MLA_KV_RANK, MLA_ROPE,
               GROUP_W,
               MIX_W)
IN_COLS = sum(SPLIT_SIZES)

kernel_name = 'hybrid_fox_stickbreak_mla_memory_deepnorm'


def _layer_norm(x, g, b):
    xf = x.astype(jnp.float32)
    mu = jnp.mean(xf, axis=-1, keepdims=True)
    var = jnp.mean(jnp.square(xf - mu), axis=-1, keepdims=True)
    y = (xf - mu) * lax.rsqrt(var + LN_EPS) * g.astype(jnp.float32) + b.astype(jnp.float32)
    return y.astype(x.dtype)


def _rms_norm(x, g):
    xf = x.astype(jnp.float32)
    y = xf * lax.rsqrt(jnp.mean(jnp.square(xf), axis=-1, keepdims=True) + RMS_EPS)
    return (y * g.astype(jnp.float32)).astype(x.dtype)


def _heads(t, n):
    b, s, _ = t.shape
    return t.reshape(b, s, n, -1).transpose(0, 2, 1, 3)


def _merge(t):
    b, h, s, d = t.shape
    return t.transpose(0, 2, 1, 3).reshape(b, s, h * d)


def _rope(x, positions):
    half = x.shape[-1] // 2
    inv_freq = ROPE_THETA ** (-jnp.arange(half, dtype=jnp.float32) / half)
    ang = positions.astype(jnp.float32)[:, None] * inv_freq[None, :]
    ang = ang.reshape((ang.shape[0],) + (1,) * (x.ndim - 3) + (half,))
    cos, sin = jnp.cos(ang), jnp.sin(ang)
    xf = x.astype(jnp.float32)
    x1, x2 = xf[..., :half], xf[..., half:]
    return jnp.concatenate([x1 * cos - x2 * sin, x1 * sin + x2 * cos], axis=-1).astype(x.dtype)


def _sweep_query_blocks(block_fn, per_query):
    b, h, s = per_query[0].shape[:3]
    nb = s // BLOCK_Q
    blocks = tuple(jnp.moveaxis(a.reshape((b, h, nb, BLOCK_Q) + a.shape[3:]), 2, 0) for a in per_query)
    out = lax.map(lambda args: block_fn(args[0], *args[1:]), (jnp.arange(nb),) + blocks)
    out = jnp.moveaxis(out, 0, 2)
    return out.reshape(b, h, s, out.shape[-1])


def _causal_softmax_attention(q, k, v, scale, log_forget_cum=None):
    key_pos = jnp.arange(k.shape[2])

    def block(i, qb, *fq):
        q_pos = i * BLOCK_Q + jnp.arange(BLOCK_Q)
        logits = jnp.einsum('bhqd,bhkd->bhqk', qb, k).astype(jnp.float32) * scale
        if log_forget_cum is not None:
            logits = logits + fq[0][..., :, None] - log_forget_cum[:, :, None, :]
        mask = key_pos[None, :] <= q_pos[:, None]
        probs = jax.nn.softmax(jnp.where(mask, logits, -jnp.inf), axis=-1)
        return jnp.einsum('bhqk,bhkd->bhqd', probs.astype(v.dtype), v)

    per_query = (q,) if log_forget_cum is None else (q, log_forget_cum)
    return _sweep_query_blocks(block, per_query)


def _stick_breaking_attention(q, k, v, scale):
    key_pos = jnp.arange(k.shape[2])

    def block(i, qb):
        q_pos = i * BLOCK_Q + jnp.arange(BLOCK_Q)
        valid = key_pos[None, :] < q_pos[:, None]
        z = jnp.einsum('bhqd,bhkd->bhqk', qb, k).astype(jnp.float32) * scale
        log_keep = jnp.where(valid, jax.nn.log_sigmoid(-z), 0.0)
        log_tail = lax.cumsum(log_keep, axis=3, reverse=True) - log_keep
        w = jnp.where(valid, jnp.exp(jax.nn.log_sigmoid(z) + log_tail), 0.0)
        return jnp.einsum('bhqk,bhkd->bhqd', w.astype(v.dtype), v)

    return _sweep_query_blocks(block, (q,))


def setup_inputs(seed: int = 0) -> dict:
    key = jax.random.key(seed)
    ks = jax.random.split(key, 18)
    f32 = jnp.float32
    nrm = lambda k, shape: jax.random.normal(k, shape, f32)
    return {
        'x': nrm(ks[0], (BATCH, SEQ, D_MODEL)),
        'mem': nrm(ks[1], (BATCH, MEM_LEN, D_MODEL)),
        'ln_in_g': 1.0 + 0.02 * nrm(ks[2], (D_MODEL,)),
        'ln_in_b': 0.02 * nrm(ks[3], (D_MODEL,)),
        'mem_ln_g': 1.0 + 0.02 * nrm(ks[4], (D_MODEL,)),
        'mem_ln_b': 0.02 * nrm(ks[5], (D_MODEL,)),
        'w_in': nrm(ks[6], (DEPTH, D_MODEL, IN_COLS)) * D_MODEL ** -0.5,
        'b_forget': FOX_FORGET_BIAS_INIT + 0.1 * nrm(ks[7], (DEPTH, N_FOX_HEADS)),
        'mla_q_norm_g': 1.0 + 0.02 * nrm(ks[8], (DEPTH, MLA_Q_RANK)),
        'w_mla_q_up': nrm(ks[9], (DEPTH, MLA_Q_RANK, N_MLA_HEADS * (MLA_NOPE + MLA_ROPE))) * MLA_Q_RANK ** -0.5,
        'mla_kv_norm_g': 1.0 + 0.02 * nrm(ks[10], (DEPTH, MLA_KV_RANK)),
        'w_mla_kv_up': nrm(ks[11], (DEPTH, MLA_KV_RANK, N_MLA_HEADS * (MLA_NOPE + MLA_V))) * MLA_KV_RANK ** -0.5,
        'w_mem_kv': nrm(ks[12], (DEPTH, D_MODEL, 2 * GROUP_W)) * D_MODEL ** -0.5,
        'w_out': nrm(ks[13], (DEPTH, MIX_W, D_MODEL)) * (MIX_W ** -0.5 * DEEPNORM_BETA),
        'ln_g': 1.0 + 0.02 * nrm(ks[14], (DEPTH, D_MODEL)),
        'ln_b': 0.02 * nrm(ks[15], (DEPTH, D_MODEL)),
    }


def reference(x, mem, ln_in_g, ln_in_b, mem_ln_g, mem_ln_b, w_in, b_forget,
              mla_q_norm_g, w_mla_q_up, mla_kv_norm_g, w_mla_kv_up, w_mem_kv,
              w_out, ln_g, ln_b):
    b, s, _ = x.shape
    positions = jnp.arange(s)
    offsets = [int(o) for o in np.cumsum(SPLIT_SIZES)[:-1]]
    head_scale = HEAD_DIM ** -0.5
    mla_scale = (MLA_NOPE + MLA_ROPE) ** -0.5

    h_res = _layer_norm(x, ln_in_g, ln_in_b)
    mem_n = _layer_norm(mem, mem_ln_g, mem_ln_b)

    for l in range(DEPTH):
        proj = jnp.einsum('bsd,dc->bsc', h_res, w_in[l])
        (fq, fk, fv, f_logit, sq, sk, sv, c_q, c_kv, k_rot, mq, gate) = jnp.split(proj, offsets, axis=-1)

        log_f = jax.nn.log_sigmoid((f_logit + b_forget[l]).astype(jnp.float32))
        f_cum = jnp.cumsum(log_f, axis=1).transpose(0, 2, 1)
        out_fox = _causal_softmax_attention(_heads(fq, N_FOX_HEADS), _heads(fk, N_FOX_HEADS),
                                            _heads(fv, N_FOX_HEADS), head_scale, f_cum)

        out_sb = _stick_breaking_attention(_heads(sq, N_SB_HEADS), _heads(sk, N_SB_HEADS),
                                           _heads(sv, N_SB_HEADS), head_scale)

        q_mla = jnp.einsum('bsr,rc->bsc', _rms_norm(c_q, mla_q_norm_g[l]), w_mla_q_up[l])
        q_mla = q_mla.reshape(b, s, N_MLA_HEADS, MLA_NOPE + MLA_ROPE)
        q_full = jnp.concatenate([q_mla[..., :MLA_NOPE], _rope(q_mla[..., MLA_NOPE:], positions)], axis=-1)
        kv_mla = jnp.einsum('bsr,rc->bsc', _rms_norm(c_kv, mla_kv_norm_g[l]), w_mla_kv_up[l])
        kv_mla = kv_mla.reshape(b, s, N_MLA_HEADS, MLA_NOPE + MLA_V)
        k_rope = jnp.broadcast_to(_rope(k_rot, positions)[:, :, None, :], (b, s, N_MLA_HEADS, MLA_ROPE))
        k_full = jnp.concatenate([kv_mla[..., :MLA_NOPE], k_rope], axis=-1)
        v_mla = kv_mla[..., MLA_NOPE:]
        out_mla = _causal_softmax_attention(q_full.transpose(0, 2, 1, 3), k_full.transpose(0, 2, 1, 3),
                                            v_mla.transpose(0, 2, 1, 3), mla_scale)

        mkv = jnp.einsum('bmd,dc->bmc', mem_n, w_mem_kv[l])
        mk, mv = _heads(mkv[..., :GROUP_W], N_MEM_HEADS), _heads(mkv[..., GROUP_W:], N_MEM_HEADS)
        mem_logits = jnp.einsum('bhsd,bhmd->bhsm', _heads(mq, N_MEM_HEADS), mk).astype(jnp.float32) * head_scale
        mem_p = jax.nn.softmax(mem_logits, axis=-1)
        out_mem = jnp.einsum('bhsm,bhmd->bhsd', mem_p.astype(mv.dtype), mv)

        mixed = jnp.concatenate([_merge(out_fox), _merge(out_sb), _merge(out_mla), _merge(out_mem)], axis=-1)
        y = jnp.einsum('bsc,cd->bsd', mixed * jax.nn.silu(gate), w_out[l])

        h_res = _layer_norm(DEEPNORM_ALPHA * h_res + y, ln_g[l], ln_b[l])

    return h_res
```

```python
import math
import numpy as np
import ml_dtypes
import concourse.bass as bass
import concourse.mybir as mybir
from concourse.bass_utils import run_bass_kernel_spmd

F32 = mybir.dt.float32
BF16 = mybir.dt.bfloat16
AF = mybir.ActivationFunctionType
ALU = mybir.AluOpType

S = 8192
D = 1024
MEM = 256
NCH = 16
CH = 512
HPG = 2
DEPTH = 2
ALPHA = (2 * DEPTH) ** 0.25
LN_EPS = 1e-5
RMS_EPS = 1e-6
NEG = -30000.0
MLA_SCALE = 96 ** -0.5

FQ, FK, SQ, SK, MQ = 0, 128, 256, 384, 512
GATE = 640
CQ = 1152
CKV = 1408
KR1 = 1536
KR2 = 1552
FL = 1568
FV = 1570
SV = 1698
NCOL = 1826

ENGS = ["pe", "act", "dve", "pool", "sp"]
NDSEM = 24
import os as _os
SUB = int(_os.environ.get('SUB', '9'))


class Sched:
    def __init__(self):
        self.q = {e: [] for e in ENGS}
        self.cnt = {e: 0 for e in ENGS}
        self.seen = {e: {} for e in ENGS}
        self.rw = {}
        self.rr = {}
        self.dval = [0] * NDSEM
        self.drr = 0
        self.nins = 0

    def _need(self, eng, reads, writes):
        need = {}

        def add(d, skip_pe):
            for sk, v in d.items():
                if skip_pe and sk == "pe" and eng == "pe":
                    continue
                if need.get(sk, 0) < v:
                    need[sk] = v

        for k in reads:
            add(self.rw.get(k, {}), False)
        for k in writes:
            add(self.rw.get(k, {}), True)
            add(self.rr.get(k, {}), False)
        out = []
        for sk, v in need.items():
            if self.seen[eng].get(sk, 0) >= v:
                continue
            self.seen[eng][sk] = v
            out.append((sk, v))
        return out

    def _record(self, tok, reads, writes):
        sk, v = tok
        for k in reads:
            self.rr.setdefault(k, {})[sk] = v
        for k in writes:
            self.rw[k] = {sk: v}
            self.rr[k] = {}

    def op(self, eng, meth, args, kwargs, r=(), w=()):
        waits = self._need(eng, r, w)
        self.cnt[eng] += 1
        tok = (eng, self.cnt[eng])
        self.q[eng].append((waits, (meth, args, kwargs), tok))
        self._record(tok, r, w)
        self.nins += 1 + max(0, len(waits) - 1)

    def proxy(self, eng):
        sch = self

        class _P:
            def __getattr__(self, meth):
                def f(*args, r=(), w=(), **kwargs):
                    sch.op(eng, meth, args, kwargs, r, w)
                return f
        return _P()

    def dma(self, out, in_, r=(), w=()):
        eng = "sp"
        waits = self._need(eng, r, w)
        i = self.drr
        self.drr = (self.drr + 1) % NDSEM
        sk = ("d", i)
        if self.dval[i] > 0 and self.seen[eng].get(sk, 0) < self.dval[i]:
            self.seen[eng][sk] = self.dval[i]
            waits.append((sk, self.dval[i]))
        self.dval[i] += 16
        tok = (sk, self.dval[i])
        self.q[eng].append((waits, (out, in_), tok))
        self._record(tok, r, w)
        self.nins += 1 + max(0, len(waits) - 1)

    def barrier(self):
        for e in ENGS:
            waits = []
            for o in ENGS:
                if o == e or o == "sp":
                    continue
                if self.cnt[o] > self.seen[e].get(o, 0):
                    self.seen[e][o] = self.cnt[o]
                    waits.append((o, self.cnt[o]))
            for i in range(NDSEM):
                sk = ("d", i)
                if self.dval[i] > self.seen[e].get(sk, 0):
                    self.seen[e][sk] = self.dval[i]
                    waits.append((sk, self.dval[i]))
            if waits:
                self.q[e].append((waits, None, None))
                self.nins += len(waits)
        self.rw = {}
        self.rr = {}

    def emit(self, nc):
        import contextlib

        with contextlib.ExitStack() as st:
            sems = {}
            for e in ["pe", "act", "dve", "pool"]:
                sems[e] = st.enter_context(nc.semaphore("s_" + e))
            for i in range(NDSEM):
                sems[("d", i)] = st.enter_context(nc.semaphore("s_d%d" % i))
            block = st.enter_context(nc.Block())

            def run(eng, e):
                for waits, fn, tok in self.q[eng]:
                    if fn is None:
                        for sk, v in waits:
                            e.wait_ge(sems[sk], v)
                        continue
                    if eng == "pe":
                        for sk, v in waits:
                            e.wait_ge(sems[sk], v)
                        waits = []
                    for sk, v in waits[:-1]:
                        e.wait_ge(sems[sk], v)
                    if eng == "sp":
                        ins = e.dma_start(out=fn[0], in_=fn[1])
                    else:
                        ins = getattr(e, fn[0])(*fn[1], **fn[2])
                    if waits:
                        ins._wait_ge(sems[waits[-1][0]], waits[-1][1])
                    if eng == "sp":
                        ins.then_inc(sems[tok[0]], 16)
                    else:
                        ins.then_inc(sems[eng], 1)

            @block.tensor
            def _(e):
                run("pe", e)

            @block.scalar
            def _(e):
                run("act", e)

            @block.vector
            def _(e):
                run("dve", e)

            @block.gpsimd
            def _(e):
                run("pool", e)

            @block.sync
            def _(e):
                run("sp", e)


class Arena:
    def __init__(self, t, nwords):
        self.t = t
        self.n = nwords
        self.off = 0

    def mark(self):
        return self.off

    def release(self, m):
        self.off = m

    def alloc(self, free_shape, dtype, parts=128):
        n = int(np.prod(free_shape))
        words = n if dtype == F32 else (n + 1) // 2
        words = (words + 1) // 2 * 2
        assert self.off + words <= self.n, ("arena overflow", self.off, words, self.n)
        ap = self.t[0:parts, self.off:self.off + words]
        self.off += words
        self.peak = max(getattr(self, "peak", 0), self.off)
        if dtype != F32:
            ap = ap.bitcast(dtype)
        ap = ap[:, 0:n]
        if len(free_shape) == 2:
            ap = ap.rearrange("p (a b) -> p a b", a=free_shape[0])
        elif len(free_shape) == 3:
            ap = ap.rearrange("p (a b c) -> p a b c", a=free_shape[0], b=free_shape[1])
        return ap


class Ring:
    def __init__(self, name, aps):
        self.name = name
        self.aps = aps
        self.i = 0

    def next(self):
        k = self.i % len(self.aps)
        self.i += 1
        return "%s%d" % (self.name, k), self.aps[k]


def build_fused(NH=4, depth=DEPTH, nchunk_lim=None, heads_lim=None, stages_lim=None, junk_fox=0, junk_sb=()):
    nc = bass.Bass("TRN2", target_bir_lowering=False)
    QW = NH * 64
    FQ, FK, SQ, SK, MQ = 0, QW, 2 * QW, 3 * QW, 4 * QW
    GATE = 5 * QW
    CQ = 9 * QW
    CKV = CQ + 256
    KR1 = CKV + 128
    KR2 = KR1 + 16
    FL = KR2 + 16
    FV = FL + NH
    SV = FV + QW
    NCOL = SV + QW
    MIX = 4 * QW
    RM = NH * 16
    dbg = bool(_os.environ.get("KDBG"))

    def din(name, shape, dt=F32):
        return nc.dram_tensor(name, list(shape), dt, kind="ExternalInput").ap()

    def dscr(name, shape, dt=BF16):
        return nc.dram_tensor(name, list(shape), dt, kind=("ExternalOutput" if dbg else "Internal")).ap()

    x_in = din("x", [S, D])
    mem_in = din("mem", [MEM, D])
    lng = [din("lng%d" % i, [128, D]) for i in range(depth + 1)]
    lnb = [din("lnb%d" % i, [128, D]) for i in range(depth + 1)]
    mlng = din("mlng", [128, D])
    mlnb = din("mlnb", [128, D])
    c_bf = din("c_bf", [128, 6 * 128], BF16)
    c_id = din("c_id", [128, 128])
    c_sel = din("c_sel", [65, 64])
    c_cos = din("c_cos", [RM, S])
    c_sin = din("c_sin", [RM, S])
    w_in = [din("w_in%d" % l, [D, NCOL]) for l in range(depth)]
    b_f = [din("b_f%d" % l, [NH, 1]) for l in range(depth)]
    gq = [din("gq%d" % l, [128, 2]) for l in range(depth)]
    gkv = [din("gkv%d" % l, [128, 1]) for l in range(depth)]
    w_qup = [din("w_qup%d" % l, [256, NH * 96]) for l in range(depth)]
    w_kvup = [din("w_kvup%d" % l, [128, 2 * QW]) for l in range(depth)]
    w_mem = [din("w_mem%d" % l, [D, 2 * QW]) for l in range(depth)]
    w_out = [din("w_out%d" % l, [MIX, D]) for l in range(depth)]
    out_d = nc.dram_tensor("out", [S, D], F32, kind="ExternalOutput").ap()

    h_d = dscr("h_scr", [S, D], F32)
    gt_d = dscr("gt_scr", [MIX, S])
    fox_qT = dscr("fox_qT", [NH, 70, S])
    fox_kT = dscr("fox_kT", [NH, 70, S])
    fox_v = dscr("fox_v", [NH, S, 65])
    sb_qT = dscr("sb_qT", [NH, 64, S])
    sb_kT = dscr("sb_kT", [NH, 64, S])
    sb_v = dscr("sb_v", [NH, S, 64])
    mla_qT = dscr("mla_qT", [NH, 96, S])
    mla_kT = dscr("mla_kT", [NH, 96, S])
    mla_v = dscr("mla_v", [NH, S, 65])
    mem_qT = dscr("mem_qT", [NH, 64, S])
    mem_kT = dscr("mem_kT", [NH, 64, MEM])
    mem_v = dscr("mem_v", [NH, MEM, 65])
    gateT = dscr("gateT", [MIX, S], F32)
    flogT = dscr("flogT", [NH, S], F32)

    sch = Sched()
    P_pe, P_act, P_dve, P_pool = sch.proxy("pe"), sch.proxy("act"), sch.proxy("dve"), sch.proxy("pool")
    NW = 52400
    import contextlib

    with contextlib.ExitStack() as st:
        arena_t = st.enter_context(nc.sbuf_tensor("arena", [128, NW], F32))
        allb = st.enter_context(nc.psum_tensor("allb", [128, 4096], F32))
        banks = [allb[:, i * 512:(i + 1) * 512] for i in range(8)]
        ar = Arena(arena_t, NW)

        cbf = ar.alloc([6, 128], BF16)
        ident_bf, ones_bf, negU, mneg_incl, mneg_strict, m01_strict = [cbf[:, i, :] for i in range(6)]
        ident_f = ar.alloc([128], F32)
        sel = ar.alloc([64], F32, parts=65)
        Gbc = ar.alloc([D], F32)
        Bbc = ar.alloc([D], F32)
        mGbc = ar.alloc([D], F32)
        mBbc = ar.alloc([D], F32)
        epsln = ar.alloc([2], F32)
        nbf = ar.alloc([2], F32, parts=NH)
        small_ring = Ring("small", [ar.alloc([16], F32) for _ in range(4)])
        sch.dma(cbf, c_bf.rearrange("p (a b) -> p a b", a=6), w=["cbf"])
        sch.dma(ident_f, c_id, w=["ident_f"])
        sch.dma(sel, c_sel, w=["sel"])
        sch.dma(mGbc, mlng, w=["mGbc"])
        sch.dma(mBbc, mlnb, w=["mBbc"])
        P_pool.memset(epsln[:, 0:1], LN_EPS, w=["epsln"])
        P_pool.memset(epsln[:, 1:2], RMS_EPS, w=["epsln2"])
        sch.barrier()

        bank_ring = Ring("bank", [b[:] for b in banks])

        def layer_norm(R2, kR2, H, kH, G, B, small):
            kst, stt = small.next()
            st6 = stt[:, 0:12].rearrange("p (a b) -> p a b", a=2)
            mv = stt[:, 12:14]
            tmp = stt[:, 14:16]
            P_dve.bn_stats(out=st6[:, 0, :], in_=R2[:, 0:512], r=[kR2], w=[kst])
            P_dve.bn_stats(out=st6[:, 1, :], in_=R2[:, 512:1024], r=[kR2], w=[kst + "b"])
            P_dve.bn_aggr(out=mv, in_=stt[:, 0:12], r=[kst, kst + "b"], w=[kst + "mv"])
            P_act.activation(out=tmp[:, 0:1], in_=mv[:, 1:2], func=AF.Ln, bias=epsln[:, 0:1], scale=1.0,
                             r=[kst + "mv"], w=[kst + "t0"])
            P_act.activation(out=tmp[:, 1:2], in_=tmp[:, 0:1], func=AF.Exp, scale=-0.5, r=[kst + "t0"], w=[kst + "t1"])
            P_dve.tensor_scalar(out=H, in0=R2, scalar1=mv[:, 0:1], scalar2=tmp[:, 1:2], op0=ALU.subtract, op1=ALU.mult,
                                r=[kR2, kst + "mv", kst + "t1"], w=[kH])
            P_pool.tensor_tensor(out=H, in0=H, in1=G, op=ALU.mult, r=[kH], w=[kH])
            P_dve.tensor_tensor(out=H, in0=H, in1=B, op=ALU.add, r=[kH], w=[kH])

        def transposes(H, kH, hT, khT, col0):
            for half in range(2):
                kb, bk = bank_ring.next()
                for cc in range(4):
                    c = half * 4 + cc
                    P_pe.transpose(out=bk[:, cc * 128:(cc + 1) * 128], in_=H[:, c * 128:(c + 1) * 128], identity=ident_f,
                                   r=[kH], w=[kb])
                src = bk.rearrange("p (a b) -> p a b", a=4)
                dst = hT[:, half * 4:(half + 1) * 4, col0:col0 + 128]
                if half == 0:
                    P_act.activation(out=dst, in_=src, func=AF.Copy, r=[kb], w=[khT + "h%d_%d" % (half, col0)])
                else:
                    P_dve.tensor_copy(out=dst, in_=src, r=[kb], w=[khT + "h%d_%d" % (half, col0)])

        def hT_keys(khT, ncols):
            return [khT + "h%d_%d" % (half, c0) for half in range(2) for c0 in range(0, ncols, 128)]

        cast_i = [0]

        def cast(out, in_, r, w):
            k = cast_i[0] % 3
            cast_i[0] += 1
            if k == 0:
                P_dve.tensor_copy(out=out, in_=in_, r=r, w=w)
            elif k == 1:
                P_pool.tensor_copy(out=out, in_=in_, r=r, w=w)
            else:
                P_act.activation(out=out, in_=in_, func=AF.Copy, r=r, w=w)

        nstage = depth + 1 if stages_lim is None else stages_lim
        for stage in range(nstage):
            body = stage < depth
            prev = stage > 0
            l = stage
            mW = ar.mark()
            sch.dma(Gbc, lng[stage], w=["Gbc"])
            sch.dma(Bbc, lnb[stage], w=["Bbc"])
            if prev:
                wout = ar.alloc([8, D], BF16)
            if body:
                wbf = ar.alloc([8, NCOL], BF16)
                wmem = ar.alloc([8, 2 * QW], BF16)
                wqg = ar.alloc([2, NH * 96], BF16)
                wkvg = ar.alloc([2 * QW], BF16)
            m0 = ar.mark()
            stg = Ring("stg", [ar.alloc([NCOL], F32) for _ in range(3)])
            if prev:
                for c in range(8):
                    ks, sg_ = stg.next()
                    sch.dma(sg_[:, 0:D], w_out[l - 1][c * 128:(c + 1) * 128, :], w=[ks])
                    cast(wout[:, c, :], sg_[:, 0:D], [ks], ["wout%d" % c])
            if body:
                for c in range(8):
                    ks, sg_ = stg.next()
                    sch.dma(sg_, w_in[l][c * 128:(c + 1) * 128, :], w=[ks])
                    cast(wbf[:, c, :], sg_, [ks], ["wbf%d" % c])
                for c in range(8):
                    ks, sg_ = stg.next()
                    sch.dma(sg_[:, 0:2 * QW], w_mem[l][c * 128:(c + 1) * 128, :], w=[ks])
                    cast(wmem[:, c, :], sg_[:, 0:2 * QW], [ks], ["wmem%d" % c])
                gq_t = ar.alloc([2], F32)
                gkv_t = ar.alloc([2], F32)
                bf_t = ar.alloc([2], F32, parts=NH)
                sch.dma(gq_t, gq[l], w=["gq_t"])
                sch.dma(gkv_t[:, 0:1], gkv[l], w=["gkv_t"])
                sch.dma(bf_t[:, 0:1], b_f[l], w=["bf_t"])
                for c in range(2):
                    ks, sg_ = stg.next()
                    sch.dma(sg_[:, 0:NH * 96], w_qup[l][c * 128:(c + 1) * 128, :], w=[ks])
                    P_dve.tensor_scalar(out=wqg[:, c, :], in0=sg_[:, 0:NH * 96], scalar1=gq_t[:, c:c + 1], scalar2=None, op0=ALU.mult,
                                        r=[ks, "gq_t"], w=["wqg%d" % c])
                ks, sg_ = stg.next()
                sch.dma(sg_[:, 0:2 * QW], w_kvup[l], w=[ks])
                P_dve.tensor_scalar(out=wkvg, in0=sg_[:, 0:2 * QW], scalar1=gkv_t[:, 0:1], scalar2=None, op0=ALU.mult,
                                    r=[ks, "gkv_t"], w=["wkvg"])
                P_dve.tensor_scalar(out=nbf[:, 0:1], in0=bf_t[:, 0:1], scalar1=-1.0, scalar2=None, op0=ALU.mult, r=["bf_t"], w=["nbf"])
            sch.barrier()
            ar.release(m0)
            mA = ar.mark()

            if body:
                Rm = Ring("Rm", [ar.alloc([D], F32) for _ in range(2)])
                Hm = Ring("Hm", [ar.alloc([D], F32) for _ in range(2)])
                mT = ar.alloc([8, MEM], BF16)
                vaugm = ar.alloc([2, NH, 65], BF16)
                kmem_r = Ring("kmem", [ar.alloc([MEM], BF16) for _ in range(2)])
                P_pool.memset(vaugm, 1.0, w=["vaugm"])
                for i in range(2):
                    kR, R = Rm.next()
                    kH, H = Hm.next()
                    sch.dma(R, mem_in[i * 128:(i + 1) * 128, :], w=[kR])
                    layer_norm(R, kR, H, kH, mGbc, mBbc, small_ring)
                    transposes(H, kH, mT, "mT", i * 128)
                for t in range(NH // 2):
                    kb, bk = bank_ring.next()
                    for c in range(8):
                        P_pe.matmul(bk[:, 0:MEM], lhsT=wmem[:, c, t * 128:(t + 1) * 128], rhs=mT[:, c, :], start=(c == 0), stop=(c == 7),
                                    r=hT_keys("mT", MEM), w=[kb])
                    kkm, kmem_sb = kmem_r.next()
                    P_act.activation(out=kmem_sb, in_=bk[:, 0:MEM], func=AF.Copy, r=[kb], w=[kkm])
                    for jj in range(2):
                        sch.dma(mem_kT[2 * t + jj, :, :], kmem_sb[jj * 64:(jj + 1) * 64, :], r=[kkm], w=[])
                for i in range(2):
                    kb, bk = bank_ring.next()
                    for c in range(8):
                        P_pe.matmul(bk[:, 0:QW], lhsT=mT[:, c, i * 128:(i + 1) * 128], rhs=wmem[:, c, QW:2 * QW], start=(c == 0), stop=(c == 7),
                                    r=hT_keys("mT", MEM), w=[kb])
                    P_dve.tensor_copy(out=vaugm[:, i, :, 0:64], in_=bk[:, 0:QW].rearrange("p (j d) -> p j d", j=NH),
                                      r=[kb, "vaugm"], w=["vaugm%d" % i])
                    for j in range(NH):
                        sch.dma(mem_v[j, i * 128:(i + 1) * 128, :], vaugm[:, i, j, :], r=["vaugm%d" % i], w=[])
                sch.barrier()
                ar.release(mA)

            Rr = Ring("R", [ar.alloc([D], F32) for _ in range(2)])
            Hr = Ring("H", [ar.alloc([D], F32) for _ in range(2)])
            if prev:
                gtfr = Ring("gtf", [ar.alloc([8, CH], BF16) for _ in range(2)])
            if body:
                hTr = Ring("hT", [ar.alloc([8, CH], BF16) for _ in range(2)])
                evb = Ring("evb", [ar.alloc([CH], BF16) for _ in range(4)])
                evf = Ring("evf", [ar.alloc([CH], F32) for _ in range(3)])
                cqn_r = Ring("cqn", [ar.alloc([2, CH], BF16) for _ in range(2)])
                sq_r = Ring("sq", [ar.alloc([2, CH], BF16) for _ in range(2)])
                ckvn_r = Ring("ckvn", [ar.alloc([CH], BF16) for _ in range(2)])
                rq_r = Ring("rq", [ar.alloc([CH], F32) for _ in range(2)])
                cs_r = Ring("cs", [ar.alloc([2, CH], F32, parts=RM) for _ in range(2)])
                rope_r = Ring("rope", [ar.alloc([CH], F32, parts=RM) for _ in range(4)])
                ropeo_r = Ring("ropeo", [ar.alloc([CH], BF16, parts=RM) for _ in range(4)])
                vaugF_r = Ring("vaugF", [ar.alloc([4, NH, 65], BF16) for _ in range(2)])
                vS_r = Ring("vS", [ar.alloc([4, NH, 64], BF16) for _ in range(2)])
                vaugM_r = Ring("vaugM", [ar.alloc([4, NH, 65], BF16) for _ in range(2)])
                fl_r = Ring("fl", [ar.alloc([CH], F32, parts=NH) for _ in range(2)])
                for k_, t_ in zip(["vaugF0", "vaugF1"], vaugF_r.aps):
                    P_pool.memset(t_, 1.0, w=[k_])
                for k_, t_ in zip(["vaugM0", "vaugM1"], vaugM_r.aps):
                    P_pool.memset(t_, 1.0, w=[k_])

            nchunk = NCH if nchunk_lim is None else min(NCH, nchunk_lim)
            src_d = x_in if stage == 0 else h_d
            dst_d = out_d if stage == depth else h_d
            cctx = {}
            rload = {}
            gload = {}
            hkeep = {}
            ntile_all = nchunk * 4

            def load_R(gi):
                if gi in rload or gi >= ntile_all:
                    return
                kR, R = Rr.next()
                sch.dma(R, src_d[gi * 128:(gi + 1) * 128, :], r=["hrow%d" % gi], w=[kR])
                rload[gi] = (kR, R)

            def load_g(ci):
                if (not prev) or ci in gload or ci >= nchunk:
                    return
                kg, gtf = gtfr.next()
                sch.dma(gtf, gt_d[:, ci * CH:(ci + 1) * CH].rearrange("(c p) s -> p c s", p=128), w=[kg])
                gload[ci] = (kg, gtf)

            def pro_a(ci, ti):
                gi = ci * 4 + ti
                if ti == 0:
                    ctx = {}
                    if body:
                        ctx["khT"], ctx["hT"] = hTr.next()
                    cctx[ci] = ctx
                    load_g(ci)
                    load_g(ci + 1)
                load_R(gi)
                load_R(gi + 1)
                kR, R = rload.pop(gi)
                kH, H = Hr.next()
                if prev:
                    kg, gtf = gload[ci]
                    for half in range(2):
                        kb, bk = bank_ring.next()
                        for c in range(8):
                            P_pe.matmul(bk, lhsT=gtf[:, c, ti * 128:(ti + 1) * 128], rhs=wout[:, c, half * 512:(half + 1) * 512],
                                        start=(c == 0), stop=(c == 7), r=[kg], w=[kb])
                        P_dve.scalar_tensor_tensor(out=R[:, half * 512:(half + 1) * 512], in0=R[:, half * 512:(half + 1) * 512],
                                                   scalar=ALPHA, in1=bk, op0=ALU.mult, op1=ALU.add, r=[kb, kR], w=[kR])
                layer_norm(R, kR, H, kH, Gbc, Bbc, small_ring)
                sch.dma(dst_d[gi * 128:(gi + 1) * 128, :], H, r=[kH], w=["hrow%d" % gi])
                hkeep[gi] = (kH, H)

            def pro_b(ci, ti):
                kH, H = hkeep.pop(ci * 4 + ti)
                if body:
                    transposes(H, kH, cctx[ci]["hT"], cctx[ci]["khT"], ti * 128)

            for ci in range(nchunk):
                c0 = ci * CH
                if ci == 0 or not body:
                    for ti in range(4):
                        pro_a(ci, ti)
                        pro_b(ci, ti)
                if not body:
                    continue
                khT, hT = cctx[ci]["khT"], cctx[ci]["hT"]

                def nxt(k, ci=ci):
                    if ci + 1 >= nchunk:
                        return
                    if k >= 1:
                        pro_b(ci + 1, k - 1)
                    if k <= 3:
                        pro_a(ci + 1, k)

                nxt(0)
                hk = hT_keys(khT, CH)

                def fm_tile(col, m, hT=hT, hk=hk):
                    kb, bk = bank_ring.next()
                    for c in range(8):
                        P_pe.matmul(bk[0:m, :], lhsT=wbf[:, c, col:col + m], rhs=hT[:, c, :], start=(c == 0), stop=(c == 7), r=hk, w=[kb])
                    return kb, bk

                for col, dst, scl in ((FQ, fox_qT, 0.125), (FK, fox_kT, 1.0), (SQ, sb_qT, 0.125), (SK, sb_kT, 1.0), (MQ, mem_qT, 0.125)):
                    for t in range(NH // 2):
                        kb, bk = fm_tile(col + t * 128, 128)
                        ke, ev = evb.next()
                        P_act.activation(out=ev, in_=bk, func=AF.Copy, scale=scl, r=[kb], w=[ke])
                        for jj in range(2):
                            sch.dma(dst[2 * t + jj, 0:64, c0:c0 + CH], ev[jj * 64:(jj + 1) * 64, :], r=[ke], w=[])
                nxt(1)
                for t in range(MIX // 128):
                    kb, bk = fm_tile(GATE + t * 128, 128)
                    ke, ev = evf.next()
                    P_act.activation(out=ev, in_=bk, func=AF.Exp, scale=-1.0, r=[kb], w=[ke])
                    P_act.activation(out=ev, in_=ev, func=AF.Ln, bias=1.0, scale=1.0, r=[ke], w=[ke])
                    P_act.activation(out=ev, in_=ev, func=AF.Exp, scale=-1.0, r=[ke], w=[ke])
                    P_dve.tensor_tensor(out=ev, in0=bk, in1=ev, op=ALU.mult, r=[kb, ke], w=[ke])
                    sch.dma(gateT[t * 128:(t + 1) * 128, c0:c0 + CH], ev, r=[ke], w=[])
                nxt(2)
                kb, bk = fm_tile(FL, NH)
                kf, fl = fl_r.next()
                P_dve.tensor_copy(out=fl, in_=bk[0:NH, :], r=[kb], w=[kf])
                sch.dma(flogT[:, c0:c0 + CH], fl, r=[kf], w=[])
                kvf, vaugF = vaugF_r.next()
                kvs, vS = vS_r.next()
                for ti in range(4):
                    kb, bk = bank_ring.next()
                    for c in range(8):
                        P_pe.matmul(bk[:, 0:2 * QW], lhsT=hT[:, c, ti * 128:(ti + 1) * 128], rhs=wbf[:, c, FV:FV + 2 * QW],
                                    start=(c == 0), stop=(c == 7), r=hk, w=[kb])
                    P_act.activation(out=vaugF[:, ti, :, 0:64], in_=bk[:, 0:QW].rearrange("p (j d) -> p j d", j=NH), func=AF.Copy,
                                     r=[kb, kvf], w=[kvf + "_%d" % ti])
                    P_act.activation(out=vS[:, ti, :, :], in_=bk[:, QW:2 * QW].rearrange("p (j d) -> p j d", j=NH), func=AF.Copy,
                                     r=[kb], w=[kvs + "_%d" % ti])
                for j in range(NH):
                    sch.dma(fox_v[j, c0:c0 + CH, :].rearrange("(t p) c -> p t c", p=128), vaugF[:, :, j, :],
                            r=[kvf + "_%d" % ti for ti in range(4)], w=[])
                    sch.dma(sb_v[j, c0:c0 + CH, :].rearrange("(t p) c -> p t c", p=128), vS[:, :, j, :],
                            r=[kvs + "_%d" % ti for ti in range(4)], w=[])
                nxt(3)
                kcs, cs = cs_r.next()
                sch.dma(cs[:, 0, :], c_cos[:, c0:c0 + CH], w=[kcs + "c"])
                sch.dma(cs[:, 1, :], c_sin[:, c0:c0 + CH], w=[kcs + "s"])

                def rope(b1, kb1, b2, kb2, m, dsts, cs=cs, kcs=kcs):
                    ka, a = rope_r.next()
                    kb_, b = rope_r.next()
                    ko1, o1 = ropeo_r.next()
                    ko2, o2 = ropeo_r.next()
                    cosv, sinv = cs[0:m, 0, :], cs[0:m, 1, :]
                    P_dve.tensor_tensor(out=a[0:m, :], in0=b1, in1=cosv, op=ALU.mult, r=[kb1, kcs + "c"], w=[ka])
                    P_dve.tensor_tensor(out=b[0:m, :], in0=b2, in1=sinv, op=ALU.mult, r=[kb2, kcs + "s"], w=[kb_])
                    P_pool.tensor_tensor(out=o1[0:m, :], in0=a[0:m, :], in1=b[0:m, :], op=ALU.subtract, r=[ka, kb_], w=[ko1])
                    kc, c_ = rope_r.next()
                    kd, d_ = rope_r.next()
                    P_dve.tensor_tensor(out=c_[0:m, :], in0=b1, in1=sinv, op=ALU.mult, r=[kb1, kcs + "s"], w=[kc])
                    P_dve.tensor_tensor(out=d_[0:m, :], in0=b2, in1=cosv, op=ALU.mult, r=[kb2, kcs + "c"], w=[kd])
                    P_pool.tensor_tensor(out=o2[0:m, :], in0=c_[0:m, :], in1=d_[0:m, :], op=ALU.add, r=[kc, kd], w=[ko2])
                    for (dst_ap, lo, which) in dsts:
                        src = (o1 if which == 0 else o2)[lo:lo + 16, :]
                        sch.dma(dst_ap, src, r=[ko1 if which == 0 else ko2], w=[])

                def rms_bcast(pstiles, nt):
                    ksq, sq = sq_r.next()
                    for t, (kb, bk) in enumerate(pstiles):
                        ke, ev = evf.next()
                        P_act.activation(out=ev, in_=bk, func=AF.Copy, r=[kb], w=[ke])
                        P_dve.tensor_tensor(out=sq[:, t, :], in0=ev, in1=bk, op=ALU.mult, r=[kb, ke], w=[ksq + "_%d" % t])
                    kss, ss = bank_ring.next()
                    for t in range(nt):
                        P_pe.matmul(ss, lhsT=ones_bf, rhs=sq[:, t, :], start=(t == 0), stop=(t == nt - 1), r=[ksq + "_%d" % t], w=[kss])
                    krq, rq = rq_r.next()
                    P_act.activation(out=rq, in_=ss, func=AF.Ln, bias=epsln[:, 1:2], scale=1.0 / (128.0 * nt), r=[kss], w=[krq])
                    P_act.activation(out=rq, in_=rq, func=AF.Exp, scale=-0.5, r=[krq], w=[krq])
                    return krq, rq

                cqt = [fm_tile(CQ + t * 128, 128) for t in range(2)]
                krq, rq = rms_bcast(cqt, 2)
                kcq, cqn = cqn_r.next()
                for t, (kb, bk) in enumerate(cqt):
                    P_dve.tensor_tensor(out=cqn[:, t, :], in0=bk, in1=rq, op=ALU.mult, r=[kb, krq], w=[kcq + "_%d" % t])
                kcqs = [kcq + "_0", kcq + "_1"]
                for tt in range(NH // 2):
                    kb, bk = bank_ring.next()
                    for t in range(2):
                        P_pe.matmul(bk, lhsT=wqg[:, t, tt * 128:(tt + 1) * 128], rhs=cqn[:, t, :], start=(t == 0), stop=(t == 1), r=kcqs, w=[kb])
                    ke, ev = evb.next()
                    P_act.activation(out=ev, in_=bk, func=AF.Copy, r=[kb], w=[ke])
                    for jj in range(2):
                        sch.dma(mla_qT[2 * tt + jj, 0:64, c0:c0 + CH], ev[jj * 64:(jj + 1) * 64, :], r=[ke], w=[])
                kb1, bk1 = bank_ring.next()
                kb2, bk2 = bank_ring.next()
                for t in range(2):
                    P_pe.matmul(bk1[0:RM, :], lhsT=wqg[:, t, QW:QW + RM], rhs=cqn[:, t, :], start=(t == 0), stop=(t == 1), r=kcqs, w=[kb1])
                for t in range(2):
                    P_pe.matmul(bk2[0:RM, :], lhsT=wqg[:, t, QW + RM:QW + 2 * RM], rhs=cqn[:, t, :], start=(t == 0), stop=(t == 1), r=kcqs, w=[kb2])
                rope(bk1[0:RM, :], kb1, bk2[0:RM, :], kb2, RM,
                     [(mla_qT[j, 64 + 16 * w_:80 + 16 * w_, c0:c0 + CH], j * 16, w_) for j in range(NH) for w_ in range(2)])
                nxt(4)
                ckt = [fm_tile(CKV, 128)]
                krk, rk = rms_bcast(ckt, 1)
                kck, ckvn = ckvn_r.next()
                P_dve.tensor_tensor(out=ckvn, in0=ckt[0][1], in1=rk, op=ALU.mult, r=[ckt[0][0], krk], w=[kck])
                for tt in range(NH // 2):
                    kb, bk = bank_ring.next()
                    P_pe.matmul(bk, lhsT=wkvg[:, tt * 128:(tt + 1) * 128], rhs=ckvn, start=True, stop=True, r=[kck], w=[kb])
                    ke, ev = evb.next()
                    P_act.activation(out=ev, in_=bk, func=AF.Copy, r=[kb], w=[ke])
                    for jj in range(2):
                        sch.dma(mla_kT[2 * tt + jj, 0:64, c0:c0 + CH], ev[jj * 64:(jj + 1) * 64, :], r=[ke], w=[])
                kvm, vaugM = vaugM_r.next()
                per_bank = 512 // QW
                for tb in range(4 // per_bank):
                    kb, bk = bank_ring.next()
                    for tq in range(per_bank):
                        ti = tb * per_bank + tq
                        P_pe.matmul(bk[:, tq * QW:(tq + 1) * QW], lhsT=ckvn[:, ti * 128:(ti + 1) * 128], rhs=wkvg[:, QW:2 * QW],
                                    start=True, stop=True, r=[kck], w=[kb])
                    P_dve.tensor_copy(out=vaugM[:, tb * per_bank:(tb + 1) * per_bank, :, 0:64],
                                      in_=bk.rearrange("p (t j d) -> p t j d", t=per_bank, j=NH), r=[kb, kvm], w=[kvm + "_%d" % tb])
                for j in range(NH):
                    sch.dma(mla_v[j, c0:c0 + CH, :].rearrange("(t p) c -> p t c", p=128), vaugM[:, :, j, :],
                            r=[kvm + "_%d" % tb for tb in range(4 // per_bank)], w=[])
                kb1, bk1 = fm_tile(KR1, 16)
                kb2, bk2 = fm_tile(KR2, 16)
                rope(bk1[0:16, :], kb1, bk2[0:16, :], kb2, 16,
                     [(mla_kT[j, 64 + 16 * w_:80 + 16 * w_, c0:c0 + CH], 0, w_) for j in range(NH) for w_ in range(2)])

            sch.barrier()
            ar.release(mW)
            if not body:
                continue

            mF = ar.mark()
            SEG = 2048
            flr = Ring("flseg", [ar.alloc([SEG], F32, parts=NH) for _ in range(2)])
            Pr = Ring("P", [ar.alloc([SEG], F32, parts=NH) for _ in range(2)])
            r1r = Ring("r1", [ar.alloc([SEG], F32, parts=NH) for _ in range(2)])
            pcr = Ring("pc", [ar.alloc([3, SEG], BF16, parts=NH) for _ in range(2)])
            ncr = Ring("nc", [ar.alloc([3, SEG], BF16, parts=NH) for _ in range(2)])
            zer = ar.alloc([SEG], F32, parts=NH)
            onesr = ar.alloc([3, SEG], BF16, parts=NH)
            carry = ar.alloc([2], F32, parts=NH)
            P_pool.memset(zer, 0.0, w=["zer"])
            P_pool.memset(onesr, 1.0, w=["onesr"])
            P_pool.memset(carry, 0.0, w=["carry"])
            for sg in range(S // SEG):
                s0 = sg * SEG
                kfl, fl = flr.next()
                kP, P = Pr.next()
                kr1, r1 = r1r.next()
                kpc, pc = pcr.next()
                knc, ncp = ncr.next()
                sch.dma(fl, flogT[:, s0:s0 + SEG], w=[kfl])
                P_act.activation(out=fl, in_=fl, func=AF.Exp, bias=nbf[:, 0:1], scale=-1.0, r=[kfl, "nbf"], w=[kfl])
                P_act.activation(out=fl, in_=fl, func=AF.Ln, bias=1.0, scale=1.0, r=[kfl], w=[kfl])
                P_dve.tensor_tensor_scan(out=P, data0=fl, data1=zer, initial=carry[:, 0:1], op0=ALU.add, op1=ALU.add,
                                         r=[kfl, "zer", "carry"], w=[kP])
                P_dve.tensor_copy(out=carry[:, 0:1], in_=P[:, SEG - 1:SEG], r=[kP], w=["carry"])
                P_dve.tensor_copy(out=pc[:, 0, :], in_=P, r=[kP], w=[kpc + "0"])
                P_dve.tensor_tensor(out=r1, in0=P, in1=pc[:, 0, :], op=ALU.subtract, r=[kP, kpc + "0"], w=[kr1])
                P_dve.tensor_copy(out=pc[:, 1, :], in_=r1, r=[kr1], w=[kpc + "1"])
                P_dve.tensor_tensor(out=r1, in0=r1, in1=pc[:, 1, :], op=ALU.subtract, r=[kr1, kpc + "1"], w=[kr1])
                P_dve.tensor_copy(out=pc[:, 2, :], in_=r1, r=[kr1], w=[kpc + "2"])
                P_dve.tensor_scalar(out=ncp, in0=pc, scalar1=-1.0, scalar2=None, op0=ALU.mult,
                                    r=[kpc + "0", kpc + "1", kpc + "2"], w=[knc])
                sch.dma(fox_qT[:, 64:67, s0:s0 + SEG], ncp, r=[knc], w=[])
                sch.dma(fox_qT[:, 67:70, s0:s0 + SEG], onesr, r=["onesr"], w=[])
                sch.dma(fox_kT[:, 64:67, s0:s0 + SEG], onesr, r=["onesr"], w=[])
                sch.dma(fox_kT[:, 67:70, s0:s0 + SEG], pc, r=[kpc + "0", kpc + "1", kpc + "2"], w=[])
            sch.barrier()
            ar.release(mF)

            qTr = Ring("qT", [ar.alloc([S], BF16, parts=96) for _ in range(2)])
            kTr = Ring("kT", [ar.alloc([S], BF16, parts=96) for _ in range(2)])
            vr = Ring("v", [ar.alloc([64, 65], BF16) for _ in range(2)])
            PT2r = Ring("PT2", [ar.alloc([2 * CH], BF16) for _ in range(3)])
            pair_i = [0]
            pair_slots = [(["bank0", "bank1"], allb[:, 0:1024]), (["bank2", "bank3"], allb[:, 1024:2048])]
            dbkr = Ring("bank", [banks[4][:], banks[5][:]])
            dbkr.aps = [banks[4][:], banks[5][:]]
            E2r = Ring("E2", [ar.alloc([2 * CH], F32) for _ in range(2)])
            Lp2r = Ring("Lp2", [ar.alloc([2 * CH], BF16) for _ in range(3)])
            A2r = Ring("A2", [ar.alloc([2 * CH], F32) for _ in range(2)])
            W2r = Ring("W2", [ar.alloc([2 * CH], BF16) for _ in range(3)])
            csb_i = [0]
            Cpr = Ring("Cp", [ar.alloc([CH], F32) for _ in range(2)])
            gater = Ring("gate", [ar.alloc([CH], F32, parts=64) for _ in range(3)])
            Osbr = Ring("Osb", [ar.alloc([CH], F32, parts=65) for _ in range(2)])
            rDr = Ring("rD", [ar.alloc([CH], F32, parts=64) for _ in range(2)])
            Gstr = Ring("Gst", [ar.alloc([CH], BF16, parts=64) for _ in range(3)])
            short = Ring("bank", [banks[i][:] for i in range(5 if (junk_fox or junk_sb) else 6)])
            oaccr = Ring("oacc", [banks[6][:], banks[7][:]])
            junkb = banks[5][:]
            jsrc = cbf.rearrange("p a b -> p (a b)")

            def junk(ncols):
                if ncols:
                    P_pe.matmul(junkb[:, 0:ncols], lhsT=ones_bf, rhs=jsrc[:, 0:ncols], start=True, stop=True, skip_group_check=True, r=[], w=[])

            heads = [(kind, j) for kind in ("fox", "sb", "mla", "mem") for j in range(NH)]
            if heads_lim is not None:
                heads = [heads[i] for i in heads_lim]
            gidx = {"fox": 0, "sb": 1, "mla": 2, "mem": 3}
            srcs = {"fox": (fox_qT, fox_kT, fox_v, 70, 65), "sb": (sb_qT, sb_kT, sb_v, 64, 64),
                    "mla": (mla_qT, mla_kT, mla_v, 96, 65), "mem": (mem_qT, mem_kT, mem_v, 64, 65)}

            def load_head(kind, j):
                qd, kd, vd, KQ, DV = srcs[kind]
                kq, qT = qTr.next()
                kk, kT = kTr.next()
                kv, v = vr.next()
                sk_ = MEM if kind == "mem" else S
                sch.dma(qT[0:KQ, :], qd[j, :, :], w=[kq])
                sch.dma(kT[0:KQ, 0:sk_], kd[j, :, :], w=[kk])
                sch.dma(v[:, 0:sk_ // 128, 0:DV], vd[j, :, :].rearrange("(b p) c -> p b c", p=128), w=[kv])
                return (kq, qT, kk, kT, kv, v)

            loaded = load_head(*heads[0]) if heads else None
            for hi, (kind, j) in enumerate(heads):
                kq, qT, kk, kT, kv, v = loaded
                if hi + 1 < len(heads):
                    loaded = load_head(*heads[hi + 1])
                _, _, _, KQ, DV = srcs[kind]
                row0 = gidx[kind] * QW + j * 64
                scale = MLA_SCALE if kind == "mla" else 1.0
                for c in range(nchunk):
                    c0 = c * CH
                    kg, gt = gater.next()
                    sch.dma(gt, gateT[row0:row0 + 64, c0:c0 + CH], w=[kg])
                    ko, oacc = oaccr.next()
                    if kind == "mem":
                        units = [(0, 0, False), (1, 0, False)]
                    elif kind == "sb":
                        units = [(kb_, max(0, kb_ - 4 * c), kb_ >= 4 * c) for kb_ in range(4 * c + 3, -1, -1)]
                    else:
                        units = [(kb_, max(0, kb_ - 4 * c), kb_ >= 4 * c) for kb_ in range(0, 4 * c + 4)]
                    nu = len(units)

                    def s_mm(u, addmask):
                        kb_, jj, diag = units[u]
                        kbk, bk = short.next()
                        lo = jj * 128
                        P_pe.matmul(bk[:, lo:CH], lhsT=kT[0:KQ, kb_ * 128:(kb_ + 1) * 128], rhs=qT[0:KQ, c0 + lo:c0 + CH],
                                    start=True, stop=not (diag and addmask), skip_group_check=True, r=[kk, kq], w=[kbk])
                        return kbk, bk, lo

                    if kind != "sb":
                        groups = []
                        i_ = 0
                        while i_ < nu:
                            if (not units[i_][2]) and i_ + 1 < nu and (not units[i_ + 1][2]):
                                groups.append([i_, i_ + 1])
                                i_ += 2
                            else:
                                groups.append([i_])
                                i_ += 1
                        ng = len(groups)

                        def s_grp(g):
                            keys, pb = pair_slots[pair_i[0] % 2]
                            pair_i[0] += 1
                            for idx, u in enumerate(groups[g]):
                                kb_, jj, diag = units[u]
                                lo = jj * 128
                                P_pe.matmul(pb[:, idx * 512 + lo:(idx + 1) * 512], lhsT=kT[0:KQ, kb_ * 128:(kb_ + 1) * 128],
                                            rhs=qT[0:KQ, c0 + lo:c0 + CH], start=True, stop=not diag, skip_group_check=True,
                                            r=[kk, kq], w=[keys[idx]])
                                if diag:
                                    P_pe.matmul(pb[:, idx * 512 + lo:idx * 512 + lo + 128], lhsT=ident_bf, rhs=mneg_incl, start=False, stop=True,
                                                skip_group_check=True, r=[], w=[keys[idx]])
                            return keys, pb

                        gg = {}
                        for g in range(min(2, ng)):
                            gg[g] = s_grp(g)
                        for g in range(ng):
                            keys, pb = gg.pop(g)
                            us = groups[g]
                            kp, PT2 = PT2r.next()
                            if len(us) == 2:
                                P_act.activation(out=PT2[:, 0:2 * CH], in_=pb[:, 0:2 * CH], func=AF.Exp, scale=scale, r=keys, w=[kp])
                            else:
                                lo = units[us[0]][1] * 128
                                P_act.activation(out=PT2[:, lo:CH], in_=pb[:, lo:CH], func=AF.Exp, scale=scale, r=[keys[0]], w=[kp])
                            if g + 2 < ng:
                                gg[g + 2] = s_grp(g + 2)
                            for idx, u in enumerate(us):
                                kb_, jj, diag = units[u]
                                lo = jj * 128
                                P_pe.matmul(oacc[0:DV, lo:CH], lhsT=v[:, kb_, 0:DV], rhs=PT2[:, idx * 512 + lo:(idx + 1) * 512], start=(u == 0),
                                            stop=(u == nu - 1), skip_group_check=True, r=[kp, kv], w=[ko])
                        kos, Osb = Osbr.next()
                        P_dve.tensor_copy(out=Osb[0:65, :], in_=oacc[0:65, :], r=[ko], w=[kos])
                        kdb, dbk = dbkr.next()
                        kdb = "bank%d" % (4 + (dbkr.i - 1) % 2)
                        P_pe.matmul(dbk[0:64, :], lhsT=sel[0:65, :], rhs=Osb[0:65, :], start=True, stop=True, r=[kos], w=[kdb])
                        krd, rD = rDr.next()
                        P_dve.reciprocal(out=rD, in_=dbk[0:64, :], r=[kdb], w=[krd])
                        P_dve.tensor_tensor(out=rD, in0=rD, in1=Osb[0:64, :], op=ALU.mult, r=[krd, kos], w=[krd])
                        kgs, Gst = Gstr.next()
                        P_pool.tensor_tensor(out=Gst, in0=rD, in1=gt, op=ALU.mult, r=[krd, kg], w=[kgs])
                        sch.dma(gt_d[row0:row0 + 64, c0:c0 + CH], Gst, r=[kgs], w=[])
                    else:
                        kcp, Cp = Cpr.next()
                        P_pool.memset(Cp, 0.0, w=[kcp])
                        groups = []
                        i_ = 0
                        while i_ < nu:
                            if (not units[i_][2]) and i_ + 1 < nu and (not units[i_ + 1][2]):
                                groups.append([i_, i_ + 1])
                                i_ += 2
                            else:
                                groups.append([i_])
                                i_ += 1
                        ng = len(groups)
                        zz = {}
                        ll = {}

                        def z_grp(g):
                            keys, pb = pair_slots[pair_i[0] % 2]
                            pair_i[0] += 1
                            for idx, u in enumerate(groups[g]):
                                kb_, jj, diag = units[u]
                                lo = jj * 128
                                P_pe.matmul(pb[:, idx * 512 + lo:(idx + 1) * 512], lhsT=kT[0:KQ, kb_ * 128:(kb_ + 1) * 128],
                                            rhs=qT[0:KQ, c0 + lo:c0 + CH], start=True, stop=True, skip_group_check=True,
                                            r=[kk, kq], w=[keys[idx]])
                            zz[g] = (keys, pb)

                        def span(g):
                            us = groups[g]
                            if len(us) == 2:
                                return 0, 2 * CH
                            return units[us[0]][1] * 128, CH

                        def act12(g):
                            keys, pb = zz.pop(g)
                            us = groups[g]
                            a_, b_ = span(g)
                            ke, E2 = E2r.next()
                            kl, Lp2 = Lp2r.next()
                            P_act.activation(out=E2[:, a_:b_], in_=pb[:, a_:b_], func=AF.Exp, r=keys[0:len(us)], w=[ke])
                            P_act.activation(out=Lp2[:, a_:b_], in_=E2[:, a_:b_], func=AF.Ln, bias=1.0, scale=1.0, r=[ke], w=[kl])
                            if units[us[0]][2]:
                                P_dve.tensor_tensor(out=Lp2[:, a_:a_ + 128], in0=Lp2[:, a_:a_ + 128], in1=m01_strict, op=ALU.mult, r=[kl], w=[kl])
                            ll[g] = (kl, Lp2, keys, pb)

                        for g in range(min(2, ng)):
                            z_grp(g)
                        act12(0)
                        pend = None
                        for g in range(ng):
                            kl, Lp2, keys, pb = ll.pop(g)
                            us = groups[g]
                            csbs = []
                            for idx, u in enumerate(us):
                                kb_, jj, diag = units[u]
                                lo = jj * 128
                                off = idx * 512
                                P_pe.matmul(pb[:, off + lo:off + CH], lhsT=negU, rhs=Lp2[:, off + lo:off + CH], start=False, stop=not diag,
                                            skip_group_check=True, r=[kl], w=[keys[idx]])
                                if diag:
                                    P_pe.matmul(pb[:, off + lo:off + lo + 128], lhsT=ident_bf, rhs=mneg_strict, start=False, stop=True,
                                                skip_group_check=True, r=[], w=[keys[idx]])
                                csb = banks[4 + csb_i[0] % 2][:]
                                kcs_ = "bank%d" % (4 + csb_i[0] % 2)
                                csb_i[0] += 1
                                P_pe.matmul(csb[:, lo:CH], lhsT=ones_bf, rhs=Lp2[:, off + lo:off + CH], start=True, stop=True, r=[kl], w=[kcs_])
                                csbs.append((kcs_, csb))
                            if g + 1 < ng:
                                act12(g + 1)
                            ka, A2 = A2r.next()
                            kas = []
                            for idx, u in enumerate(us):
                                kb_, jj, diag = units[u]
                                lo = jj * 128
                                off = idx * 512
                                kcs_, csb = csbs[idx]
                                P_dve.tensor_tensor(out=A2[:, off + lo:off + CH], in0=pb[:, off + lo:off + CH], in1=Cp[:, lo:CH], op=ALU.subtract,
                                                    r=[keys[idx], kcp], w=[ka + "_%d" % idx])
                                P_dve.tensor_tensor(out=Cp[:, lo:CH], in0=csb[:, lo:CH], in1=Cp[:, lo:CH], op=ALU.add, r=[kcs_, kcp], w=[kcp])
                                kas.append(ka + "_%d" % idx)
                            if g + 2 < ng:
                                z_grp(g + 2)
                            if pend is not None:
                                pend()
                            kw, W2 = W2r.next()
                            a_, b_ = span(g)
                            P_act.activation(out=W2[:, a_:b_], in_=A2[:, a_:b_], func=AF.Exp, r=kas, w=[kw])

                            def pv(W2=W2, kw=kw, us=us):
                                for idx, u in enumerate(us):
                                    kb_, jj, diag = units[u]
                                    lo = jj * 128
                                    off = idx * 512
                                    P_pe.matmul(oacc[0:64, lo:CH], lhsT=v[:, kb_, 0:64], rhs=W2[:, off + lo:off + CH], start=(u == 0),
                                                stop=(u == nu - 1), skip_group_check=True, r=[kw, kv], w=[ko])
                            pend = pv
                        pend()
                        kgs, Gst = Gstr.next()
                        P_dve.tensor_tensor(out=Gst, in0=oacc[0:64, :], in1=gt, op=ALU.mult, r=[ko, kg], w=[kgs])
                        sch.dma(gt_d[row0:row0 + 64, c0:c0 + CH], Gst, r=[kgs], w=[])
            sch.barrier()
            ar.release(mW)

        sch.barrier()
        sch.emit(nc)
    if _os.environ.get('KPEAK'):
        print('arena peak words', ar.peak, 'of', NW)
    return nc, sch.nins


NHK = 4


def _consts():
    i = np.arange(128)
    ident = np.eye(128, dtype=np.float32)
    ones = np.ones((128, 128), np.float32)
    negU = -(i[:, None] >= i[None, :]).astype(np.float32)
    m_incl = np.where(i[:, None] <= i[None, :], 0.0, NEG).astype(np.float32)
    m_strict = np.where(i[:, None] < i[None, :], 0.0, NEG).astype(np.float32)
    m01 = (i[:, None] < i[None, :]).astype(np.float32)
    c_bf = np.concatenate([ident, ones, negU, m_incl, m_strict, m01], axis=1).astype(ml_dtypes.bfloat16)
    sel = np.zeros((65, 64), np.float32)
    sel[64, :] = 1.0
    half = 16
    inv_freq = (np.float32(10000.0) ** (-np.arange(half, dtype=np.float32) / np.float32(half))).astype(np.float32)
    ang = (np.arange(S, dtype=np.float32)[:, None] * inv_freq[None, :]).astype(np.float32)
    cos = np.cos(ang).astype(np.float32).T
    sin = np.sin(ang).astype(np.float32).T
    c_cos = np.ascontiguousarray(np.concatenate([cos] * NHK, axis=0))
    c_sin = np.ascontiguousarray(np.concatenate([sin] * NHK, axis=0))
    return dict(c_bf=np.ascontiguousarray(c_bf), c_id=ident, c_sel=sel, c_cos=c_cos, c_sin=c_sin)


def _rep(v):
    return np.ascontiguousarray(np.broadcast_to(np.asarray(v, np.float32)[None, :], (128, v.shape[0])))


def _w_in_cols():
    o_fq, o_fk, o_fv, o_fl = 0, 256, 512, 768
    o_sq, o_sk, o_sv = 772, 1028, 1284
    o_cq, o_ckv, o_kr = 1540, 1796, 1924
    o_mq, o_gate = 1956, 2212
    r = lambda a, n: list(range(a, a + n))
    cols = r(o_fq, 256) + r(o_fk, 256) + r(o_sq, 256) + r(o_sk, 256) + r(o_mq, 256) + r(o_gate, 1024)
    cols += r(o_cq, 256) + r(o_ckv, 128) + r(o_kr, 16) + r(o_kr + 16, 16) + r(o_fl, 4) + r(o_fv, 256) + r(o_sv, 256)
    assert len(cols) == 3236
    return np.array(cols)


def make_in_maps(x, mem, ln_in_g, ln_in_b, mem_ln_g, mem_ln_b, w_in, b_forget, mla_q_norm_g, w_mla_q_up,
                 mla_kv_norm_g, w_mla_kv_up, w_mem_kv, w_out, ln_g, ln_b):
    f = lambda a: np.ascontiguousarray(np.asarray(a, dtype=np.float32))
    cst = _consts()
    wcols = _w_in_cols()
    hs = range(4)
    qcols = [h * 96 + d for h in hs for d in range(64)] + [h * 96 + 64 + d for h in hs for d in range(16)] + \
            [h * 96 + 80 + d for h in hs for d in range(16)]
    kvcols = [h * 128 + d for h in hs for d in range(64)] + [h * 128 + 64 + d for h in hs for d in range(64)]
    shared = dict(mlng=_rep(f(mem_ln_g)), mlnb=_rep(f(mem_ln_b)), **cst)
    lgs = [f(ln_in_g)] + [f(ln_g)[l] for l in range(DEPTH)]
    lbs = [f(ln_in_b)] + [f(ln_b)[l] for l in range(DEPTH)]
    for i in range(DEPTH + 1):
        shared["lng%d" % i] = _rep(lgs[i])
        shared["lnb%d" % i] = _rep(lbs[i])
    for l in range(DEPTH):
        shared["w_in%d" % l] = f(f(w_in)[l][:, wcols])
        shared["b_f%d" % l] = f(f(b_forget)[l].reshape(4, 1))
        shared["gq%d" % l] = f(f(mla_q_norm_g)[l].reshape(2, 128).T)
        shared["gkv%d" % l] = f(f(mla_kv_norm_g)[l].reshape(128, 1))
        shared["w_qup%d" % l] = f(f(w_mla_q_up)[l][:, qcols])
        shared["w_kvup%d" % l] = f(f(w_mla_kv_up)[l][:, kvcols])
        shared["w_mem%d" % l] = f(f(w_mem_kv)[l])
        shared["w_out%d" % l] = f(f(w_out)[l])
    maps = []
    for b in range(4):
        m = dict(shared)
        m["x"] = f(x[b])
        m["mem"] = f(mem[b])
        maps.append(m)
    return maps


_PROG = {}


def kernel(x, mem, ln_in_g, ln_in_b, mem_ln_g, mem_ln_b, w_in, b_forget, mla_q_norm_g, w_mla_q_up,
           mla_kv_norm_g, w_mla_kv_up, w_mem_kv, w_out, ln_g, ln_b):
    if "fused" not in _PROG:
        _PROG["fused"] = build_fused()[0]
    nc = _PROG["fused"]
    maps = make_in_maps(x, mem, ln_in_g, ln_in_b, mem_ln_g, mem_ln_b, w_in, b_forget, mla_q_norm_g, w_mla_q_up,
                        mla_kv_norm_g, w_mla_kv_up, w_mem_kv, w_out, ln_g, ln_b)
    res = run_bass_kernel_spmd(nc, maps, core_ids=list(range(4))).results
    return np.stack([np.asarray(res[b]["out"]) for b in range(4)], axis=0).astype(np.float32)
```

```python
import math
import numpy as np
import ml_dtypes
import concourse.bass as bass
import concourse.mybir as mybir
from concourse.bass_utils import run_bass_kernel_spmd

F32 = mybir.dt.float32
BF16 = mybir.dt.bfloat16
AF = mybir.ActivationFunctionType
ALU = mybir.AluOpType

S = 8192
D = 1024
MEM = 256
NCH = 16
CH = 512
HPG = 2
DEPTH = 2
ALPHA = (2 * DEPTH) ** 0.25
LN_EPS = 1e-5
RMS_EPS = 1e-6
NEG = -30000.0
MLA_SCALE = 96 ** -0.5

FQ, FK, SQ, SK, MQ = 0, 128, 256, 384, 512
GATE = 640
CQ = 1152
CKV = 1408
KR1 = 1536
KR2 = 1552
FL = 1568
FV = 1570
SV = 1698
NCOL = 1826

ENGS = ["pe", "act", "dve", "pool", "sp"]
NDSEM = 24
import os as _os
SUB = int(_os.environ.get('SUB', '9'))


class Sched:
    def __init__(self):
        self.q = {e: [] for e in ENGS}
        self.cnt = {e: 0 for e in ENGS}
        self.seen = {e: {} for e in ENGS}
        self.rw = {}
        self.rr = {}
        self.dval = [0] * NDSEM
        self.drr = 0
        self.nins = 0

    def _need(self, eng, reads, writes):
        need = {}

        def add(d, skip_pe):
            for sk, v in d.items():
                if skip_pe and sk == "pe" and eng == "pe":
                    continue
                if need.get(sk, 0) < v:
                    need[sk] = v

        for k in reads:
            add(self.rw.get(k, {}), False)
        for k in writes:
            add(self.rw.get(k, {}), True)
            add(self.rr.get(k, {}), False)
        out = []
        for sk, v in need.items():
            if self.seen[eng].get(sk, 0) >= v:
                continue
            self.seen[eng][sk] = v
            out.append((sk, v))
        return out

    def _record(self, tok, reads, writes):
        sk, v = tok
        for k in reads:
            self.rr.setdefault(k, {})[sk] = v
        for k in writes:
            self.rw[k] = {sk: v}
            self.rr[k] = {}

    def op(self, eng, meth, args, kwargs, r=(), w=()):
        waits = self._need(eng, r, w)
        self.cnt[eng] += 1
        tok = (eng, self.cnt[eng])
        self.q[eng].append((waits, (meth, args, kwargs), tok))
        self._record(tok, r, w)
        self.nins += 1 + max(0, len(waits) - 1)

    def proxy(self, eng):
        sch = self

        class _P:
            def __getattr__(self, meth):
                def f(*args, r=(), w=(), **kwargs):
                    sch.op(eng, meth, args, kwargs, r, w)
                return f
        return _P()

    def dma(self, out, in_, r=(), w=()):
        eng = "sp"
        waits = self._need(eng, r, w)
        i = self.drr
        self.drr = (self.drr + 1) % NDSEM
        sk = ("d", i)
        if self.dval[i] > 0 and self.seen[eng].get(sk, 0) < self.dval[i]:
            self.seen[eng][sk] = self.dval[i]
            waits.append((sk, self.dval[i]))
        self.dval[i] += 16
        tok = (sk, self.dval[i])
        self.q[eng].append((waits, (out, in_), tok))
        self._record(tok, r, w)
        self.nins += 1 + max(0, len(waits) - 1)

    def barrier(self):
        for e in ENGS:
            waits = []
            for o in ENGS:
                if o == e or o == "sp":
                    continue
                if self.cnt[o] > self.seen[e].get(o, 0):
                    self.seen[e][o] = self.cnt[o]
                    waits.append((o, self.cnt[o]))
            for i in range(NDSEM):
                sk = ("d", i)
                if self.dval[i] > self.seen[e].get(sk, 0):
                    self.seen[e][sk] = self.dval[i]
                    waits.append((sk, self.dval[i]))
            if waits:
                self.q[e].append((waits, None, None))
                self.nins += len(waits)
        self.rw = {}
        self.rr = {}

    def emit(self, nc):
        import contextlib

        with contextlib.ExitStack() as st:
            sems = {}
            for e in ["pe", "act", "dve", "pool"]:
                sems[e] = st.enter_context(nc.semaphore("s_" + e))
            for i in range(NDSEM):
                sems[("d", i)] = st.enter_context(nc.semaphore("s_d%d" % i))
            block = st.enter_context(nc.Block())

            def run(eng, e):
                for waits, fn, tok in self.q[eng]:
                    if fn is None:
                        for sk, v in waits:
                            e.wait_ge(sems[sk], v)
                        continue
                    if eng == "pe":
                        for sk, v in waits:
                            e.wait_ge(sems[sk], v)
                        waits = []
                    for sk, v in waits[:-1]:
                        e.wait_ge(sems[sk], v)
                    if eng == "sp":
                        ins = e.dma_start(out=fn[0], in_=fn[1])
                    else:
                        ins = getattr(e, fn[0])(*fn[1], **fn[2])
                    if waits:
                        ins._wait_ge(sems[waits[-1][0]], waits[-1][1])
                    if eng == "sp":
                        ins.then_inc(sems[tok[0]], 16)
                    else:
                        ins.then_inc(sems[eng], 1)

            @block.tensor
            def _(e):
                run("pe", e)

            @block.scalar
            def _(e):
                run("act", e)

            @block.vector
            def _(e):
                run("dve", e)

            @block.gpsimd
            def _(e):
                run("pool", e)

            @block.sync
            def _(e):
                run("sp", e)


class Arena:
    def __init__(self, t, nwords):
        self.t = t
        self.n = nwords
        self.off = 0

    def mark(self):
        return self.off

    def release(self, m):
        self.off = m

    def alloc(self, free_shape, dtype, parts=128):
        n = int(np.prod(free_shape))
        words = n if dtype == F32 else (n + 1) // 2
        words = (words + 1) // 2 * 2
        assert self.off + words <= self.n, ("arena overflow", self.off, words, self.n)
        ap = self.t[0:parts, self.off:self.off + words]
        self.off += words
        self.peak = max(getattr(self, "peak", 0), self.off)
        if dtype != F32:
            ap = ap.bitcast(dtype)
        ap = ap[:, 0:n]
        if len(free_shape) == 2:
            ap = ap.rearrange("p (a b) -> p a b", a=free_shape[0])
        elif len(free_shape) == 3:
            ap = ap.rearrange("p (a b c) -> p a b c", a=free_shape[0], b=free_shape[1])
        return ap


class Ring:
    def __init__(self, name, aps):
        self.name = name
        self.aps = aps
        self.i = 0

    def next(self):
        k = self.i % len(self.aps)
        self.i += 1
        return "%s%d" % (self.name, k), self.aps[k]


def build_fused(NH=4, depth=DEPTH, nchunk_lim=None, heads_lim=None, stages_lim=None, junk_fox=0, junk_sb=()):
    nc = bass.Bass("TRN2", target_bir_lowering=False)
    QW = NH * 64
    FQ, FK, SQ, SK, MQ = 0, QW, 2 * QW, 3 * QW, 4 * QW
    GATE = 5 * QW
    CQ = 9 * QW
    CKV = CQ + 256
    KR1 = CKV + 128
    KR2 = KR1 + 16
    FL = KR2 + 16
    FV = FL + NH
    SV = FV + QW
    NCOL = SV + QW
    MIX = 4 * QW
    RM = NH * 16
    dbg = bool(_os.environ.get("KDBG"))

    def din(name, shape, dt=F32):
        return nc.dram_tensor(name, list(shape), dt, kind="ExternalInput").ap()

    def dscr(name, shape, dt=BF16):
        return nc.dram_tensor(name, list(shape), dt, kind=("ExternalOutput" if dbg else "Internal")).ap()

    x_in = din("x", [S, D])
    mem_in = din("mem", [MEM, D])
    lng = [din("lng%d" % i, [128, D]) for i in range(depth + 1)]
    lnb = [din("lnb%d" % i, [128, D]) for i in range(depth + 1)]
    mlng = din("mlng", [128, D])
    mlnb = din("mlnb", [128, D])
    c_bf = din("c_bf", [128, 6 * 128], BF16)
    c_id = din("c_id", [128, 128])
    c_sel = din("c_sel", [65, 64])
    c_cos = din("c_cos", [RM, S])
    c_sin = din("c_sin", [RM, S])
    w_in = [din("w_in%d" % l, [D, NCOL]) for l in range(depth)]
    b_f = [din("b_f%d" % l, [NH, 1]) for l in range(depth)]
    gq = [din("gq%d" % l, [128, 2]) for l in range(depth)]
    gkv = [din("gkv%d" % l, [128, 1]) for l in range(depth)]
    w_qup = [din("w_qup%d" % l, [256, NH * 96]) for l in range(depth)]
    w_kvup = [din("w_kvup%d" % l, [128, 2 * QW]) for l in range(depth)]
    w_mem = [din("w_mem%d" % l, [D, 2 * QW]) for l in range(depth)]
    w_out = [din("w_out%d" % l, [MIX, D]) for l in range(depth)]
    out_d = nc.dram_tensor("out", [S, D], F32, kind="ExternalOutput").ap()

    h_d = dscr("h_scr", [S, D], F32)
    gt_d = dscr("gt_scr", [MIX, S])
    fox_qT = dscr("fox_qT", [NH, 70, S])
    fox_kT = dscr("fox_kT", [NH, 70, S])
    fox_v = dscr("fox_v", [NH, S, 65])
    sb_qT = dscr("sb_qT", [NH, 64, S])
    sb_kT = dscr("sb_kT", [NH, 64, S])
    sb_v = dscr("sb_v", [NH, S, 64])
    mla_qT = dscr("mla_qT", [NH, 96, S])
    mla_kT = dscr("mla_kT", [NH, 96, S])
    mla_v = dscr("mla_v", [NH, S, 65])
    mem_qT = dscr("mem_qT", [NH, 64, S])
    mem_kT = dscr("mem_kT", [NH, 64, MEM])
    mem_v = dscr("mem_v", [NH, MEM, 65])
    gateT = dscr("gateT", [MIX, S], F32)
    flogT = dscr("flogT", [NH, S], F32)

    sch = Sched()
    P_pe, P_act, P_dve, P_pool = sch.proxy("pe"), sch.proxy("act"), sch.proxy("dve"), sch.proxy("pool")
    NW = 52400
    import contextlib

    with contextlib.ExitStack() as st:
        arena_t = st.enter_context(nc.sbuf_tensor("arena", [128, NW], F32))
        allb = st.enter_context(nc.psum_tensor("allb", [128, 4096], F32))
        banks = [allb[:, i * 512:(i + 1) * 512] for i in range(8)]
        ar = Arena(arena_t, NW)

        cbf = ar.alloc([6, 128], BF16)
        ident_bf, ones_bf, negU, mneg_incl, mneg_strict, m01_strict = [cbf[:, i, :] for i in range(6)]
        ident_f = ar.alloc([128], F32)
        sel = ar.alloc([64], F32, parts=65)
        Gbc = ar.alloc([D], F32)
        Bbc = ar.alloc([D], F32)
        mGbc = ar.alloc([D], F32)
        mBbc = ar.alloc([D], F32)
        epsln = ar.alloc([2], F32)
        nbf = ar.alloc([2], F32, parts=NH)
        small_ring = Ring("small", [ar.alloc([16], F32) for _ in range(4)])
        sch.dma(cbf, c_bf.rearrange("p (a b) -> p a b", a=6), w=["cbf"])
        sch.dma(ident_f, c_id, w=["ident_f"])
        sch.dma(sel, c_sel, w=["sel"])
        sch.dma(mGbc, mlng, w=["mGbc"])
        sch.dma(mBbc, mlnb, w=["mBbc"])
        P_pool.memset(epsln[:, 0:1], LN_EPS, w=["epsln"])
        P_pool.memset(epsln[:, 1:2], RMS_EPS, w=["epsln2"])
        sch.barrier()

        bank_ring = Ring("bank", [b[:] for b in banks])

        def layer_norm(R2, kR2, H, kH, G, B, small):
            kst, stt = small.next()
            st6 = stt[:, 0:12].rearrange("p (a b) -> p a b", a=2)
            mv = stt[:, 12:14]
            tmp = stt[:, 14:16]
            P_dve.bn_stats(out=st6[:, 0, :], in_=R2[:, 0:512], r=[kR2], w=[kst])
            P_dve.bn_stats(out=st6[:, 1, :], in_=R2[:, 512:1024], r=[kR2], w=[kst + "b"])
            P_dve.bn_aggr(out=mv, in_=stt[:, 0:12], r=[kst, kst + "b"], w=[kst + "mv"])
            P_act.activation(out=tmp[:, 0:1], in_=mv[:, 1:2], func=AF.Ln, bias=epsln[:, 0:1], scale=1.0,
                             r=[kst + "mv"], w=[kst + "t0"])
            P_act.activation(out=tmp[:, 1:2], in_=tmp[:, 0:1], func=AF.Exp, scale=-0.5, r=[kst + "t0"], w=[kst + "t1"])
            P_dve.tensor_scalar(out=H, in0=R2, scalar1=mv[:, 0:1], scalar2=tmp[:, 1:2], op0=ALU.subtract, op1=ALU.mult,
                                r=[kR2, kst + "mv", kst + "t1"], w=[kH])
            P_pool.tensor_tensor(out=H, in0=H, in1=G, op=ALU.mult, r=[kH], w=[kH])
            P_dve.tensor_tensor(out=H, in0=H, in1=B, op=ALU.add, r=[kH], w=[kH])

        def transposes(H, kH, hT, khT, col0):
            for half in range(2):
                kb, bk = bank_ring.next()
                for cc in range(4):
                    c = half * 4 + cc
                    P_pe.transpose(out=bk[:, cc * 128:(cc + 1) * 128], in_=H[:, c * 128:(c + 1) * 128], identity=ident_f,
                                   r=[kH], w=[kb])
                src = bk.rearrange("p (a b) -> p a b", a=4)
                dst = hT[:, half * 4:(half + 1) * 4, col0:col0 + 128]
                if half == 0:
                    P_act.activation(out=dst, in_=src, func=AF.Copy, r=[kb], w=[khT + "h%d_%d" % (half, col0)])
                else:
                    P_dve.tensor_copy(out=dst, in_=src, r=[kb], w=[khT + "h%d_%d" % (half, col0)])

        def hT_keys(khT, ncols):
            return [khT + "h%d_%d" % (half, c0) for half in range(2) for c0 in range(0, ncols, 128)]

        cast_i = [0]

        def cast(out, in_, r, w):
            k = cast_i[0] % 3
            cast_i[0] += 1
            if k == 0:
                P_dve.tensor_copy(out=out, in_=in_, r=r, w=w)
            elif k == 1:
                P_pool.tensor_copy(out=out, in_=in_, r=r, w=w)
            else:
                P_act.activation(out=out, in_=in_, func=AF.Copy, r=r, w=w)

        nstage = depth + 1 if stages_lim is None else stages_lim
        for stage in range(nstage):
            body = stage < depth
            prev = stage > 0
            l = stage
            mW = ar.mark()
            sch.dma(Gbc, lng[stage], w=["Gbc"])
            sch.dma(Bbc, lnb[stage], w=["Bbc"])
            if prev:
                wout = ar.alloc([8, D], BF16)
            if body:
                wbf = ar.alloc([8, NCOL], BF16)
                wmem = ar.alloc([8, 2 * QW], BF16)
                wqg = ar.alloc([2, NH * 96], BF16)
                wkvg = ar.alloc([2 * QW], BF16)
            m0 = ar.mark()
            stg = Ring("stg", [ar.alloc([NCOL], F32) for _ in range(3)])
            if prev:
                for c in range(8):
                    ks, sg_ = stg.next()
                    sch.dma(sg_[:, 0:D], w_out[l - 1][c * 128:(c + 1) * 128, :], w=[ks])
                    cast(wout[:, c, :], sg_[:, 0:D], [ks], ["wout%d" % c])
            if body:
                for c in range(8):
                    ks, sg_ = stg.next()
                    sch.dma(sg_, w_in[l][c * 128:(c + 1) * 128, :], w=[ks])
                    cast(wbf[:, c, :], sg_, [ks], ["wbf%d" % c])
                for c in range(8):
                    ks, sg_ = stg.next()
                    sch.dma(sg_[:, 0:2 * QW], w_mem[l][c * 128:(c + 1) * 128, :], w=[ks])
                    cast(wmem[:, c, :], sg_[:, 0:2 * QW], [ks], ["wmem%d" % c])
                gq_t = ar.alloc([2], F32)
                gkv_t = ar.alloc([2], F32)
                bf_t = ar.alloc([2], F32, parts=NH)
                sch.dma(gq_t, gq[l], w=["gq_t"])
                sch.dma(gkv_t[:, 0:1], gkv[l], w=["gkv_t"])
                sch.dma(bf_t[:, 0:1], b_f[l], w=["bf_t"])
                for c in range(2):
                    ks, sg_ = stg.next()
                    sch.dma(sg_[:, 0:NH * 96], w_qup[l][c * 128:(c + 1) * 128, :], w=[ks])
                    P_dve.tensor_scalar(out=wqg[:, c, :], in0=sg_[:, 0:NH * 96], scalar1=gq_t[:, c:c + 1], scalar2=None, op0=ALU.mult,
                                        r=[ks, "gq_t"], w=["wqg%d" % c])
                ks, sg_ = stg.next()
                sch.dma(sg_[:, 0:2 * QW], w_kvup[l], w=[ks])
                P_dve.tensor_scalar(out=wkvg, in0=sg_[:, 0:2 * QW], scalar1=gkv_t[:, 0:1], scalar2=None, op0=ALU.mult,
                                    r=[ks, "gkv_t"], w=["wkvg"])
                P_dve.tensor_scalar(out=nbf[:, 0:1], in0=bf_t[:, 0:1], scalar1=-1.0, scalar2=None, op0=ALU.mult, r=["bf_t"], w=["nbf"])
            sch.barrier()
            ar.release(m0)
            mA = ar.mark()

            if body:
                Rm = Ring("Rm", [ar.alloc([D], F32) for _ in range(2)])
                Hm = Ring("Hm", [ar.alloc([D], F32) for _ in range(2)])
                mT = ar.alloc([8, MEM], BF16)
                vaugm = ar.alloc([2, NH, 65], BF16)
                kmem_r = Ring("kmem", [ar.alloc([MEM], BF16) for _ in range(2)])
                P_pool.memset(vaugm, 1.0, w=["vaugm"])
                for i in range(2):
                    kR, R = Rm.next()
                    kH, H = Hm.next()
                    sch.dma(R, mem_in[i * 128:(i + 1) * 128, :], w=[kR])
                    layer_norm(R, kR, H, kH, mGbc, mBbc, small_ring)
                    transposes(H, kH, mT, "mT", i * 128)
                for t in range(NH // 2):
                    kb, bk = bank_ring.next()
                    for c in range(8):
                        P_pe.matmul(bk[:, 0:MEM], lhsT=wmem[:, c, t * 128:(t + 1) * 128], rhs=mT[:, c, :], start=(c == 0), stop=(c == 7),
                                    r=hT_keys("mT", MEM), w=[kb])
                    kkm, kmem_sb = kmem_r.next()
                    P_act.activation(out=kmem_sb, in_=bk[:, 0:MEM], func=AF.Copy, r=[kb], w=[kkm])
                    for jj in range(2):
                        sch.dma(mem_kT[2 * t + jj, :, :], kmem_sb[jj * 64:(jj + 1) * 64, :], r=[kkm], w=[])
                for i in range(2):
                    kb, bk = bank_ring.next()
                    for c in range(8):
                        P_pe.matmul(bk[:, 0:QW], lhsT=mT[:, c, i * 128:(i + 1) * 128], rhs=wmem[:, c, QW:2 * QW], start=(c == 0), stop=(c == 7),
                                    r=hT_keys("mT", MEM), w=[kb])
                    P_dve.tensor_copy(out=vaugm[:, i, :, 0:64], in_=bk[:, 0:QW].rearrange("p (j d) -> p j d", j=NH),
                                      r=[kb, "vaugm"], w=["vaugm%d" % i])
                    for j in range(NH):
                        sch.dma(mem_v[j, i * 128:(i + 1) * 128, :], vaugm[:, i, j, :], r=["vaugm%d" % i], w=[])
                sch.barrier()
                ar.release(mA)

            Rr = Ring("R", [ar.alloc([D], F32) for _ in range(2)])
            Hr = Ring("H", [ar.alloc([D], F32) for _ in range(2)])
            if prev:
                gtfr = Ring("gtf", [ar.alloc([8, CH], BF16) for _ in range(2)])
            if body:
                hTr = Ring("hT", [ar.alloc([8, CH], BF16) for _ in range(2)])
                evb = Ring("evb", [ar.alloc([CH], BF16) for _ in range(4)])
                evf = Ring("evf", [ar.alloc([CH], F32) for _ in range(3)])
                cqn_r = Ring("cqn", [ar.alloc([2, CH], BF16) for _ in range(2)])
                sq_r = Ring("sq", [ar.alloc([2, CH], BF16) for _ in range(2)])
                ckvn_r = Ring("ckvn", [ar.alloc([CH], BF16) for _ in range(2)])
                rq_r = Ring("rq", [ar.alloc([CH], F32) for _ in range(2)])
                cs_r = Ring("cs", [ar.alloc([2, CH], F32, parts=RM) for _ in range(2)])
                rope_r = Ring("rope", [ar.alloc([CH], F32, parts=RM) for _ in range(4)])
                ropeo_r = Ring("ropeo", [ar.alloc([CH], BF16, parts=RM) for _ in range(4)])
                vaugF_r = Ring("vaugF", [ar.alloc([4, NH, 65], BF16) for _ in range(2)])
                vS_r = Ring("vS", [ar.alloc([4, NH, 64], BF16) for _ in range(2)])
                vaugM_r = Ring("vaugM", [ar.alloc([4, NH, 65], BF16) for _ in range(2)])
                fl_r = Ring("fl", [ar.alloc([CH], F32, parts=NH) for _ in range(2)])
                for k_, t_ in zip(["vaugF0", "vaugF1"], vaugF_r.aps):
                    P_pool.memset(t_, 1.0, w=[k_])
                for k_, t_ in zip(["vaugM0", "vaugM1"], vaugM_r.aps):
                    P_pool.memset(t_, 1.0, w=[k_])

            nchunk = NCH if nchunk_lim is None else min(NCH, nchunk_lim)
            src_d = x_in if stage == 0 else h_d
            dst_d = out_d if stage == depth else h_d
            cctx = {}
            rload = {}
            gload = {}
            hkeep = {}
            ntile_all = nchunk * 4

            def load_R(gi):
                if gi in rload or gi >= ntile_all:
                    return
                kR, R = Rr.next()
                sch.dma(R, src_d[gi * 128:(gi + 1) * 128, :], r=["hrow%d" % gi], w=[kR])
                rload[gi] = (kR, R)

            def load_g(ci):
                if (not prev) or ci in gload or ci >= nchunk:
                    return
                kg, gtf = gtfr.next()
                sch.dma(gtf, gt_d[:, ci * CH:(ci + 1) * CH].rearrange("(c p) s -> p c s", p=128), w=[kg])
                gload[ci] = (kg, gtf)

            def pro_a(ci, ti):
                gi = ci * 4 + ti
                if ti == 0:
                    ctx = {}
                    if body:
                        ctx["khT"], ctx["hT"] = hTr.next()
                    cctx[ci] = ctx
                    load_g(ci)
                    load_g(ci + 1)
                load_R(gi)
                load_R(gi + 1)
                kR, R = rload.pop(gi)
                kH, H = Hr.next()
                if prev:
                    kg, gtf = gload[ci]
                    for half in range(2):
                        kb, bk = bank_ring.next()
                        for c in range(8):
                            P_pe.matmul(bk, lhsT=gtf[:, c, ti * 128:(ti + 1) * 128], rhs=wout[:, c, half * 512:(half + 1) * 512],
                                        start=(c == 0), stop=(c == 7), r=[kg], w=[kb])
                        P_dve.scalar_tensor_tensor(out=R[:, half * 512:(half + 1) * 512], in0=R[:, half * 512:(half + 1) * 512],
                                                   scalar=ALPHA, in1=bk, op0=ALU.mult, op1=ALU.add, r=[kb, kR], w=[kR])
                layer_norm(R, kR, H, kH, Gbc, Bbc, small_ring)
                sch.dma(dst_d[gi * 128:(gi + 1) * 128, :], H, r=[kH], w=["hrow%d" % gi])
                hkeep[gi] = (kH, H)

            def pro_b(ci, ti):
                kH, H = hkeep.pop(ci * 4 + ti)
                if body:
                    transposes(H, kH, cctx[ci]["hT"], cctx[ci]["khT"], ti * 128)

            for ci in range(nchunk):
                c0 = ci * CH
                if ci == 0 or not body:
                    for ti in range(4):
                        pro_a(ci, ti)
                        pro_b(ci, ti)
                if not body:
                    continue
                khT, hT = cctx[ci]["khT"], cctx[ci]["hT"]

                def nxt(k, ci=ci):
                    if ci + 1 >= nchunk:
                        return
                    if k >= 1:
                        pro_b(ci + 1, k - 1)
                    if k <= 3:
                        pro_a(ci + 1, k)

                nxt(0)
                hk = hT_keys(khT, CH)

                def fm_tile(col, m, hT=hT, hk=hk):
                    kb, bk = bank_ring.next()
                    for c in range(8):
                        P_pe.matmul(bk[0:m, :], lhsT=wbf[:, c, col:col + m], rhs=hT[:, c, :], start=(c == 0), stop=(c == 7), r=hk, w=[kb])
                    return kb, bk

                for col, dst, scl in ((FQ, fox_qT, 0.125), (FK, fox_kT, 1.0), (SQ, sb_qT, 0.125), (SK, sb_kT, 1.0), (MQ, mem_qT, 0.125)):
                    for t in range(NH // 2):
                        kb, bk = fm_tile(col + t * 128, 128)
                        ke, ev = evb.next()
                        P_act.activation(out=ev, in_=bk, func=AF.Copy, scale=scl, r=[kb], w=[ke])
                        for jj in range(2):
                            sch.dma(dst[2 * t + jj, 0:64, c0:c0 + CH], ev[jj * 64:(jj + 1) * 64, :], r=[ke], w=[])
                nxt(1)
                for t in range(MIX // 128):
                    kb, bk = fm_tile(GATE + t * 128, 128)
                    ke, ev = evf.next()
                    P_act.activation(out=ev, in_=bk, func=AF.Exp, scale=-1.0, r=[kb], w=[ke])
                    P_act.activation(out=ev, in_=ev, func=AF.Ln, bias=1.0, scale=1.0, r=[ke], w=[ke])
                    P_act.activation(out=ev, in_=ev, func=AF.Exp, scale=-1.0, r=[ke], w=[ke])
                    P_dve.tensor_tensor(out=ev, in0=bk, in1=ev, op=ALU.mult, r=[kb, ke], w=[ke])
                    sch.dma(gateT[t * 128:(t + 1) * 128, c0:c0 + CH], ev, r=[ke], w=[])
                nxt(2)
                kb, bk = fm_tile(FL, NH)
                kf, fl = fl_r.next()
                P_dve.tensor_copy(out=fl, in_=bk[0:NH, :], r=[kb], w=[kf])
                sch.dma(flogT[:, c0:c0 + CH], fl, r=[kf], w=[])
                kvf, vaugF = vaugF_r.next()
                kvs, vS = vS_r.next()
                for ti in range(4):
                    kb, bk = bank_ring.next()
                    for c in range(8):
                        P_pe.matmul(bk[:, 0:2 * QW], lhsT=hT[:, c, ti * 128:(ti + 1) * 128], rhs=wbf[:, c, FV:FV + 2 * QW],
                                    start=(c == 0), stop=(c == 7), r=hk, w=[kb])
                    P_act.activation(out=vaugF[:, ti, :, 0:64], in_=bk[:, 0:QW].rearrange("p (j d) -> p j d", j=NH), func=AF.Copy,
                                     r=[kb, kvf], w=[kvf + "_%d" % ti])
                    P_act.activation(out=vS[:, ti, :, :], in_=bk[:, QW:2 * QW].rearrange("p (j d) -> p j d", j=NH), func=AF.Copy,
                                     r=[kb], w=[kvs + "_%d" % ti])
                for j in range(NH):
                    sch.dma(fox_v[j, c0:c0 + CH, :].rearrange("(t p) c -> p t c", p=128), vaugF[:, :, j, :],
                            r=[kvf + "_%d" % ti for ti in range(4)], w=[])
                    sch.dma(sb_v[j, c0:c0 + CH, :].rearrange("(t p) c -> p t c", p=128), vS[:, :, j, :],
                            r=[kvs + "_%d" % ti for ti in range(4)], w=[])
                nxt(3)
                kcs, cs = cs_r.next()
                sch.dma(cs[:, 0, :], c_cos[:, c0:c0 + CH], w=[kcs + "c"])
                sch.dma(cs[:, 1, :], c_sin[:, c0:c0 + CH], w=[kcs + "s"])

                def rope(b1, kb1, b2, kb2, m, dsts, cs=cs, kcs=kcs):
                    ka, a = rope_r.next()
                    kb_, b = rope_r.next()
                    ko1, o1 = ropeo_r.next()
                    ko2, o2 = ropeo_r.next()
                    cosv, sinv = cs[0:m, 0, :], cs[0:m, 1, :]
                    P_dve.tensor_tensor(out=a[0:m, :], in0=b1, in1=cosv, op=ALU.mult, r=[kb1, kcs + "c"], w=[ka])
                    P_dve.tensor_tensor(out=b[0:m, :], in0=b2, in1=sinv, op=ALU.mult, r=[kb2, kcs + "s"], w=[kb_])
                    P_pool.tensor_tensor(out=o1[0:m, :], in0=a[0:m, :], in1=b[0:m, :], op=ALU.subtract, r=[ka, kb_], w=[ko1])
                    kc, c_ = rope_r.next()
                    kd, d_ = rope_r.next()
                    P_dve.tensor_tensor(out=c_[0:m, :], in0=b1, in1=sinv, op=ALU.mult, r=[kb1, kcs + "s"], w=[kc])
                    P_dve.tensor_tensor(out=d_[0:m, :], in0=b2, in1=cosv, op=ALU.mult, r=[kb2, kcs + "c"], w=[kd])
                    P_pool.tensor_tensor(out=o2[0:m, :], in0=c_[0:m, :], in1=d_[0:m, :], op=ALU.add, r=[kc, kd], w=[ko2])
                    for (dst_ap, lo, which) in dsts:
                        src = (o1 if which == 0 else o2)[lo:lo + 16, :]
                        sch.dma(dst_ap, src, r=[ko1 if which == 0 else ko2], w=[])

                def rms_bcast(pstiles, nt):
                    ksq, sq = sq_r.next()
                    for t, (kb, bk) in enumerate(pstiles):
                        ke, ev = evf.next()
                        P_act.activation(out=ev, in_=bk, func=AF.Copy, r=[kb], w=[ke])
                        P_dve.tensor_tensor(out=sq[:, t, :], in0=ev, in1=bk, op=ALU.mult, r=[kb, ke], w=[ksq + "_%d" % t])
                    kss, ss = bank_ring.next()
                    for t in range(nt):
                        P_pe.matmul(ss, lhsT=ones_bf, rhs=sq[:, t, :], start=(t == 0), stop=(t == nt - 1), r=[ksq + "_%d" % t], w=[kss])
                    krq, rq = rq_r.next()
                    P_act.activation(out=rq, in_=ss, func=AF.Ln, bias=epsln[:, 1:2], scale=1.0 / (128.0 * nt), r=[kss], w=[krq])
                    P_act.activation(out=rq, in_=rq, func=AF.Exp, scale=-0.5, r=[krq], w=[krq])
                    return krq, rq

                cqt = [fm_tile(CQ + t * 128, 128) for t in range(2)]
                krq, rq = rms_bcast(cqt, 2)
                kcq, cqn = cqn_r.next()
                for t, (kb, bk) in enumerate(cqt):
                    P_dve.tensor_tensor(out=cqn[:, t, :], in0=bk, in1=rq, op=ALU.mult, r=[kb, krq], w=[kcq + "_%d" % t])
                kcqs = [kcq + "_0", kcq + "_1"]
                for tt in range(NH // 2):
                    kb, bk = bank_ring.next()
                    for t in range(2):
                        P_pe.matmul(bk, lhsT=wqg[:, t, tt * 128:(tt + 1) * 128], rhs=cqn[:, t, :], start=(t == 0), stop=(t == 1), r=kcqs, w=[kb])
                    ke, ev = evb.next()
                    P_act.activation(out=ev, in_=bk, func=AF.Copy, r=[kb], w=[ke])
                    for jj in range(2):
                        sch.dma(mla_qT[2 * tt + jj, 0:64, c0:c0 + CH], ev[jj * 64:(jj + 1) * 64, :], r=[ke], w=[])
                kb1, bk1 = bank_ring.next()
                kb2, bk2 = bank_ring.next()
                for t in range(2):
                    P_pe.matmul(bk1[0:RM, :], lhsT=wqg[:, t, QW:QW + RM], rhs=cqn[:, t, :], start=(t == 0), stop=(t == 1), r=kcqs, w=[kb1])
                for t in range(2):
                    P_pe.matmul(bk2[0:RM, :], lhsT=wqg[:, t, QW + RM:QW + 2 * RM], rhs=cqn[:, t, :], start=(t == 0), stop=(t == 1), r=kcqs, w=[kb2])
                rope(bk1[0:RM, :], kb1, bk2[0:RM, :], kb2, RM,
                     [(mla_qT[j, 64 + 16 * w_:80 + 16 * w_, c0:c0 + CH], j * 16, w_) for j in range(NH) for w_ in range(2)])
                nxt(4)
                ckt = [fm_tile(CKV, 128)]
                krk, rk = rms_bcast(ckt, 1)
                kck, ckvn = ckvn_r.next()
                P_dve.tensor_tensor(out=ckvn, in0=ckt[0][1], in1=rk, op=ALU.mult, r=[ckt[0][0], krk], w=[kck])
                for tt in range(NH // 2):
                    kb, bk = bank_ring.next()
                    P_pe.matmul(bk, lhsT=wkvg[:, tt * 128:(tt + 1) * 128], rhs=ckvn, start=True, stop=True, r=[kck], w=[kb])
                    ke, ev = evb.next()
                    P_act.activation(out=ev, in_=bk, func=AF.Copy, r=[kb], w=[ke])
                    for jj in range(2):
                        sch.dma(mla_kT[2 * tt + jj, 0:64, c0:c0 + CH], ev[jj * 64:(jj + 1) * 64, :], r=[ke], w=[])
                kvm, vaugM = vaugM_r.next()
                per_bank = 512 // QW
                for tb in range(4 // per_bank):
                    kb, bk = bank_ring.next()
                    for tq in range(per_bank):
                        ti = tb * per_bank + tq
                        P_pe.matmul(bk[:, tq * QW:(tq + 1) * QW], lhsT=ckvn[:, ti * 128:(ti + 1) * 128], rhs=wkvg[:, QW:2 * QW],
                                    start=True, stop=True, r=[kck], w=[kb])
                    P_dve.tensor_copy(out=vaugM[:, tb * per_bank:(tb + 1) * per_bank, :, 0:64],
                                      in_=bk.rearrange("p (t j d) -> p t j d", t=per_bank, j=NH), r=[kb, kvm], w=[kvm + "_%d" % tb])
                for j in range(NH):
                    sch.dma(mla_v[j, c0:c0 + CH, :].rearrange("(t p) c -> p t c", p=128), vaugM[:, :, j, :],
                            r=[kvm + "_%d" % tb for tb in range(4 // per_bank)], w=[])
                kb1, bk1 = fm_tile(KR1, 16)
                kb2, bk2 = fm_tile(KR2, 16)
                rope(bk1[0:16, :], kb1, bk2[0:16, :], kb2, 16,
                     [(mla_kT[j, 64 + 16 * w_:80 + 16 * w_, c0:c0 + CH], 0, w_) for j in range(NH) for w_ in range(2)])

            sch.barrier()
            ar.release(mW)
            if not body:
                continue

            mF = ar.mark()
            SEG = 2048
            flr = Ring("flseg", [ar.alloc([SEG], F32, parts=NH) for _ in range(2)])
            Pr = Ring("P", [ar.alloc([SEG], F32, parts=NH) for _ in range(2)])
            r1r = Ring("r1", [ar.alloc([SEG], F32, parts=NH) for _ in range(2)])
            pcr = Ring("pc", [ar.alloc([3, SEG], BF16, parts=NH) for _ in range(2)])
            ncr = Ring("nc", [ar.alloc([3, SEG], BF16, parts=NH) for _ in range(2)])
            zer = ar.alloc([SEG], F32, parts=NH)
            onesr = ar.alloc([3, SEG], BF16, parts=NH)
            carry = ar.alloc([2], F32, parts=NH)
            P_pool.memset(zer, 0.0, w=["zer"])
            P_pool.memset(onesr, 1.0, w=["onesr"])
            P_pool.memset(carry, 0.0, w=["carry"])
            for sg in range(S // SEG):
                s0 = sg * SEG
                kfl, fl = flr.next()
                kP, P = Pr.next()
                kr1, r1 = r1r.next()
                kpc, pc = pcr.next()
                knc, ncp = ncr.next()
                sch.dma(fl, flogT[:, s0:s0 + SEG], w=[kfl])
                P_act.activation(out=fl, in_=fl, func=AF.Exp, bias=nbf[:, 0:1], scale=-1.0, r=[kfl, "nbf"], w=[kfl])
                P_act.activation(out=fl, in_=fl, func=AF.Ln, bias=1.0, scale=1.0, r=[kfl], w=[kfl])
                P_dve.tensor_tensor_scan(out=P, data0=fl, data1=zer, initial=carry[:, 0:1], op0=ALU.add, op1=ALU.add,
                                         r=[kfl, "zer", "carry"], w=[kP])
                P_dve.tensor_copy(out=carry[:, 0:1], in_=P[:, SEG - 1:SEG], r=[kP], w=["carry"])
                P_dve.tensor_copy(out=pc[:, 0, :], in_=P, r=[kP], w=[kpc + "0"])
                P_dve.tensor_tensor(out=r1, in0=P, in1=pc[:, 0, :], op=ALU.subtract, r=[kP, kpc + "0"], w=[kr1])
                P_dve.tensor_copy(out=pc[:, 1, :], in_=r1, r=[kr1], w=[kpc + "1"])
                P_dve.tensor_tensor(out=r1, in0=r1, in1=pc[:, 1, :], op=ALU.subtract, r=[kr1, kpc + "1"], w=[kr1])
                P_dve.tensor_copy(out=pc[:, 2, :], in_=r1, r=[kr1], w=[kpc + "2"])
                P_dve.tensor_scalar(out=ncp, in0=pc, scalar1=-1.0, scalar2=None, op0=ALU.mult,
                                    r=[kpc + "0", kpc + "1", kpc + "2"], w=[knc])
                sch.dma(fox_qT[:, 64:67, s0:s0 + SEG], ncp, r=[knc], w=[])
                sch.dma(fox_qT[:, 67:70, s0:s0 + SEG], onesr, r=["onesr"], w=[])
                sch.dma(fox_kT[:, 64:67, s0:s0 + SEG], onesr, r=["onesr"], w=[])
                sch.dma(fox_kT[:, 67:70, s0:s0 + SEG], pc, r=[kpc + "0", kpc + "1", kpc + "2"], w=[])
            sch.barrier()
            ar.release(mF)

            qTr = Ring("qT", [ar.alloc([S], BF16, parts=96) for _ in range(2)])
            kTr = Ring("kT", [ar.alloc([S], BF16, parts=96) for _ in range(2)])
            vr = Ring("v", [ar.alloc([64, 65], BF16) for _ in range(2)])
            PT2r = Ring("PT2", [ar.alloc([2 * CH], BF16) for _ in range(3)])
            pair_i = [0]
            pair_slots = [(["bank0", "bank1"], allb[:, 0:1024]), (["bank2", "bank3"], allb[:, 1024:2048])]
            dbkr = Ring("bank", [banks[4][:], banks[5][:]])
            dbkr.aps = [banks[4][:], banks[5][:]]
            Er = Ring("E", [ar.alloc([CH], F32) for _ in range(2)])
            Lr = Ring("Lp", [ar.alloc([CH], BF16) for _ in range(3)])
            A2r = Ring("A2", [ar.alloc([2 * CH], F32) for _ in range(2)])
            W2r = Ring("W2", [ar.alloc([2 * CH], BF16) for _ in range(3)])
            Cpr = Ring("Cp", [ar.alloc([CH], F32) for _ in range(2)])
            gater = Ring("gate", [ar.alloc([CH], F32, parts=64) for _ in range(3)])
            Osbr = Ring("Osb", [ar.alloc([CH], F32, parts=65) for _ in range(2)])
            rDr = Ring("rD", [ar.alloc([CH], F32, parts=64) for _ in range(2)])
            Gstr = Ring("Gst", [ar.alloc([CH], BF16, parts=64) for _ in range(3)])
            short = Ring("bank", [banks[i][:] for i in range(5 if (junk_fox or junk_sb) else 6)])
            oaccr = Ring("oacc", [banks[6][:], banks[7][:]])
            junkb = banks[5][:]
            jsrc = cbf.rearrange("p a b -> p (a b)")

            def junk(ncols):
                if ncols:
                    P_pe.matmul(junkb[:, 0:ncols], lhsT=ones_bf, rhs=jsrc[:, 0:ncols], start=True, stop=True, skip_group_check=True, r=[], w=[])

            heads = [(kind, j) for kind in ("fox", "sb", "mla", "mem") for j in range(NH)]
            if heads_lim is not None:
                heads = [heads[i] for i in heads_lim]
            gidx = {"fox": 0, "sb": 1, "mla": 2, "mem": 3}
            srcs = {"fox": (fox_qT, fox_kT, fox_v, 70, 65), "sb": (sb_qT, sb_kT, sb_v, 64, 64),
                    "mla": (mla_qT, mla_kT, mla_v, 96, 65), "mem": (mem_qT, mem_kT, mem_v, 64, 65)}

            def load_head(kind, j):
                qd, kd, vd, KQ, DV = srcs[kind]
                kq, qT = qTr.next()
                kk, kT = kTr.next()
                kv, v = vr.next()
                sk_ = MEM if kind == "mem" else S
                sch.dma(qT[0:KQ, :], qd[j, :, :], w=[kq])
                sch.dma(kT[0:KQ, 0:sk_], kd[j, :, :], w=[kk])
                sch.dma(v[:, 0:sk_ // 128, 0:DV], vd[j, :, :].rearrange("(b p) c -> p b c", p=128), w=[kv])
                return (kq, qT, kk, kT, kv, v)

            loaded = load_head(*heads[0]) if heads else None
            for hi, (kind, j) in enumerate(heads):
                kq, qT, kk, kT, kv, v = loaded
                if hi + 1 < len(heads):
                    loaded = load_head(*heads[hi + 1])
                _, _, _, KQ, DV = srcs[kind]
                row0 = gidx[kind] * QW + j * 64
                scale = MLA_SCALE if kind == "mla" else 1.0
                for c in range(nchunk):
                    c0 = c * CH
                    kg, gt = gater.next()
                    sch.dma(gt, gateT[row0:row0 + 64, c0:c0 + CH], w=[kg])
                    ko, oacc = oaccr.next()
                    if kind == "mem":
                        units = [(0, 0, False), (1, 0, False)]
                    elif kind == "sb":
                        units = [(kb_, max(0, kb_ - 4 * c), kb_ >= 4 * c) for kb_ in range(4 * c + 3, -1, -1)]
                    else:
                        units = [(kb_, max(0, kb_ - 4 * c), kb_ >= 4 * c) for kb_ in range(0, 4 * c + 4)]
                    nu = len(units)

                    def s_mm(u, addmask):
                        kb_, jj, diag = units[u]
                        kbk, bk = short.next()
                        lo = jj * 128
                        P_pe.matmul(bk[:, lo:CH], lhsT=kT[0:KQ, kb_ * 128:(kb_ + 1) * 128], rhs=qT[0:KQ, c0 + lo:c0 + CH],
                                    start=True, stop=not (diag and addmask), skip_group_check=True, r=[kk, kq], w=[kbk])
                        return kbk, bk, lo

                    if kind != "sb":
                        groups = []
                        i_ = 0
                        while i_ < nu:
                            if (not units[i_][2]) and i_ + 1 < nu and (not units[i_ + 1][2]):
                                groups.append([i_, i_ + 1])
                                i_ += 2
                            else:
                                groups.append([i_])
                                i_ += 1
                        ng = len(groups)

                        def s_grp(g):
                            keys, pb = pair_slots[pair_i[0] % 2]
                            pair_i[0] += 1
                            for idx, u in enumerate(groups[g]):
                                kb_, jj, diag = units[u]
                                lo = jj * 128
                                P_pe.matmul(pb[:, idx * 512 + lo:(idx + 1) * 512], lhsT=kT[0:KQ, kb_ * 128:(kb_ + 1) * 128],
                                            rhs=qT[0:KQ, c0 + lo:c0 + CH], start=True, stop=not diag, skip_group_check=True,
                                            r=[kk, kq], w=[keys[idx]])
                                if diag:
                                    P_pe.matmul(pb[:, idx * 512 + lo:idx * 512 + lo + 128], lhsT=ident_bf, rhs=mneg_incl, start=False, stop=True,
                                                skip_group_check=True, r=[], w=[keys[idx]])
                            return keys, pb

                        gg = {}
                        for g in range(min(2, ng)):
                            gg[g] = s_grp(g)
                        for g in range(ng):
                            keys, pb = gg.pop(g)
                            us = groups[g]
                            kp, PT2 = PT2r.next()
                            if len(us) == 2:
                                P_act.activation(out=PT2[:, 0:2 * CH], in_=pb[:, 0:2 * CH], func=AF.Exp, scale=scale, r=keys, w=[kp])
                            else:
                                lo = units[us[0]][1] * 128
                                P_act.activation(out=PT2[:, lo:CH], in_=pb[:, lo:CH], func=AF.Exp, scale=scale, r=[keys[0]], w=[kp])
                            if g + 2 < ng:
                                gg[g + 2] = s_grp(g + 2)
                            for idx, u in enumerate(us):
                                kb_, jj, diag = units[u]
                                lo = jj * 128
                                P_pe.matmul(oacc[0:DV, lo:CH], lhsT=v[:, kb_, 0:DV], rhs=PT2[:, idx * 512 + lo:(idx + 1) * 512], start=(u == 0),
                                            stop=(u == nu - 1), skip_group_check=True, r=[kp, kv], w=[ko])
                        kos, Osb = Osbr.next()
                        P_dve.tensor_copy(out=Osb[0:65, :], in_=oacc[0:65, :], r=[ko], w=[kos])
                        kdb, dbk = dbkr.next()
                        kdb = "bank%d" % (4 + (dbkr.i - 1) % 2)
                        P_pe.matmul(dbk[0:64, :], lhsT=sel[0:65, :], rhs=Osb[0:65, :], start=True, stop=True, r=[kos], w=[kdb])
                        krd, rD = rDr.next()
                        P_dve.reciprocal(out=rD, in_=dbk[0:64, :], r=[kdb], w=[krd])
                        P_dve.tensor_tensor(out=rD, in0=rD, in1=Osb[0:64, :], op=ALU.mult, r=[krd, kos], w=[krd])
                        kgs, Gst = Gstr.next()
                        P_pool.tensor_tensor(out=Gst, in0=rD, in1=gt, op=ALU.mult, r=[krd, kg], w=[kgs])
                        sch.dma(gt_d[row0:row0 + 64, c0:c0 + CH], Gst, r=[kgs], w=[])
                    else:
                        kcp, Cp = Cpr.next()
                        P_pool.memset(Cp, 0.0, w=[kcp])
                        zz = {}
                        ll = {}

                        def act12(u):
                            kb_, jj, diag = units[u]
                            kbk, bk, lo = zz.pop(u)
                            ke, E = Er.next()
                            kl, Lp = Lr.next()
                            P_act.activation(out=E[:, lo:CH], in_=bk[:, lo:CH], func=AF.Exp, r=[kbk], w=[ke, kbk + "E"])
                            P_act.activation(out=Lp[:, lo:CH], in_=E[:, lo:CH], func=AF.Ln, bias=1.0, scale=1.0, r=[ke], w=[kl])
                            if diag:
                                P_dve.tensor_tensor(out=Lp[:, lo:lo + 128], in0=Lp[:, lo:lo + 128], in1=m01_strict, op=ALU.mult, r=[kl], w=[kl])
                            ll[u] = (kl, Lp, kbk, bk)

                        for u in range(min(2, nu)):
                            zz[u] = s_mm(u, False)
                        act12(0)
                        pend = None
                        opened = None
                        for u in range(nu):
                            kb_, jj, diag = units[u]
                            lo = jj * 128
                            kl, Lp, kab, ab = ll.pop(u)
                            P_pe.matmul(ab[:, lo:CH], lhsT=negU, rhs=Lp[:, lo:CH], start=False, stop=not diag, skip_group_check=True, r=[kl, kab + "E"], w=[kab])
                            if diag:
                                P_pe.matmul(ab[:, lo:lo + 128], lhsT=ident_bf, rhs=mneg_strict, start=False, stop=True, skip_group_check=True,
                                            r=[], w=[kab])
                            kcs_, csb = short.next()
                            P_pe.matmul(csb[:, lo:CH], lhsT=ones_bf, rhs=Lp[:, lo:CH], start=True, stop=True, r=[kl], w=[kcs_])
                            for jn in junk_sb:
                                junk(jn)
                            if u + 2 < nu:
                                zz[u + 2] = s_mm(u + 2, False)
                            if u + 1 < nu:
                                act12(u + 1)
                            second = opened is not None
                            first = (not second) and (not diag) and u + 1 < nu and (not units[u + 1][2])
                            if second:
                                ka, A2, kw, W2, u0, kb0 = opened
                                off = CH
                            else:
                                ka, A2 = A2r.next()
                                kw, W2 = W2r.next()
                                off = 0
                            P_dve.tensor_tensor(out=A2[:, off + lo:off + CH], in0=ab[:, lo:CH], in1=Cp[:, lo:CH], op=ALU.subtract,
                                                r=[kab, kcp], w=[ka + "_%d" % (off // CH)])
                            P_dve.tensor_tensor(out=Cp[:, lo:CH], in0=csb[:, lo:CH], in1=Cp[:, lo:CH], op=ALU.add, r=[kcs_, kcp], w=[kcp])
                            if pend is not None:
                                pend()
                                pend = None
                            if first:
                                opened = (ka, A2, kw, W2, u, kb_)
                                continue
                            if second:
                                opened = None
                                P_act.activation(out=W2[:, 0:2 * CH], in_=A2[:, 0:2 * CH], func=AF.Exp, r=[ka + "_0", ka + "_1"], w=[kw])
                                pvs = [(u0, kb0, 0, 0), (u, kb_, CH, lo)]
                            else:
                                P_act.activation(out=W2[:, lo:CH], in_=A2[:, lo:CH], func=AF.Exp, r=[ka + "_0"], w=[kw])
                                pvs = [(u, kb_, 0, lo)]

                            def pv(W2=W2, kw=kw, pvs=pvs):
                                for (uu, kbb, off_, lo_) in pvs:
                                    P_pe.matmul(oacc[0:64, lo_:CH], lhsT=v[:, kbb, 0:64], rhs=W2[:, off_ + lo_:off_ + CH], start=(uu == 0),
                                                stop=(uu == nu - 1), skip_group_check=True, r=[kw, kv], w=[ko])
                            pend = pv
                        if pend is not None:
                            pend()
                        kgs, Gst = Gstr.next()
                        P_dve.tensor_tensor(out=Gst, in0=oacc[0:64, :], in1=gt, op=ALU.mult, r=[ko, kg], w=[kgs])
                        sch.dma(gt_d[row0:row0 + 64, c0:c0 + CH], Gst, r=[kgs], w=[])
            sch.barrier()
            ar.release(mW)

        sch.barrier()
        sch.emit(nc)
    if _os.environ.get('KPEAK'):
        print('arena peak words', ar.peak, 'of', NW)
    return nc, sch.nins


NHK = 4


def _consts():
    i = np.arange(128)
    ident = np.eye(128, dtype=np.float32)
    ones = np.ones((128, 128), np.float32)
    negU = -(i[:, None] >= i[None, :]).astype(np.float32)
    m_incl = np.where(i[:, None] <= i[None, :], 0.0, NEG).astype(np.float32)
    m_strict = np.where(i[:, None] < i[None, :], 0.0, NEG).astype(np.float32)
    m01 = (i[:, None] < i[None, :]).astype(np.float32)
    c_bf = np.concatenate([ident, ones, negU, m_incl, m_strict, m01], axis=1).astype(ml_dtypes.bfloat16)
    sel = np.zeros((65, 64), np.float32)
    sel[64, :] = 1.0
    half = 16
    inv_freq = (np.float32(10000.0) ** (-np.arange(half, dtype=np.float32) / np.float32(half))).astype(np.float32)
    ang = (np.arange(S, dtype=np.float32)[:, None] * inv_freq[None, :]).astype(np.float32)
    cos = np.cos(ang).astype(np.float32).T
    sin = np.sin(ang).astype(np.float32).T
    c_cos = np.ascontiguousarray(np.concatenate([cos] * NHK, axis=0))
    c_sin = np.ascontiguousarray(np.concatenate([sin] * NHK, axis=0))
    return dict(c_bf=np.ascontiguousarray(c_bf), c_id=ident, c_sel=sel, c_cos=c_cos, c_sin=c_sin)


def _rep(v):
    return np.ascontiguousarray(np.broadcast_to(np.asarray(v, np.float32)[None, :], (128, v.shape[0])))


def _w_in_cols():
    o_fq, o_fk, o_fv, o_fl = 0, 256, 512, 768
    o_sq, o_sk, o_sv = 772, 1028, 1284
    o_cq, o_ckv, o_kr = 1540, 1796, 1924
    o_mq, o_gate = 1956, 2212
    r = lambda a, n: list(range(a, a + n))
    cols = r(o_fq, 256) + r(o_fk, 256) + r(o_sq, 256) + r(o_sk, 256) + r(o_mq, 256) + r(o_gate, 1024)
    cols += r(o_cq, 256) + r(o_ckv, 128) + r(o_kr, 16) + r(o_kr + 16, 16) + r(o_fl, 4) + r(o_fv, 256) + r(o_sv, 256)
    assert len(cols) == 3236
    return np.array(cols)


def make_in_maps(x, mem, ln_in_g, ln_in_b, mem_ln_g, mem_ln_b, w_in, b_forget, mla_q_norm_g, w_mla_q_up,
                 mla_kv_norm_g, w_mla_kv_up, w_mem_kv, w_out, ln_g, ln_b):
    f = lambda a: np.ascontiguousarray(np.asarray(a, dtype=np.float32))
    cst = _consts()
    wcols = _w_in_cols()
    hs = range(4)
    qcols = [h * 96 + d for h in hs for d in range(64)] + [h * 96 + 64 + d for h in hs for d in range(16)] + \
            [h * 96 + 80 + d for h in hs for d in range(16)]
    kvcols = [h * 128 + d for h in hs for d in range(64)] + [h * 128 + 64 + d for h in hs for d in range(64)]
    shared = dict(mlng=_rep(f(mem_ln_g)), mlnb=_rep(f(mem_ln_b)), **cst)
    lgs = [f(ln_in_g)] + [f(ln_g)[l] for l in range(DEPTH)]
    lbs = [f(ln_in_b)] + [f(ln_b)[l] for l in range(DEPTH)]
    for i in range(DEPTH + 1):
        shared["lng%d" % i] = _rep(lgs[i])
        shared["lnb%d" % i] = _rep(lbs[i])
    for l in range(DEPTH):
        shared["w_in%d" % l] = f(f(w_in)[l][:, wcols])
        shared["b_f%d" % l] = f(f(b_forget)[l].reshape(4, 1))
        shared["gq%d" % l] = f(f(mla_q_norm_g)[l].reshape(2, 128).T)
        shared["gkv%d" % l] = f(f(mla_kv_norm_g)[l].reshape(128, 1))
        shared["w_qup%d" % l] = f(f(w_mla_q_up)[l][:, qcols])
        shared["w_kvup%d" % l] = f(f(w_mla_kv_up)[l][:, kvcols])
        shared["w_mem%d" % l] = f(f(w_mem_kv)[l])
        shared["w_out%d" % l] = f(f(w_out)[l])
    maps = []
    for b in range(4):
        m = dict(shared)
        m["x"] = f(x[b])
        m["mem"] = f(mem[b])
        maps.append(m)
    return maps


_PROG = {}


def kernel(x, mem, ln_in_g, ln_in_b, mem_ln_g, mem_ln_b, w_in, b_forget, mla_q_norm_g, w_mla_q_up,
           mla_kv_norm_g, w_mla_kv_up, w_mem_kv, w_out, ln_g, ln_b):
    if "fused" not in _PROG:
        _PROG["fused"] = build_fused()[0]
    nc = _PROG["fused"]
    maps = make_in_maps(x, mem, ln_in_g, ln_in_b, mem_ln_g, mem_ln_b, w_in, b_forget, mla_q_norm_g, w_mla_q_up,
                        mla_kv_norm_g, w_mla_kv_up, w_mem_kv, w_out, ln_g, ln_b)
    res = run_bass_kernel_spmd(nc, maps, core_ids=list(range(4))).results
    return np.stack([np.asarray(res[b]["out"]) for b in range(4)], axis=0).astype(np.float32)
```

```python
import math
import numpy as np
import ml_dtypes
import concourse.bass as bass
import concourse.mybir as mybir
from concourse.bass_utils import run_bass_kernel_spmd

F32 = mybir.dt.float32
BF16 = mybir.dt.bfloat16
AF = mybir.ActivationFunctionType
ALU = mybir.AluOpType

S = 8192
D = 1024
MEM = 256
NCH = 16
CH = 512
HPG = 2
DEPTH = 2
ALPHA = (2 * DEPTH) ** 0.25
LN_EPS = 1e-5
RMS_EPS = 1e-6
NEG = -30000.0
MLA_SCALE = 96 ** -0.5

FQ, FK, SQ, SK, MQ = 0, 128, 256, 384, 512
GATE = 640
CQ = 1152
CKV = 1408
KR1 = 1536
KR2 = 1552
FL = 1568
FV = 1570
SV = 1698
NCOL = 1826

ENGS = ["pe", "act", "dve", "pool", "sp"]
NDSEM = 24
import os as _os
SUB = int(_os.environ.get('SUB', '9'))


class Sched:
    def __init__(self):
        self.q = {e: [] for e in ENGS}
        self.cnt = {e: 0 for e in ENGS}
        self.seen = {e: {} for e in ENGS}
        self.rw = {}
        self.rr = {}
        self.dval = [0] * NDSEM
        self.drr = 0
        self.nins = 0

    def _need(self, eng, reads, writes):
        need = {}

        def add(d, skip_pe):
            for sk, v in d.items():
                if skip_pe and sk == "pe" and eng == "pe":
                    continue
                if need.get(sk, 0) < v:
                    need[sk] = v

        for k in reads:
            add(self.rw.get(k, {}), False)
        for k in writes:
            add(self.rw.get(k, {}), True)
            add(self.rr.get(k, {}), False)
        out = []
        for sk, v in need.items():
            if self.seen[eng].get(sk, 0) >= v:
                continue
            self.seen[eng][sk] = v
            out.append((sk, v))
        return out

    def _record(self, tok, reads, writes):
        sk, v = tok
        for k in reads:
            self.rr.setdefault(k, {})[sk] = v
        for k in writes:
            self.rw[k] = {sk: v}
            self.rr[k] = {}

    def op(self, eng, meth, args, kwargs, r=(), w=()):
        waits = self._need(eng, r, w)
        self.cnt[eng] += 1
        tok = (eng, self.cnt[eng])
        self.q[eng].append((waits, (meth, args, kwargs), tok))
        self._record(tok, r, w)
        self.nins += 1 + max(0, len(waits) - 1)

    def proxy(self, eng):
        sch = self

        class _P:
            def __getattr__(self, meth):
                def f(*args, r=(), w=(), **kwargs):
                    sch.op(eng, meth, args, kwargs, r, w)
                return f
        return _P()

    def dma(self, out, in_, r=(), w=()):
        eng = "sp"
        waits = self._need(eng, r, w)
        i = self.drr
        self.drr = (self.drr + 1) % NDSEM
        sk = ("d", i)
        if self.dval[i] > 0 and self.seen[eng].get(sk, 0) < self.dval[i]:
            self.seen[eng][sk] = self.dval[i]
            waits.append((sk, self.dval[i]))
        self.dval[i] += 16
        tok = (sk, self.dval[i])
        self.q[eng].append((waits, (out, in_), tok))
        self._record(tok, r, w)
        self.nins += 1 + max(0, len(waits) - 1)

    def barrier(self):
        for e in ENGS:
            waits = []
            for o in ENGS:
                if o == e or o == "sp":
                    continue
                if self.cnt[o] > self.seen[e].get(o, 0):
                    self.seen[e][o] = self.cnt[o]
                    waits.append((o, self.cnt[o]))
            for i in range(NDSEM):
                sk = ("d", i)
                if self.dval[i] > self.seen[e].get(sk, 0):
                    self.seen[e][sk] = self.dval[i]
                    waits.append((sk, self.dval[i]))
            if waits:
                self.q[e].append((waits, None, None))
                self.nins += len(waits)
        self.rw = {}
        self.rr = {}

    def emit(self, nc):
        import contextlib

        with contextlib.ExitStack() as st:
            sems = {}
            for e in ["pe", "act", "dve", "pool"]:
                sems[e] = st.enter_context(nc.semaphore("s_" + e))
            for i in range(NDSEM):
                sems[("d", i)] = st.enter_context(nc.semaphore("s_d%d" % i))
            block = st.enter_context(nc.Block())

            def run(eng, e):
                for waits, fn, tok in self.q[eng]:
                    if fn is None:
                        for sk, v in waits:
                            e.wait_ge(sems[sk], v)
                        continue
                    if eng == "pe":
                        for sk, v in waits:
                            e.wait_ge(sems[sk], v)
                        waits = []
                    for sk, v in waits[:-1]:
                        e.wait_ge(sems[sk], v)
                    if eng == "sp":
                        ins = e.dma_start(out=fn[0], in_=fn[1])
                    else:
                        ins = getattr(e, fn[0])(*fn[1], **fn[2])
                    if waits:
                        ins._wait_ge(sems[waits[-1][0]], waits[-1][1])
                    if eng == "sp":
                        ins.then_inc(sems[tok[0]], 16)
                    else:
                        ins.then_inc(sems[eng], 1)

            @block.tensor
            def _(e):
                run("pe", e)

            @block.scalar
            def _(e):
                run("act", e)

            @block.vector
            def _(e):
                run("dve", e)

            @block.gpsimd
            def _(e):
                run("pool", e)

            @block.sync
            def _(e):
                run("sp", e)


class Arena:
    def __init__(self, t, nwords):
        self.t = t
        self.n = nwords
        self.off = 0

    def mark(self):
        return self.off

    def release(self, m):
        self.off = m

    def alloc(self, free_shape, dtype, parts=128):
        n = int(np.prod(free_shape))
        words = n if dtype == F32 else (n + 1) // 2
        words = (words + 1) // 2 * 2
        assert self.off + words <= self.n, ("arena overflow", self.off, words, self.n)
        ap = self.t[0:parts, self.off:self.off + words]
        self.off += words
        self.peak = max(getattr(self, "peak", 0), self.off)
        if dtype != F32:
            ap = ap.bitcast(dtype)
        ap = ap[:, 0:n]
        if len(free_shape) == 2:
            ap = ap.rearrange("p (a b) -> p a b", a=free_shape[0])
        elif len(free_shape) == 3:
            ap = ap.rearrange("p (a b c) -> p a b c", a=free_shape[0], b=free_shape[1])
        return ap


class Ring:
    def __init__(self, name, aps):
        self.name = name
        self.aps = aps
        self.i = 0

    def next(self):
        k = self.i % len(self.aps)
        self.i += 1
        return "%s%d" % (self.name, k), self.aps[k]


def build_fused(NH=4, depth=DEPTH, nchunk_lim=None, heads_lim=None, stages_lim=None, junk_fox=0, junk_sb=()):
    nc = bass.Bass("TRN2", target_bir_lowering=False)
    QW = NH * 64
    FQ, FK, SQ, SK, MQ = 0, QW, 2 * QW, 3 * QW, 4 * QW
    GATE = 5 * QW
    CQ = 9 * QW
    CKV = CQ + 256
    KR1 = CKV + 128
    KR2 = KR1 + 16
    FL = KR2 + 16
    FV = FL + NH
    SV = FV + QW
    NCOL = SV + QW
    MIX = 4 * QW
    RM = NH * 16
    dbg = bool(_os.environ.get("KDBG"))

    def din(name, shape, dt=F32):
        return nc.dram_tensor(name, list(shape), dt, kind="ExternalInput").ap()

    def dscr(name, shape, dt=BF16):
        return nc.dram_tensor(name, list(shape), dt, kind=("ExternalOutput" if dbg else "Internal")).ap()

    x_in = din("x", [S, D])
    mem_in = din("mem", [MEM, D])
    lng = [din("lng%d" % i, [128, D]) for i in range(depth + 1)]
    lnb = [din("lnb%d" % i, [128, D]) for i in range(depth + 1)]
    mlng = din("mlng", [128, D])
    mlnb = din("mlnb", [128, D])
    c_bf = din("c_bf", [128, 6 * 128], BF16)
    c_id = din("c_id", [128, 128])
    c_sel = din("c_sel", [65, 64])
    c_cos = din("c_cos", [RM, S])
    c_sin = din("c_sin", [RM, S])
    w_in = [din("w_in%d" % l, [D, NCOL]) for l in range(depth)]
    b_f = [din("b_f%d" % l, [NH, 1]) for l in range(depth)]
    gq = [din("gq%d" % l, [128, 2]) for l in range(depth)]
    gkv = [din("gkv%d" % l, [128, 1]) for l in range(depth)]
    w_qup = [din("w_qup%d" % l, [256, NH * 96]) for l in range(depth)]
    w_kvup = [din("w_kvup%d" % l, [128, 2 * QW]) for l in range(depth)]
    w_mem = [din("w_mem%d" % l, [D, 2 * QW]) for l in range(depth)]
    w_out = [din("w_out%d" % l, [MIX, D]) for l in range(depth)]
    out_d = nc.dram_tensor("out", [S, D], F32, kind="ExternalOutput").ap()

    h_d = dscr("h_scr", [S, D], F32)
    gt_d = dscr("gt_scr", [MIX, S])
    fox_qT = dscr("fox_qT", [NH, 70, S])
    fox_kT = dscr("fox_kT", [NH, 70, S])
    fox_v = dscr("fox_v", [NH, S, 65])
    sb_qT = dscr("sb_qT", [NH, 64, S])
    sb_kT = dscr("sb_kT", [NH, 64, S])
    sb_v = dscr("sb_v", [NH, S, 64])
    mla_qT = dscr("mla_qT", [NH, 96, S])
    mla_kT = dscr("mla_kT", [NH, 96, S])
    mla_v = dscr("mla_v", [NH, S, 65])
    mem_qT = dscr("mem_qT", [NH, 64, S])
    mem_kT = dscr("mem_kT", [NH, 64, MEM])
    mem_v = dscr("mem_v", [NH, MEM, 65])
    gateT = dscr("gateT", [MIX, S], F32)
    flogT = dscr("flogT", [NH, S], F32)

    sch = Sched()
    P_pe, P_act, P_dve, P_pool = sch.proxy("pe"), sch.proxy("act"), sch.proxy("dve"), sch.proxy("pool")
    NW = 52400
    import contextlib

    with contextlib.ExitStack() as st:
        arena_t = st.enter_context(nc.sbuf_tensor("arena", [128, NW], F32))
        allb = st.enter_context(nc.psum_tensor("allb", [128, 4096], F32))
        banks = [allb[:, i * 512:(i + 1) * 512] for i in range(8)]
        ar = Arena(arena_t, NW)

        cbf = ar.alloc([6, 128], BF16)
        ident_bf, ones_bf, negU, mneg_incl, mneg_strict, m01_strict = [cbf[:, i, :] for i in range(6)]
        ident_f = ar.alloc([128], F32)
        sel = ar.alloc([64], F32, parts=65)
        Gbc = ar.alloc([D], F32)
        Bbc = ar.alloc([D], F32)
        mGbc = ar.alloc([D], F32)
        mBbc = ar.alloc([D], F32)
        epsln = ar.alloc([2], F32)
        nbf = ar.alloc([2], F32, parts=NH)
        small_ring = Ring("small", [ar.alloc([16], F32) for _ in range(4)])
        sch.dma(cbf, c_bf.rearrange("p (a b) -> p a b", a=6), w=["cbf"])
        sch.dma(ident_f, c_id, w=["ident_f"])
        sch.dma(sel, c_sel, w=["sel"])
        sch.dma(mGbc, mlng, w=["mGbc"])
        sch.dma(mBbc, mlnb, w=["mBbc"])
        P_pool.memset(epsln[:, 0:1], LN_EPS, w=["epsln"])
        P_pool.memset(epsln[:, 1:2], RMS_EPS, w=["epsln2"])
        sch.barrier()

        bank_ring = Ring("bank", [b[:] for b in banks])

        def layer_norm(R2, kR2, H, kH, G, B, small):
            kst, stt = small.next()
            st6 = stt[:, 0:12].rearrange("p (a b) -> p a b", a=2)
            mv = stt[:, 12:14]
            tmp = stt[:, 14:16]
            P_dve.bn_stats(out=st6[:, 0, :], in_=R2[:, 0:512], r=[kR2], w=[kst])
            P_dve.bn_stats(out=st6[:, 1, :], in_=R2[:, 512:1024], r=[kR2], w=[kst + "b"])
            P_dve.bn_aggr(out=mv, in_=stt[:, 0:12], r=[kst, kst + "b"], w=[kst + "mv"])
            P_act.activation(out=tmp[:, 0:1], in_=mv[:, 1:2], func=AF.Ln, bias=epsln[:, 0:1], scale=1.0,
                             r=[kst + "mv"], w=[kst + "t0"])
            P_act.activation(out=tmp[:, 1:2], in_=tmp[:, 0:1], func=AF.Exp, scale=-0.5, r=[kst + "t0"], w=[kst + "t1"])
            P_dve.tensor_scalar(out=H, in0=R2, scalar1=mv[:, 0:1], scalar2=tmp[:, 1:2], op0=ALU.subtract, op1=ALU.mult,
                                r=[kR2, kst + "mv", kst + "t1"], w=[kH])
            P_pool.tensor_tensor(out=H, in0=H, in1=G, op=ALU.mult, r=[kH], w=[kH])
            P_dve.tensor_tensor(out=H, in0=H, in1=B, op=ALU.add, r=[kH], w=[kH])

        def transposes(H, kH, hT, khT, col0):
            for half in range(2):
                kb, bk = bank_ring.next()
                for cc in range(4):
                    c = half * 4 + cc
                    P_pe.transpose(out=bk[:, cc * 128:(cc + 1) * 128], in_=H[:, c * 128:(c + 1) * 128], identity=ident_f,
                                   r=[kH], w=[kb])
                src = bk.rearrange("p (a b) -> p a b", a=4)
                dst = hT[:, half * 4:(half + 1) * 4, col0:col0 + 128]
                if half == 0:
                    P_act.activation(out=dst, in_=src, func=AF.Copy, r=[kb], w=[khT + "h%d_%d" % (half, col0)])
                else:
                    P_dve.tensor_copy(out=dst, in_=src, r=[kb], w=[khT + "h%d_%d" % (half, col0)])

        def hT_keys(khT, ncols):
            return [khT + "h%d_%d" % (half, c0) for half in range(2) for c0 in range(0, ncols, 128)]

        cast_i = [0]

        def cast(out, in_, r, w):
            k = cast_i[0] % 3
            cast_i[0] += 1
            if k == 0:
                P_dve.tensor_copy(out=out, in_=in_, r=r, w=w)
            elif k == 1:
                P_pool.tensor_copy(out=out, in_=in_, r=r, w=w)
            else:
                P_act.activation(out=out, in_=in_, func=AF.Copy, r=r, w=w)

        nstage = depth + 1 if stages_lim is None else stages_lim
        for stage in range(nstage):
            body = stage < depth
            prev = stage > 0
            l = stage
            mW = ar.mark()
            sch.dma(Gbc, lng[stage], w=["Gbc"])
            sch.dma(Bbc, lnb[stage], w=["Bbc"])
            if prev:
                wout = ar.alloc([8, D], BF16)
            if body:
                wbf = ar.alloc([8, NCOL], BF16)
                wmem = ar.alloc([8, 2 * QW], BF16)
                wqg = ar.alloc([2, NH * 96], BF16)
                wkvg = ar.alloc([2 * QW], BF16)
            m0 = ar.mark()
            stg = Ring("stg", [ar.alloc([NCOL], F32) for _ in range(3)])
            if prev:
                for c in range(8):
                    ks, sg_ = stg.next()
                    sch.dma(sg_[:, 0:D], w_out[l - 1][c * 128:(c + 1) * 128, :], w=[ks])
                    cast(wout[:, c, :], sg_[:, 0:D], [ks], ["wout%d" % c])
            if body:
                for c in range(8):
                    ks, sg_ = stg.next()
                    sch.dma(sg_, w_in[l][c * 128:(c + 1) * 128, :], w=[ks])
                    cast(wbf[:, c, :], sg_, [ks], ["wbf%d" % c])
                for c in range(8):
                    ks, sg_ = stg.next()
                    sch.dma(sg_[:, 0:2 * QW], w_mem[l][c * 128:(c + 1) * 128, :], w=[ks])
                    cast(wmem[:, c, :], sg_[:, 0:2 * QW], [ks], ["wmem%d" % c])
                gq_t = ar.alloc([2], F32)
                gkv_t = ar.alloc([2], F32)
                bf_t = ar.alloc([2], F32, parts=NH)
                sch.dma(gq_t, gq[l], w=["gq_t"])
                sch.dma(gkv_t[:, 0:1], gkv[l], w=["gkv_t"])
                sch.dma(bf_t[:, 0:1], b_f[l], w=["bf_t"])
                for c in range(2):
                    ks, sg_ = stg.next()
                    sch.dma(sg_[:, 0:NH * 96], w_qup[l][c * 128:(c + 1) * 128, :], w=[ks])
                    P_dve.tensor_scalar(out=wqg[:, c, :], in0=sg_[:, 0:NH * 96], scalar1=gq_t[:, c:c + 1], scalar2=None, op0=ALU.mult,
                                        r=[ks, "gq_t"], w=["wqg%d" % c])
                ks, sg_ = stg.next()
                sch.dma(sg_[:, 0:2 * QW], w_kvup[l], w=[ks])
                P_dve.tensor_scalar(out=wkvg, in0=sg_[:, 0:2 * QW], scalar1=gkv_t[:, 0:1], scalar2=None, op0=ALU.mult,
                                    r=[ks, "gkv_t"], w=["wkvg"])
                P_dve.tensor_scalar(out=nbf[:, 0:1], in0=bf_t[:, 0:1], scalar1=-1.0, scalar2=None, op0=ALU.mult, r=["bf_t"], w=["nbf"])
            sch.barrier()
            ar.release(m0)
            mA = ar.mark()

            if body:
                Rm = Ring("Rm", [ar.alloc([D], F32) for _ in range(2)])
                Hm = Ring("Hm", [ar.alloc([D], F32) for _ in range(2)])
                mT = ar.alloc([8, MEM], BF16)
                vaugm = ar.alloc([2, NH, 65], BF16)
                kmem_r = Ring("kmem", [ar.alloc([MEM], BF16) for _ in range(2)])
                P_pool.memset(vaugm, 1.0, w=["vaugm"])
                for i in range(2):
                    kR, R = Rm.next()
                    kH, H = Hm.next()
                    sch.dma(R, mem_in[i * 128:(i + 1) * 128, :], w=[kR])
                    layer_norm(R, kR, H, kH, mGbc, mBbc, small_ring)
                    transposes(H, kH, mT, "mT", i * 128)
                for t in range(NH // 2):
                    kb, bk = bank_ring.next()
                    for c in range(8):
                        P_pe.matmul(bk[:, 0:MEM], lhsT=wmem[:, c, t * 128:(t + 1) * 128], rhs=mT[:, c, :], start=(c == 0), stop=(c == 7),
                                    r=hT_keys("mT", MEM), w=[kb])
                    kkm, kmem_sb = kmem_r.next()
                    P_act.activation(out=kmem_sb, in_=bk[:, 0:MEM], func=AF.Copy, r=[kb], w=[kkm])
                    for jj in range(2):
                        sch.dma(mem_kT[2 * t + jj, :, :], kmem_sb[jj * 64:(jj + 1) * 64, :], r=[kkm], w=[])
                for i in range(2):
                    kb, bk = bank_ring.next()
                    for c in range(8):
                        P_pe.matmul(bk[:, 0:QW], lhsT=mT[:, c, i * 128:(i + 1) * 128], rhs=wmem[:, c, QW:2 * QW], start=(c == 0), stop=(c == 7),
                                    r=hT_keys("mT", MEM), w=[kb])
                    P_dve.tensor_copy(out=vaugm[:, i, :, 0:64], in_=bk[:, 0:QW].rearrange("p (j d) -> p j d", j=NH),
                                      r=[kb, "vaugm"], w=["vaugm%d" % i])
                    for j in range(NH):
                        sch.dma(mem_v[j, i * 128:(i + 1) * 128, :], vaugm[:, i, j, :], r=["vaugm%d" % i], w=[])
                sch.barrier()
                ar.release(mA)

            Rr = Ring("R", [ar.alloc([D], F32) for _ in range(2)])
            Hr = Ring("H", [ar.alloc([D], F32) for _ in range(2)])
            if prev:
                gtfr = Ring("gtf", [ar.alloc([8, CH], BF16) for _ in range(2)])
            if body:
                hTr = Ring("hT", [ar.alloc([8, CH], BF16) for _ in range(2)])
                evb = Ring("evb", [ar.alloc([CH], BF16) for _ in range(4)])
                evf = Ring("evf", [ar.alloc([CH], F32) for _ in range(3)])
                cqn_r = Ring("cqn", [ar.alloc([2, CH], BF16) for _ in range(2)])
                sq_r = Ring("sq", [ar.alloc([2, CH], BF16) for _ in range(2)])
                ckvn_r = Ring("ckvn", [ar.alloc([CH], BF16) for _ in range(2)])
                rq_r = Ring("rq", [ar.alloc([CH], F32) for _ in range(2)])
                cs_r = Ring("cs", [ar.alloc([2, CH], F32, parts=RM) for _ in range(2)])
                rope_r = Ring("rope", [ar.alloc([CH], F32, parts=RM) for _ in range(4)])
                ropeo_r = Ring("ropeo", [ar.alloc([CH], BF16, parts=RM) for _ in range(4)])
                vaugF_r = Ring("vaugF", [ar.alloc([4, NH, 65], BF16) for _ in range(2)])
                vS_r = Ring("vS", [ar.alloc([4, NH, 64], BF16) for _ in range(2)])
                vaugM_r = Ring("vaugM", [ar.alloc([4, NH, 65], BF16) for _ in range(2)])
                fl_r = Ring("fl", [ar.alloc([CH], F32, parts=NH) for _ in range(2)])
                for k_, t_ in zip(["vaugF0", "vaugF1"], vaugF_r.aps):
                    P_pool.memset(t_, 1.0, w=[k_])
                for k_, t_ in zip(["vaugM0", "vaugM1"], vaugM_r.aps):
                    P_pool.memset(t_, 1.0, w=[k_])

            nchunk = NCH if nchunk_lim is None else min(NCH, nchunk_lim)
            src_d = x_in if stage == 0 else h_d
            dst_d = out_d if stage == depth else h_d
            cctx = {}
            rload = {}
            gload = {}
            hkeep = {}
            ntile_all = nchunk * 4

            def load_R(gi):
                if gi in rload or gi >= ntile_all:
                    return
                kR, R = Rr.next()
                sch.dma(R, src_d[gi * 128:(gi + 1) * 128, :], r=["hrow%d" % gi], w=[kR])
                rload[gi] = (kR, R)

            def load_g(ci):
                if (not prev) or ci in gload or ci >= nchunk:
                    return
                kg, gtf = gtfr.next()
                sch.dma(gtf, gt_d[:, ci * CH:(ci + 1) * CH].rearrange("(c p) s -> p c s", p=128), w=[kg])
                gload[ci] = (kg, gtf)

            def pro_a(ci, ti):
                gi = ci * 4 + ti
                if ti == 0:
                    ctx = {}
                    if body:
                        ctx["khT"], ctx["hT"] = hTr.next()
                    cctx[ci] = ctx
                    load_g(ci)
                    load_g(ci + 1)
                load_R(gi)
                load_R(gi + 1)
                kR, R = rload.pop(gi)
                kH, H = Hr.next()
                if prev:
                    kg, gtf = gload[ci]
                    for half in range(2):
                        kb, bk = bank_ring.next()
                        for c in range(8):
                            P_pe.matmul(bk, lhsT=gtf[:, c, ti * 128:(ti + 1) * 128], rhs=wout[:, c, half * 512:(half + 1) * 512],
                                        start=(c == 0), stop=(c == 7), r=[kg], w=[kb])
                        P_dve.scalar_tensor_tensor(out=R[:, half * 512:(half + 1) * 512], in0=R[:, half * 512:(half + 1) * 512],
                                                   scalar=ALPHA, in1=bk, op0=ALU.mult, op1=ALU.add, r=[kb, kR], w=[kR])
                layer_norm(R, kR, H, kH, Gbc, Bbc, small_ring)
                sch.dma(dst_d[gi * 128:(gi + 1) * 128, :], H, r=[kH], w=["hrow%d" % gi])
                hkeep[gi] = (kH, H)

            def pro_b(ci, ti):
                kH, H = hkeep.pop(ci * 4 + ti)
                if body:
                    transposes(H, kH, cctx[ci]["hT"], cctx[ci]["khT"], ti * 128)

            for ci in range(nchunk):
                c0 = ci * CH
                if ci == 0 or not body:
                    for ti in range(4):
                        pro_a(ci, ti)
                        pro_b(ci, ti)
                if not body:
                    continue
                khT, hT = cctx[ci]["khT"], cctx[ci]["hT"]

                def nxt(k, ci=ci):
                    if ci + 1 >= nchunk:
                        return
                    if k >= 1:
                        pro_b(ci + 1, k - 1)
                    if k <= 3:
                        pro_a(ci + 1, k)

                nxt(0)
                hk = hT_keys(khT, CH)

                def fm_tile(col, m, hT=hT, hk=hk):
                    kb, bk = bank_ring.next()
                    for c in range(8):
                        P_pe.matmul(bk[0:m, :], lhsT=wbf[:, c, col:col + m], rhs=hT[:, c, :], start=(c == 0), stop=(c == 7), r=hk, w=[kb])
                    return kb, bk

                for col, dst, scl in ((FQ, fox_qT, 0.125), (FK, fox_kT, 1.0), (SQ, sb_qT, 0.125), (SK, sb_kT, 1.0), (MQ, mem_qT, 0.125)):
                    for t in range(NH // 2):
                        kb, bk = fm_tile(col + t * 128, 128)
                        ke, ev = evb.next()
                        P_act.activation(out=ev, in_=bk, func=AF.Copy, scale=scl, r=[kb], w=[ke])
                        for jj in range(2):
                            sch.dma(dst[2 * t + jj, 0:64, c0:c0 + CH], ev[jj * 64:(jj + 1) * 64, :], r=[ke], w=[])
                nxt(1)
                for t in range(MIX // 128):
                    kb, bk = fm_tile(GATE + t * 128, 128)
                    ke, ev = evf.next()
                    P_act.activation(out=ev, in_=bk, func=AF.Exp, scale=-1.0, r=[kb], w=[ke])
                    P_act.activation(out=ev, in_=ev, func=AF.Ln, bias=1.0, scale=1.0, r=[ke], w=[ke])
                    P_act.activation(out=ev, in_=ev, func=AF.Exp, scale=-1.0, r=[ke], w=[ke])
                    P_dve.tensor_tensor(out=ev, in0=bk, in1=ev, op=ALU.mult, r=[kb, ke], w=[ke])
                    sch.dma(gateT[t * 128:(t + 1) * 128, c0:c0 + CH], ev, r=[ke], w=[])
                nxt(2)
                kb, bk = fm_tile(FL, NH)
                kf, fl = fl_r.next()
                P_dve.tensor_copy(out=fl, in_=bk[0:NH, :], r=[kb], w=[kf])
                sch.dma(flogT[:, c0:c0 + CH], fl, r=[kf], w=[])
                kvf, vaugF = vaugF_r.next()
                kvs, vS = vS_r.next()
                for ti in range(4):
                    kb, bk = bank_ring.next()
                    for c in range(8):
                        P_pe.matmul(bk[:, 0:2 * QW], lhsT=hT[:, c, ti * 128:(ti + 1) * 128], rhs=wbf[:, c, FV:FV + 2 * QW],
                                    start=(c == 0), stop=(c == 7), r=hk, w=[kb])
                    P_act.activation(out=vaugF[:, ti, :, 0:64], in_=bk[:, 0:QW].rearrange("p (j d) -> p j d", j=NH), func=AF.Copy,
                                     r=[kb, kvf], w=[kvf + "_%d" % ti])
                    P_act.activation(out=vS[:, ti, :, :], in_=bk[:, QW:2 * QW].rearrange("p (j d) -> p j d", j=NH), func=AF.Copy,
                                     r=[kb], w=[kvs + "_%d" % ti])
                for j in range(NH):
                    sch.dma(fox_v[j, c0:c0 + CH, :].rearrange("(t p) c -> p t c", p=128), vaugF[:, :, j, :],
                            r=[kvf + "_%d" % ti for ti in range(4)], w=[])
                    sch.dma(sb_v[j, c0:c0 + CH, :].rearrange("(t p) c -> p t c", p=128), vS[:, :, j, :],
                            r=[kvs + "_%d" % ti for ti in range(4)], w=[])
                nxt(3)
                kcs, cs = cs_r.next()
                sch.dma(cs[:, 0, :], c_cos[:, c0:c0 + CH], w=[kcs + "c"])
                sch.dma(cs[:, 1, :], c_sin[:, c0:c0 + CH], w=[kcs + "s"])

                def rope(b1, kb1, b2, kb2, m, dsts, cs=cs, kcs=kcs):
                    ka, a = rope_r.next()
                    kb_, b = rope_r.next()
                    ko1, o1 = ropeo_r.next()
                    ko2, o2 = ropeo_r.next()
                    cosv, sinv = cs[0:m, 0, :], cs[0:m, 1, :]
                    P_dve.tensor_tensor(out=a[0:m, :], in0=b1, in1=cosv, op=ALU.mult, r=[kb1, kcs + "c"], w=[ka])
                    P_dve.tensor_tensor(out=b[0:m, :], in0=b2, in1=sinv, op=ALU.mult, r=[kb2, kcs + "s"], w=[kb_])
                    P_pool.tensor_tensor(out=o1[0:m, :], in0=a[0:m, :], in1=b[0:m, :], op=ALU.subtract, r=[ka, kb_], w=[ko1])
                    kc, c_ = rope_r.next()
                    kd, d_ = rope_r.next()
                    P_dve.tensor_tensor(out=c_[0:m, :], in0=b1, in1=sinv, op=ALU.mult, r=[kb1, kcs + "s"], w=[kc])
                    P_dve.tensor_tensor(out=d_[0:m, :], in0=b2, in1=cosv, op=ALU.mult, r=[kb2, kcs + "c"], w=[kd])
                    P_pool.tensor_tensor(out=o2[0:m, :], in0=c_[0:m, :], in1=d_[0:m, :], op=ALU.add, r=[kc, kd], w=[ko2])
                    for (dst_ap, lo, which) in dsts:
                        src = (o1 if which == 0 else o2)[lo:lo + 16, :]
                        sch.dma(dst_ap, src, r=[ko1 if which == 0 else ko2], w=[])

                def rms_bcast(pstiles, nt):
                    ksq, sq = sq_r.next()
                    for t, (kb, bk) in enumerate(pstiles):
                        ke, ev = evf.next()
                        P_act.activation(out=ev, in_=bk, func=AF.Copy, r=[kb], w=[ke])
                        P_dve.tensor_tensor(out=sq[:, t, :], in0=ev, in1=bk, op=ALU.mult, r=[kb, ke], w=[ksq + "_%d" % t])
                    kss, ss = bank_ring.next()
                    for t in range(nt):
                        P_pe.matmul(ss, lhsT=ones_bf, rhs=sq[:, t, :], start=(t == 0), stop=(t == nt - 1), r=[ksq + "_%d" % t], w=[kss])
                    krq, rq = rq_r.next()
                    P_act.activation(out=rq, in_=ss, func=AF.Ln, bias=epsln[:, 1:2], scale=1.0 / (128.0 * nt), r=[kss], w=[krq])
                    P_act.activation(out=rq, in_=rq, func=AF.Exp, scale=-0.5, r=[krq], w=[krq])
                    return krq, rq

                cqt = [fm_tile(CQ + t * 128, 128) for t in range(2)]
                krq, rq = rms_bcast(cqt, 2)
                kcq, cqn = cqn_r.next()
                for t, (kb, bk) in enumerate(cqt):
                    P_dve.tensor_tensor(out=cqn[:, t, :], in0=bk, in1=rq, op=ALU.mult, r=[kb, krq], w=[kcq + "_%d" % t])
                kcqs = [kcq + "_0", kcq + "_1"]
                for tt in range(NH // 2):
                    kb, bk = bank_ring.next()
                    for t in range(2):
                        P_pe.matmul(bk, lhsT=wqg[:, t, tt * 128:(tt + 1) * 128], rhs=cqn[:, t, :], start=(t == 0), stop=(t == 1), r=kcqs, w=[kb])
                    ke, ev = evb.next()
                    P_act.activation(out=ev, in_=bk, func=AF.Copy, r=[kb], w=[ke])
                    for jj in range(2):
                        sch.dma(mla_qT[2 * tt + jj, 0:64, c0:c0 + CH], ev[jj * 64:(jj + 1) * 64, :], r=[ke], w=[])
                kb1, bk1 = bank_ring.next()
                kb2, bk2 = bank_ring.next()
                for t in range(2):
                    P_pe.matmul(bk1[0:RM, :], lhsT=wqg[:, t, QW:QW + RM], rhs=cqn[:, t, :], start=(t == 0), stop=(t == 1), r=kcqs, w=[kb1])
                for t in range(2):
                    P_pe.matmul(bk2[0:RM, :], lhsT=wqg[:, t, QW + RM:QW + 2 * RM], rhs=cqn[:, t, :], start=(t == 0), stop=(t == 1), r=kcqs, w=[kb2])
                rope(bk1[0:RM, :], kb1, bk2[0:RM, :], kb2, RM,
                     [(mla_qT[j, 64 + 16 * w_:80 + 16 * w_, c0:c0 + CH], j * 16, w_) for j in range(NH) for w_ in range(2)])
                nxt(4)
                ckt = [fm_tile(CKV, 128)]
                krk, rk = rms_bcast(ckt, 1)
                kck, ckvn = ckvn_r.next()
                P_dve.tensor_tensor(out=ckvn, in0=ckt[0][1], in1=rk, op=ALU.mult, r=[ckt[0][0], krk], w=[kck])
                for tt in range(NH // 2):
                    kb, bk = bank_ring.next()
                    P_pe.matmul(bk, lhsT=wkvg[:, tt * 128:(tt + 1) * 128], rhs=ckvn, start=True, stop=True, r=[kck], w=[kb])
                    ke, ev = evb.next()
                    P_act.activation(out=ev, in_=bk, func=AF.Copy, r=[kb], w=[ke])
                    for jj in range(2):
                        sch.dma(mla_kT[2 * tt + jj, 0:64, c0:c0 + CH], ev[jj * 64:(jj + 1) * 64, :], r=[ke], w=[])
                kvm, vaugM = vaugM_r.next()
                per_bank = 512 // QW
                for tb in range(4 // per_bank):
                    kb, bk = bank_ring.next()
                    for tq in range(per_bank):
                        ti = tb * per_bank + tq
                        P_pe.matmul(bk[:, tq * QW:(tq + 1) * QW], lhsT=ckvn[:, ti * 128:(ti + 1) * 128], rhs=wkvg[:, QW:2 * QW],
                                    start=True, stop=True, r=[kck], w=[kb])
                    P_dve.tensor_copy(out=vaugM[:, tb * per_bank:(tb + 1) * per_bank, :, 0:64],
                                      in_=bk.rearrange("p (t j d) -> p t j d", t=per_bank, j=NH), r=[kb, kvm], w=[kvm + "_%d" % tb])
                for j in range(NH):
                    sch.dma(mla_v[j, c0:c0 + CH, :].rearrange("(t p) c -> p t c", p=128), vaugM[:, :, j, :],
                            r=[kvm + "_%d" % tb for tb in range(4 // per_bank)], w=[])
                kb1, bk1 = fm_tile(KR1, 16)
                kb2, bk2 = fm_tile(KR2, 16)
                rope(bk1[0:16, :], kb1, bk2[0:16, :], kb2, 16,
                     [(mla_kT[j, 64 + 16 * w_:80 + 16 * w_, c0:c0 + CH], 0, w_) for j in range(NH) for w_ in range(2)])

            sch.barrier()
            ar.release(mW)
            if not body:
                continue

            mF = ar.mark()
            SEG = 2048
            flr = Ring("flseg", [ar.alloc([SEG], F32, parts=NH) for _ in range(2)])
            Pr = Ring("P", [ar.alloc([SEG], F32, parts=NH) for _ in range(2)])
            r1r = Ring("r1", [ar.alloc([SEG], F32, parts=NH) for _ in range(2)])
            pcr = Ring("pc", [ar.alloc([3, SEG], BF16, parts=NH) for _ in range(2)])
            ncr = Ring("nc", [ar.alloc([3, SEG], BF16, parts=NH) for _ in range(2)])
            zer = ar.alloc([SEG], F32, parts=NH)
            onesr = ar.alloc([3, SEG], BF16, parts=NH)
            carry = ar.alloc([2], F32, parts=NH)
            P_pool.memset(zer, 0.0, w=["zer"])
            P_pool.memset(onesr, 1.0, w=["onesr"])
            P_pool.memset(carry, 0.0, w=["carry"])
            for sg in range(S // SEG):
                s0 = sg * SEG
                kfl, fl = flr.next()
                kP, P = Pr.next()
                kr1, r1 = r1r.next()
                kpc, pc = pcr.next()
                knc, ncp = ncr.next()
                sch.dma(fl, flogT[:, s0:s0 + SEG], w=[kfl])
                P_act.activation(out=fl, in_=fl, func=AF.Exp, bias=nbf[:, 0:1], scale=-1.0, r=[kfl, "nbf"], w=[kfl])
                P_act.activation(out=fl, in_=fl, func=AF.Ln, bias=1.0, scale=1.0, r=[kfl], w=[kfl])
                P_dve.tensor_tensor_scan(out=P, data0=fl, data1=zer, initial=carry[:, 0:1], op0=ALU.add, op1=ALU.add,
                                         r=[kfl, "zer", "carry"], w=[kP])
                P_dve.tensor_copy(out=carry[:, 0:1], in_=P[:, SEG - 1:SEG], r=[kP], w=["carry"])
                P_dve.tensor_copy(out=pc[:, 0, :], in_=P, r=[kP], w=[kpc + "0"])
                P_dve.tensor_tensor(out=r1, in0=P, in1=pc[:, 0, :], op=ALU.subtract, r=[kP, kpc + "0"], w=[kr1])
                P_dve.tensor_copy(out=pc[:, 1, :], in_=r1, r=[kr1], w=[kpc + "1"])
                P_dve.tensor_tensor(out=r1, in0=r1, in1=pc[:, 1, :], op=ALU.subtract, r=[kr1, kpc + "1"], w=[kr1])
                P_dve.tensor_copy(out=pc[:, 2, :], in_=r1, r=[kr1], w=[kpc + "2"])
                P_dve.tensor_scalar(out=ncp, in0=pc, scalar1=-1.0, scalar2=None, op0=ALU.mult,
                                    r=[kpc + "0", kpc + "1", kpc + "2"], w=[knc])
                sch.dma(fox_qT[:, 64:67, s0:s0 + SEG], ncp, r=[knc], w=[])
                sch.dma(fox_qT[:, 67:70, s0:s0 + SEG], onesr, r=["onesr"], w=[])
                sch.dma(fox_kT[:, 64:67, s0:s0 + SEG], onesr, r=["onesr"], w=[])
                sch.dma(fox_kT[:, 67:70, s0:s0 + SEG], pc, r=[kpc + "0", kpc + "1", kpc + "2"], w=[])
            sch.barrier()
            ar.release(mF)

            qTr = Ring("qT", [ar.alloc([S], BF16, parts=96) for _ in range(2)])
            kTr = Ring("kT", [ar.alloc([S], BF16, parts=96) for _ in range(2)])
            vr = Ring("v", [ar.alloc([64, 65], BF16) for _ in range(2)])
            PT2r = Ring("PT2", [ar.alloc([2 * CH], BF16) for _ in range(3)])
            pair_i = [0]
            pair_slots = [(["bank0", "bank1"], allb[:, 0:1024]), (["bank2", "bank3"], allb[:, 1024:2048])]
            dbkr = Ring("bank", [banks[4][:], banks[5][:]])
            dbkr.aps = [banks[4][:], banks[5][:]]
            Er = Ring("E", [ar.alloc([CH], F32) for _ in range(2)])
            Lr = Ring("Lp", [ar.alloc([CH], BF16) for _ in range(3)])
            Ar = Ring("A", [ar.alloc([CH], F32) for _ in range(2)])
            Wr = Ring("W", [ar.alloc([CH], BF16) for _ in range(3)])
            Cpr = Ring("Cp", [ar.alloc([CH], F32) for _ in range(2)])
            gater = Ring("gate", [ar.alloc([CH], F32, parts=64) for _ in range(3)])
            Osbr = Ring("Osb", [ar.alloc([CH], F32, parts=65) for _ in range(2)])
            rDr = Ring("rD", [ar.alloc([CH], F32, parts=64) for _ in range(2)])
            Gstr = Ring("Gst", [ar.alloc([CH], BF16, parts=64) for _ in range(3)])
            short = Ring("bank", [banks[i][:] for i in range(5 if (junk_fox or junk_sb) else 6)])
            oaccr = Ring("oacc", [banks[6][:], banks[7][:]])
            junkb = banks[5][:]
            jsrc = cbf.rearrange("p a b -> p (a b)")

            def junk(ncols):
                if ncols:
                    P_pe.matmul(junkb[:, 0:ncols], lhsT=ones_bf, rhs=jsrc[:, 0:ncols], start=True, stop=True, skip_group_check=True, r=[], w=[])

            heads = [(kind, j) for kind in ("fox", "sb", "mla", "mem") for j in range(NH)]
            if heads_lim is not None:
                heads = [heads[i] for i in heads_lim]
            gidx = {"fox": 0, "sb": 1, "mla": 2, "mem": 3}
            srcs = {"fox": (fox_qT, fox_kT, fox_v, 70, 65), "sb": (sb_qT, sb_kT, sb_v, 64, 64),
                    "mla": (mla_qT, mla_kT, mla_v, 96, 65), "mem": (mem_qT, mem_kT, mem_v, 64, 65)}

            def load_head(kind, j):
                qd, kd, vd, KQ, DV = srcs[kind]
                kq, qT = qTr.next()
                kk, kT = kTr.next()
                kv, v = vr.next()
                sk_ = MEM if kind == "mem" else S
                sch.dma(qT[0:KQ, :], qd[j, :, :], w=[kq])
                sch.dma(kT[0:KQ, 0:sk_], kd[j, :, :], w=[kk])
                sch.dma(v[:, 0:sk_ // 128, 0:DV], vd[j, :, :].rearrange("(b p) c -> p b c", p=128), w=[kv])
                return (kq, qT, kk, kT, kv, v)

            pend_epi = [None]
            dbk_i = [0]
            loaded = load_head(*heads[0]) if heads else None
            for hi, (kind, j) in enumerate(heads):
                kq, qT, kk, kT, kv, v = loaded
                if hi + 1 < len(heads):
                    loaded = load_head(*heads[hi + 1])
                _, _, _, KQ, DV = srcs[kind]
                row0 = gidx[kind] * QW + j * 64
                scale = MLA_SCALE if kind == "mla" else 1.0
                for c in range(nchunk):
                    c0 = c * CH
                    kg, gt = gater.next()
                    sch.dma(gt, gateT[row0:row0 + 64, c0:c0 + CH], w=[kg])
                    ko, oacc = oaccr.next()
                    if kind == "mem":
                        units = [(0, 0, False), (1, 0, False)]
                    elif kind == "sb":
                        units = [(kb_, max(0, kb_ - 4 * c), kb_ >= 4 * c) for kb_ in range(4 * c + 3, -1, -1)]
                    else:
                        units = [(kb_, max(0, kb_ - 4 * c), kb_ >= 4 * c) for kb_ in range(0, 4 * c + 4)]
                    nu = len(units)

                    def s_mm(u, addmask):
                        kb_, jj, diag = units[u]
                        kbk, bk = short.next()
                        lo = jj * 128
                        P_pe.matmul(bk[:, lo:CH], lhsT=kT[0:KQ, kb_ * 128:(kb_ + 1) * 128], rhs=qT[0:KQ, c0 + lo:c0 + CH],
                                    start=True, stop=not (diag and addmask), skip_group_check=True, r=[kk, kq], w=[kbk])
                        return kbk, bk, lo

                    if kind != "sb":
                        groups = []
                        i_ = 0
                        while i_ < nu:
                            if (not units[i_][2]) and i_ + 1 < nu and (not units[i_ + 1][2]):
                                groups.append([i_, i_ + 1])
                                i_ += 2
                            else:
                                groups.append([i_])
                                i_ += 1
                        ng = len(groups)

                        def s_grp(g):
                            keys, pb = pair_slots[pair_i[0] % 2]
                            pair_i[0] += 1
                            for idx, u in enumerate(groups[g]):
                                kb_, jj, diag = units[u]
                                lo = jj * 128
                                P_pe.matmul(pb[:, idx * 512 + lo:(idx + 1) * 512], lhsT=kT[0:KQ, kb_ * 128:(kb_ + 1) * 128],
                                            rhs=qT[0:KQ, c0 + lo:c0 + CH], start=True, stop=not diag, skip_group_check=True,
                                            r=[kk, kq], w=[keys[idx]])
                                if diag:
                                    P_pe.matmul(pb[:, idx * 512 + lo:idx * 512 + lo + 128], lhsT=ident_bf, rhs=mneg_incl, start=False, stop=True,
                                                skip_group_check=True, r=[], w=[keys[idx]])
                            return keys, pb

                        gg = {}
                        for g in range(min(2, ng)):
                            gg[g] = s_grp(g)
                        for g in range(ng):
                            keys, pb = gg.pop(g)
                            us = groups[g]
                            kp, PT2 = PT2r.next()
                            if len(us) == 2:
                                P_act.activation(out=PT2[:, 0:2 * CH], in_=pb[:, 0:2 * CH], func=AF.Exp, scale=scale, r=keys, w=[kp])
                            else:
                                lo = units[us[0]][1] * 128
                                P_act.activation(out=PT2[:, lo:CH], in_=pb[:, lo:CH], func=AF.Exp, scale=scale, r=[keys[0]], w=[kp])
                            if g + 2 < ng:
                                gg[g + 2] = s_grp(g + 2)
                            for idx, u in enumerate(us):
                                kb_, jj, diag = units[u]
                                lo = jj * 128
                                P_pe.matmul(oacc[0:DV, lo:CH], lhsT=v[:, kb_, 0:DV], rhs=PT2[:, idx * 512 + lo:(idx + 1) * 512], start=(u == 0),
                                            stop=(u == nu - 1), skip_group_check=True, r=[kp, kv], w=[ko])
                            if pend_epi[0] is not None and g == min(1, ng - 1):
                                pend_epi[0]()
                                pend_epi[0] = None
                        kos, Osb = Osbr.next()
                        P_dve.tensor_copy(out=Osb[0:65, :], in_=oacc[0:65, :], r=[ko], w=[kos])

                        def epi(Osb=Osb, kos=kos, gt=gt, kg=kg, row0=row0, c0=c0):
                            dbk = banks[4 + dbk_i[0] % 2][:]
                            kdb = "bank%d" % (4 + dbk_i[0] % 2)
                            dbk_i[0] += 1
                            P_pe.matmul(dbk[0:64, :], lhsT=sel[0:65, :], rhs=Osb[0:65, :], start=True, stop=True, r=[kos], w=[kdb])
                            krd, rD = rDr.next()
                            P_dve.reciprocal(out=rD, in_=dbk[0:64, :], r=[kdb], w=[krd])
                            P_dve.tensor_tensor(out=rD, in0=rD, in1=Osb[0:64, :], op=ALU.mult, r=[krd, kos], w=[krd])
                            kgs, Gst = Gstr.next()
                            P_pool.tensor_tensor(out=Gst, in0=rD, in1=gt, op=ALU.mult, r=[krd, kg], w=[kgs])
                            sch.dma(gt_d[row0:row0 + 64, c0:c0 + CH], Gst, r=[kgs], w=[])
                        pend_epi[0] = epi
                    else:
                        if pend_epi[0] is not None:
                            pend_epi[0]()
                            pend_epi[0] = None
                        kcp, Cp = Cpr.next()
                        P_pool.memset(Cp, 0.0, w=[kcp])
                        zz = {}
                        ll = {}

                        def act12(u):
                            kb_, jj, diag = units[u]
                            kbk, bk, lo = zz.pop(u)
                            ke, E = Er.next()
                            kl, Lp = Lr.next()
                            P_act.activation(out=E[:, lo:CH], in_=bk[:, lo:CH], func=AF.Exp, r=[kbk], w=[ke, kbk + "E"])
                            P_act.activation(out=Lp[:, lo:CH], in_=E[:, lo:CH], func=AF.Ln, bias=1.0, scale=1.0, r=[ke], w=[kl])
                            if diag:
                                P_dve.tensor_tensor(out=Lp[:, lo:lo + 128], in0=Lp[:, lo:lo + 128], in1=m01_strict, op=ALU.mult, r=[kl], w=[kl])
                            ll[u] = (kl, Lp, kbk, bk)

                        for u in range(min(2, nu)):
                            zz[u] = s_mm(u, False)
                        act12(0)
                        pend = None
                        for u in range(nu):
                            kb_, jj, diag = units[u]
                            lo = jj * 128
                            kl, Lp, kab, ab = ll.pop(u)
                            P_pe.matmul(ab[:, lo:CH], lhsT=negU, rhs=Lp[:, lo:CH], start=False, stop=not diag, skip_group_check=True, r=[kl, kab + "E"], w=[kab])
                            if diag:
                                P_pe.matmul(ab[:, lo:lo + 128], lhsT=ident_bf, rhs=mneg_strict, start=False, stop=True, skip_group_check=True,
                                            r=[], w=[kab])
                            kcs_, csb = short.next()
                            P_pe.matmul(csb[:, lo:CH], lhsT=ones_bf, rhs=Lp[:, lo:CH], start=True, stop=True, r=[kl], w=[kcs_])
                            for jn in junk_sb:
                                junk(jn)
                            if u + 2 < nu:
                                zz[u + 2] = s_mm(u + 2, False)
                            if u + 1 < nu:
                                act12(u + 1)
                            ka, A = Ar.next()
                            P_dve.tensor_tensor(out=A[:, lo:CH], in0=ab[:, lo:CH], in1=Cp[:, lo:CH], op=ALU.subtract, r=[kab, kcp], w=[ka])
                            P_dve.tensor_tensor(out=Cp[:, lo:CH], in0=csb[:, lo:CH], in1=Cp[:, lo:CH], op=ALU.add, r=[kcs_, kcp], w=[kcp])
                            if pend is not None:
                                pend()
                            kw, W = Wr.next()
                            P_act.activation(out=W[:, lo:CH], in_=A[:, lo:CH], func=AF.Exp, r=[ka], w=[kw])

                            def pv(W=W, kw=kw, lo=lo, kb_=kb_, u=u):
                                P_pe.matmul(oacc[0:64, lo:CH], lhsT=v[:, kb_, 0:64], rhs=W[:, lo:CH], start=(u == 0), stop=(u == nu - 1),
                                            skip_group_check=True, r=[kw, kv], w=[ko])
                            pend = pv
                        pend()
                        kgs, Gst = Gstr.next()
                        P_dve.tensor_tensor(out=Gst, in0=oacc[0:64, :], in1=gt, op=ALU.mult, r=[ko, kg], w=[kgs])
                        sch.dma(gt_d[row0:row0 + 64, c0:c0 + CH], Gst, r=[kgs], w=[])
            if pend_epi[0] is not None:
                pend_epi[0]()
                pend_epi[0] = None
            sch.barrier()
            ar.release(mW)

        sch.barrier()
        sch.emit(nc)
    if _os.environ.get('KPEAK'):
        print('arena peak words', ar.peak, 'of', NW)
    return nc, sch.nins


NHK = 4


def _consts():
    i = np.arange(128)
    ident = np.eye(128, dtype=np.float32)
    ones = np.ones((128, 128), np.float32)
    negU = -(i[:, None] >= i[None, :]).astype(np.float32)
    m_incl = np.where(i[:, None] <= i[None, :], 0.0, NEG).astype(np.float32)
    m_strict = np.where(i[:, None] < i[None, :], 0.0, NEG).astype(np.float32)
    m01 = (i[:, None] < i[None, :]).astype(np.float32)
    c_bf = np.concatenate([ident, ones, negU, m_incl, m_strict, m01], axis=1).astype(ml_dtypes.bfloat16)
    sel = np.zeros((65, 64), np.float32)
    sel[64, :] = 1.0
    half = 16
    inv_freq = (np.float32(10000.0) ** (-np.arange(half, dtype=np.float32) / np.float32(half))).astype(np.float32)
    ang = (np.arange(S, dtype=np.float32)[:, None] * inv_freq[None, :]).astype(np.float32)
    cos = np.cos(ang).astype(np.float32).T
    sin = np.sin(ang).astype(np.float32).T
    c_cos = np.ascontiguousarray(np.concatenate([cos] * NHK, axis=0))
    c_sin = np.ascontiguousarray(np.concatenate([sin] * NHK, axis=0))
    return dict(c_bf=np.ascontiguousarray(c_bf), c_id=ident, c_sel=sel, c_cos=c_cos, c_sin=c_sin)


def _rep(v):
    return np.ascontiguousarray(np.broadcast_to(np.asarray(v, np.float32)[None, :], (128, v.shape[0])))


def _w_in_cols():
    o_fq, o_fk, o_fv, o_fl = 0, 256, 512, 768
    o_sq, o_sk, o_sv = 772, 1028, 1284
    o_cq, o_ckv, o_kr = 1540, 1796, 1924
    o_mq, o_gate = 1956, 2212
    r = lambda a, n: list(range(a, a + n))
    cols = r(o_fq, 256) + r(o_fk, 256) + r(o_sq, 256) + r(o_sk, 256) + r(o_mq, 256) + r(o_gate, 1024)
    cols += r(o_cq, 256) + r(o_ckv, 128) + r(o_kr, 16) + r(o_kr + 16, 16) + r(o_fl, 4) + r(o_fv, 256) + r(o_sv, 256)
    assert len(cols) == 3236
    return np.array(cols)


def make_in_maps(x, mem, ln_in_g, ln_in_b, mem_ln_g, mem_ln_b, w_in, b_forget, mla_q_norm_g, w_mla_q_up,
                 mla_kv_norm_g, w_mla_kv_up, w_mem_kv, w_out, ln_g, ln_b):
    f = lambda a: np.ascontiguousarray(np.asarray(a, dtype=np.float32))
    cst = _consts()
    wcols = _w_in_cols()
    hs = range(4)
    qcols = [h * 96 + d for h in hs for d in range(64)] + [h * 96 + 64 + d for h in hs for d in range(16)] + \
            [h * 96 + 80 + d for h in hs for d in range(16)]
    kvcols = [h * 128 + d for h in hs for d in range(64)] + [h * 128 + 64 + d for h in hs for d in range(64)]
    shared = dict(mlng=_rep(f(mem_ln_g)), mlnb=_rep(f(mem_ln_b)), **cst)
    lgs = [f(ln_in_g)] + [f(ln_g)[l] for l in range(DEPTH)]
    lbs = [f(ln_in_b)] + [f(ln_b)[l] for l in range(DEPTH)]
    for i in range(DEPTH + 1):
        shared["lng%d" % i] = _rep(lgs[i])
        shared["lnb%d" % i] = _rep(lbs[i])
    for l in range(DEPTH):
        shared["w_in%d" % l] = f(f(w_in)[l][:, wcols])
        shared["b_f%d" % l] = f(f(b_forget)[l].reshape(4, 1))
        shared["gq%d" % l] = f(f(mla_q_norm_g)[l].reshape(2, 128).T)
        shared["gkv%d" % l] = f(f(mla_kv_norm_g)[l].reshape(128, 1))
        shared["w_qup%d" % l] = f(f(w_mla_q_up)[l][:, qcols])
        shared["w_kvup%d" % l] = f(f(w_mla_kv_up)[l][:, kvcols])
        shared["w_mem%d" % l] = f(f(w_mem_kv)[l])
        shared["w_out%d" % l] = f(f(w_out)[l])
    maps = []
    for b in range(4):
        m = dict(shared)
        m["x"] = f(x[b])
        m["mem"] = f(mem[b])
        maps.append(m)
    return maps


_PROG = {}


def kernel(x, mem, ln_in_g, ln_in_b, mem_ln_g, mem_ln_b, w_in, b_forget, mla_q_norm_g, w_mla_q_up,
           mla_kv_norm_g, w_mla_kv_up, w_mem_kv, w_out, ln_g, ln_b):
    if "fused" not in _PROG:
        _PROG["fused"] = build_fused()[0]
    nc = _PROG["fused"]
    maps = make_in_maps(x, mem, ln_in_g, ln_in_b, mem_ln_g, mem_ln_b, w_in, b_forget, mla_q_norm_g, w_mla_q_up,
                        mla_kv_norm_g, w_mla_kv_up, w_mem_kv, w_out, ln_g, ln_b)
    res = run_bass_kernel_spmd(nc, maps, core_ids=list(range(4))).results
    return np.stack([np.asarray(res[b]["out"]) for b in range(4)], axis=0).astype(np.float32)
```

```python
import math
import numpy as np
import ml_dtypes
import concourse.bass as bass
import concourse.mybir as mybir
from concourse.bass_utils import run_bass_kernel_spmd

F32 = mybir.dt.float32
BF16 = mybir.dt.bfloat16
AF = mybir.ActivationFunctionType
ALU = mybir.AluOpType

S = 8192
D = 1024
MEM = 256
NCH = 16
CH = 512
HPG = 2
DEPTH = 2
ALPHA = (2 * DEPTH) ** 0.25
LN_EPS = 1e-5
RMS_EPS = 1e-6
NEG = -30000.0
MLA_SCALE = 96 ** -0.5

FQ, FK, SQ, SK, MQ = 0, 128, 256, 384, 512
GATE = 640
CQ = 1152
CKV = 1408
KR1 = 1536
KR2 = 1552
FL = 1568
FV = 1570
SV = 1698
NCOL = 1826

ENGS = ["pe", "act", "dve", "pool", "sp"]
NDSEM = 24
import os as _os
SUB = int(_os.environ.get('SUB', '9'))


class Sched:
    def __init__(self):
        self.q = {e: [] for e in ENGS}
        self.cnt = {e: 0 for e in ENGS}
        self.seen = {e: {} for e in ENGS}
        self.rw = {}
        self.rr = {}
        self.dval = [0] * NDSEM
        self.drr = 0
        self.nins = 0

    def _need(self, eng, reads, writes):
        need = {}

        def add(d, skip_pe):
            for sk, v in d.items():
                if skip_pe and sk == "pe" and eng == "pe":
                    continue
                if need.get(sk, 0) < v:
                    need[sk] = v

        for k in reads:
            add(self.rw.get(k, {}), False)
        for k in writes:
            add(self.rw.get(k, {}), True)
            add(self.rr.get(k, {}), False)
        out = []
        for sk, v in need.items():
            if self.seen[eng].get(sk, 0) >= v:
                continue
            self.seen[eng][sk] = v
            out.append((sk, v))
        return out

    def _record(self, tok, reads, writes):
        sk, v = tok
        for k in reads:
            self.rr.setdefault(k, {})[sk] = v
        for k in writes:
            self.rw[k] = {sk: v}
            self.rr[k] = {}

    def op(self, eng, meth, args, kwargs, r=(), w=()):
        waits = self._need(eng, r, w)
        self.cnt[eng] += 1
        tok = (eng, self.cnt[eng])
        self.q[eng].append((waits, (meth, args, kwargs), tok))
        self._record(tok, r, w)
        self.nins += 1 + max(0, len(waits) - 1)

    def proxy(self, eng):
        sch = self

        class _P:
            def __getattr__(self, meth):
                def f(*args, r=(), w=(), **kwargs):
                    sch.op(eng, meth, args, kwargs, r, w)
                return f
        return _P()

    def dma(self, out, in_, r=(), w=()):
        eng = "sp"
        waits = self._need(eng, r, w)
        i = self.drr
        self.drr = (self.drr + 1) % NDSEM
        sk = ("d", i)
        if self.dval[i] > 0 and self.seen[eng].get(sk, 0) < self.dval[i]:
            self.seen[eng][sk] = self.dval[i]
            waits.append((sk, self.dval[i]))
        self.dval[i] += 16
        tok = (sk, self.dval[i])
        self.q[eng].append((waits, (out, in_), tok))
        self._record(tok, r, w)
        self.nins += 1 + max(0, len(waits) - 1)

    def barrier(self):
        for e in ENGS:
            waits = []
            for o in ENGS:
                if o == e or o == "sp":
                    continue
                if self.cnt[o] > self.seen[e].get(o, 0):
                    self.seen[e][o] = self.cnt[o]
                    waits.append((o, self.cnt[o]))
            for i in range(NDSEM):
                sk = ("d", i)
                if self.dval[i] > self.seen[e].get(sk, 0):
                    self.seen[e][sk] = self.dval[i]
                    waits.append((sk, self.dval[i]))
            if waits:
                self.q[e].append((waits, None, None))
                self.nins += len(waits)
        self.rw = {}
        self.rr = {}

    def emit(self, nc):
        import contextlib

        with contextlib.ExitStack() as st:
            sems = {}
            for e in ["pe", "act", "dve", "pool"]:
                sems[e] = st.enter_context(nc.semaphore("s_" + e))
            for i in range(NDSEM):
                sems[("d", i)] = st.enter_context(nc.semaphore("s_d%d" % i))
            block = st.enter_context(nc.Block())

            def run(eng, e):
                for waits, fn, tok in self.q[eng]:
                    if fn is None:
                        for sk, v in waits:
                            e.wait_ge(sems[sk], v)
                        continue
                    if eng == "pe":
                        for sk, v in waits:
                            e.wait_ge(sems[sk], v)
                        waits = []
                    for sk, v in waits[:-1]:
                        e.wait_ge(sems[sk], v)
                    if eng == "sp":
                        ins = e.dma_start(out=fn[0], in_=fn[1])
                    else:
                        ins = getattr(e, fn[0])(*fn[1], **fn[2])
                    if waits:
                        ins._wait_ge(sems[waits[-1][0]], waits[-1][1])
                    if eng == "sp":
                        ins.then_inc(sems[tok[0]], 16)
                    else:
                        ins.then_inc(sems[eng], 1)

            @block.tensor
            def _(e):
                run("pe", e)

            @block.scalar
            def _(e):
                run("act", e)

            @block.vector
            def _(e):
                run("dve", e)

            @block.gpsimd
            def _(e):
                run("pool", e)

            @block.sync
            def _(e):
                run("sp", e)


class Arena:
    def __init__(self, t, nwords):
        self.t = t
        self.n = nwords
        self.off = 0

    def mark(self):
        return self.off

    def release(self, m):
        self.off = m

    def alloc(self, free_shape, dtype, parts=128):
        n = int(np.prod(free_shape))
        words = n if dtype == F32 else (n + 1) // 2
        words = (words + 1) // 2 * 2
        assert self.off + words <= self.n, ("arena overflow", self.off, words, self.n)
        ap = self.t[0:parts, self.off:self.off + words]
        self.off += words
        self.peak = max(getattr(self, "peak", 0), self.off)
        if dtype != F32:
            ap = ap.bitcast(dtype)
        ap = ap[:, 0:n]
        if len(free_shape) == 2:
            ap = ap.rearrange("p (a b) -> p a b", a=free_shape[0])
        elif len(free_shape) == 3:
            ap = ap.rearrange("p (a b c) -> p a b c", a=free_shape[0], b=free_shape[1])
        return ap


class Ring:
    def __init__(self, name, aps):
        self.name = name
        self.aps = aps
        self.i = 0

    def next(self):
        k = self.i % len(self.aps)
        self.i += 1
        return "%s%d" % (self.name, k), self.aps[k]


def build_fused(NH=4, depth=DEPTH, nchunk_lim=None, heads_lim=None, stages_lim=None, junk_fox=0, junk_sb=()):
    nc = bass.Bass("TRN2", target_bir_lowering=False)
    QW = NH * 64
    FQ, FK, SQ, SK, MQ = 0, QW, 2 * QW, 3 * QW, 4 * QW
    GATE = 5 * QW
    CQ = 9 * QW
    CKV = CQ + 256
    KR1 = CKV + 128
    KR2 = KR1 + 16
    FL = KR2 + 16
    FV = FL + NH
    SV = FV + QW
    NCOL = SV + QW
    MIX = 4 * QW
    RM = NH * 16
    dbg = bool(_os.environ.get("KDBG"))

    def din(name, shape, dt=F32):
        return nc.dram_tensor(name, list(shape), dt, kind="ExternalInput").ap()

    def dscr(name, shape, dt=BF16):
        return nc.dram_tensor(name, list(shape), dt, kind=("ExternalOutput" if dbg else "Internal")).ap()

    x_in = din("x", [S, D])
    mem_in = din("mem", [MEM, D])
    lng = [din("lng%d" % i, [128, D]) for i in range(depth + 1)]
    lnb = [din("lnb%d" % i, [128, D]) for i in range(depth + 1)]
    mlng = din("mlng", [128, D])
    mlnb = din("mlnb", [128, D])
    c_bf = din("c_bf", [128, 6 * 128], BF16)
    c_id = din("c_id", [128, 128])
    c_sel = din("c_sel", [65, 64])
    c_cos = din("c_cos", [RM, S])
    c_sin = din("c_sin", [RM, S])
    w_in = [din("w_in%d" % l, [D, NCOL]) for l in range(depth)]
    b_f = [din("b_f%d" % l, [NH, 1]) for l in range(depth)]
    gq = [din("gq%d" % l, [128, 2]) for l in range(depth)]
    gkv = [din("gkv%d" % l, [128, 1]) for l in range(depth)]
    w_qup = [din("w_qup%d" % l, [256, NH * 96]) for l in range(depth)]
    w_kvup = [din("w_kvup%d" % l, [128, 2 * QW]) for l in range(depth)]
    w_mem = [din("w_mem%d" % l, [D, 2 * QW]) for l in range(depth)]
    w_out = [din("w_out%d" % l, [MIX, D]) for l in range(depth)]
    out_d = nc.dram_tensor("out", [S, D], F32, kind="ExternalOutput").ap()

    h_d = dscr("h_scr", [S, D], F32)
    gt_d = dscr("gt_scr", [MIX, S])
    fox_qT = dscr("fox_qT", [NH, 70, S])
    fox_kT = dscr("fox_kT", [NH, 70, S])
    fox_v = dscr("fox_v", [NH, S, 65])
    sb_qT = dscr("sb_qT", [NH, 64, S])
    sb_kT = dscr("sb_kT", [NH, 64, S])
    sb_v = dscr("sb_v", [NH, S, 64])
    mla_qT = dscr("mla_qT", [NH, 96, S])
    mla_kT = dscr("mla_kT", [NH, 96, S])
    mla_v = dscr("mla_v", [NH, S, 65])
    mem_qT = dscr("mem_qT", [NH, 64, S])
    mem_kT = dscr("mem_kT", [NH, 64, MEM])
    mem_v = dscr("mem_v", [NH, MEM, 65])
    gateT = dscr("gateT", [MIX, S], F32)
    flogT = dscr("flogT", [NH, S], F32)

    sch = Sched()
    P_pe, P_act, P_dve, P_pool = sch.proxy("pe"), sch.proxy("act"), sch.proxy("dve"), sch.proxy("pool")
    NW = 52400
    import contextlib

    with contextlib.ExitStack() as st:
        arena_t = st.enter_context(nc.sbuf_tensor("arena", [128, NW], F32))
        allb = st.enter_context(nc.psum_tensor("allb", [128, 4096], F32))
        banks = [allb[:, i * 512:(i + 1) * 512] for i in range(8)]
        ar = Arena(arena_t, NW)

        cbf = ar.alloc([6, 128], BF16)
        ident_bf, ones_bf, negU, mneg_incl, mneg_strict, m01_strict = [cbf[:, i, :] for i in range(6)]
        ident_f = ar.alloc([128], F32)
        sel = ar.alloc([64], F32, parts=65)
        Gbc = ar.alloc([D], F32)
        Bbc = ar.alloc([D], F32)
        mGbc = ar.alloc([D], F32)
        mBbc = ar.alloc([D], F32)
        epsln = ar.alloc([2], F32)
        nbf = ar.alloc([2], F32, parts=NH)
        small_ring = Ring("small", [ar.alloc([16], F32) for _ in range(8)])
        sch.dma(cbf, c_bf.rearrange("p (a b) -> p a b", a=6), w=["cbf"])
        sch.dma(ident_f, c_id, w=["ident_f"])
        sch.dma(sel, c_sel, w=["sel"])
        sch.dma(mGbc, mlng, w=["mGbc"])
        sch.dma(mBbc, mlnb, w=["mBbc"])
        P_pool.memset(epsln[:, 0:1], LN_EPS, w=["epsln"])
        P_pool.memset(epsln[:, 1:2], RMS_EPS, w=["epsln2"])
        sch.barrier()

        bank_ring = Ring("bank", [b[:] for b in banks])

        def layer_norm(R2, kR2, H, kH, G, B, small):
            kst, stt = small.next()
            st6 = stt[:, 0:12].rearrange("p (a b) -> p a b", a=2)
            mv = stt[:, 12:14]
            tmp = stt[:, 14:16]
            P_dve.bn_stats(out=st6[:, 0, :], in_=R2[:, 0:512], r=[kR2], w=[kst])
            P_dve.bn_stats(out=st6[:, 1, :], in_=R2[:, 512:1024], r=[kR2], w=[kst + "b"])
            P_dve.bn_aggr(out=mv, in_=stt[:, 0:12], r=[kst, kst + "b"], w=[kst + "mv"])
            P_act.activation(out=tmp[:, 0:1], in_=mv[:, 1:2], func=AF.Ln, bias=epsln[:, 0:1], scale=1.0,
                             r=[kst + "mv"], w=[kst + "t0"])
            P_act.activation(out=tmp[:, 1:2], in_=tmp[:, 0:1], func=AF.Exp, scale=-0.5, r=[kst + "t0"], w=[kst + "t1"])
            P_dve.tensor_scalar(out=H, in0=R2, scalar1=mv[:, 0:1], scalar2=tmp[:, 1:2], op0=ALU.subtract, op1=ALU.mult,
                                r=[kR2, kst + "mv", kst + "t1"], w=[kH])
            P_pool.tensor_tensor(out=H, in0=H, in1=G, op=ALU.mult, r=[kH], w=[kH])
            P_dve.tensor_tensor(out=H, in0=H, in1=B, op=ALU.add, r=[kH], w=[kH])

        def transposes(H, kH, hT, khT, col0):
            for half in range(2):
                kb, bk = bank_ring.next()
                for cc in range(4):
                    c = half * 4 + cc
                    P_pe.transpose(out=bk[:, cc * 128:(cc + 1) * 128], in_=H[:, c * 128:(c + 1) * 128], identity=ident_f,
                                   r=[kH], w=[kb])
                src = bk.rearrange("p (a b) -> p a b", a=4)
                dst = hT[:, half * 4:(half + 1) * 4, col0:col0 + 128]
                if half == 0:
                    P_act.activation(out=dst, in_=src, func=AF.Copy, r=[kb], w=[khT + "h%d_%d" % (half, col0)])
                else:
                    P_dve.tensor_copy(out=dst, in_=src, r=[kb], w=[khT + "h%d_%d" % (half, col0)])

        def hT_keys(khT, ncols):
            return [khT + "h%d_%d" % (half, c0) for half in range(2) for c0 in range(0, ncols, 128)]

        cast_i = [0]

        def cast(out, in_, r, w):
            k = cast_i[0] % 3
            cast_i[0] += 1
            if k == 0:
                P_dve.tensor_copy(out=out, in_=in_, r=r, w=w)
            elif k == 1:
                P_pool.tensor_copy(out=out, in_=in_, r=r, w=w)
            else:
                P_act.activation(out=out, in_=in_, func=AF.Copy, r=r, w=w)

        nstage = depth + 1 if stages_lim is None else stages_lim
        for stage in range(nstage):
            body = stage < depth
            prev = stage > 0
            l = stage
            mW = ar.mark()
            sch.dma(Gbc, lng[stage], w=["Gbc"])
            sch.dma(Bbc, lnb[stage], w=["Bbc"])
            if prev:
                wout = ar.alloc([8, D], BF16)
            if body:
                wbf = ar.alloc([8, NCOL], BF16)
                wmem = ar.alloc([8, 2 * QW], BF16)
                wqg = ar.alloc([2, NH * 96], BF16)
                wkvg = ar.alloc([2 * QW], BF16)
            m0 = ar.mark()
            stg = Ring("stg", [ar.alloc([NCOL], F32) for _ in range(3)])
            if prev:
                for c in range(8):
                    ks, sg_ = stg.next()
                    sch.dma(sg_[:, 0:D], w_out[l - 1][c * 128:(c + 1) * 128, :], w=[ks])
                    cast(wout[:, c, :], sg_[:, 0:D], [ks], ["wout%d" % c])
            if body:
                for c in range(8):
                    ks, sg_ = stg.next()
                    sch.dma(sg_, w_in[l][c * 128:(c + 1) * 128, :], w=[ks])
                    cast(wbf[:, c, :], sg_, [ks], ["wbf%d" % c])
                for c in range(8):
                    ks, sg_ = stg.next()
                    sch.dma(sg_[:, 0:2 * QW], w_mem[l][c * 128:(c + 1) * 128, :], w=[ks])
                    cast(wmem[:, c, :], sg_[:, 0:2 * QW], [ks], ["wmem%d" % c])
                gq_t = ar.alloc([2], F32)
                gkv_t = ar.alloc([2], F32)
                bf_t = ar.alloc([2], F32, parts=NH)
                sch.dma(gq_t, gq[l], w=["gq_t"])
                sch.dma(gkv_t[:, 0:1], gkv[l], w=["gkv_t"])
                sch.dma(bf_t[:, 0:1], b_f[l], w=["bf_t"])
                for c in range(2):
                    ks, sg_ = stg.next()
                    sch.dma(sg_[:, 0:NH * 96], w_qup[l][c * 128:(c + 1) * 128, :], w=[ks])
                    P_dve.tensor_scalar(out=wqg[:, c, :], in0=sg_[:, 0:NH * 96], scalar1=gq_t[:, c:c + 1], scalar2=None, op0=ALU.mult,
                                        r=[ks, "gq_t"], w=["wqg%d" % c])
                ks, sg_ = stg.next()
                sch.dma(sg_[:, 0:2 * QW], w_kvup[l], w=[ks])
                P_dve.tensor_scalar(out=wkvg, in0=sg_[:, 0:2 * QW], scalar1=gkv_t[:, 0:1], scalar2=None, op0=ALU.mult,
                                    r=[ks, "gkv_t"], w=["wkvg"])
                P_dve.tensor_scalar(out=nbf[:, 0:1], in0=bf_t[:, 0:1], scalar1=-1.0, scalar2=None, op0=ALU.mult, r=["bf_t"], w=["nbf"])
            sch.barrier()
            ar.release(m0)
            mA = ar.mark()

            if body:
                Rm = Ring("Rm", [ar.alloc([D], F32) for _ in range(2)])
                Hm = Ring("Hm", [ar.alloc([D], F32) for _ in range(2)])
                mT = ar.alloc([8, MEM], BF16)
                vaugm = ar.alloc([2, NH, 65], BF16)
                kmem_r = Ring("kmem", [ar.alloc([MEM], BF16) for _ in range(2)])
                P_pool.memset(vaugm, 1.0, w=["vaugm"])
                for i in range(2):
                    kR, R = Rm.next()
                    kH, H = Hm.next()
                    sch.dma(R, mem_in[i * 128:(i + 1) * 128, :], w=[kR])
                    layer_norm(R, kR, H, kH, mGbc, mBbc, small_ring)
                    transposes(H, kH, mT, "mT", i * 128)
                for t in range(NH // 2):
                    kb, bk = bank_ring.next()
                    for c in range(8):
                        P_pe.matmul(bk[:, 0:MEM], lhsT=wmem[:, c, t * 128:(t + 1) * 128], rhs=mT[:, c, :], start=(c == 0), stop=(c == 7),
                                    r=hT_keys("mT", MEM), w=[kb])
                    kkm, kmem_sb = kmem_r.next()
                    P_act.activation(out=kmem_sb, in_=bk[:, 0:MEM], func=AF.Copy, r=[kb], w=[kkm])
                    for jj in range(2):
                        sch.dma(mem_kT[2 * t + jj, :, :], kmem_sb[jj * 64:(jj + 1) * 64, :], r=[kkm], w=[])
                for i in range(2):
                    kb, bk = bank_ring.next()
                    for c in range(8):
                        P_pe.matmul(bk[:, 0:QW], lhsT=mT[:, c, i * 128:(i + 1) * 128], rhs=wmem[:, c, QW:2 * QW], start=(c == 0), stop=(c == 7),
                                    r=hT_keys("mT", MEM), w=[kb])
                    P_dve.tensor_copy(out=vaugm[:, i, :, 0:64], in_=bk[:, 0:QW].rearrange("p (j d) -> p j d", j=NH),
                                      r=[kb, "vaugm"], w=["vaugm%d" % i])
                    for j in range(NH):
                        sch.dma(mem_v[j, i * 128:(i + 1) * 128, :], vaugm[:, i, j, :], r=["vaugm%d" % i], w=[])
                sch.barrier()
                ar.release(mA)

            Rr = Ring("R", [ar.alloc([D], F32) for _ in range(2 if body else 6)])
            Hr = Ring("H", [ar.alloc([D], F32) for _ in range(2 if body else 6)])
            if prev:
                gtfr = Ring("gtf", [ar.alloc([8, CH], BF16) for _ in range(2)])
            if body:
                hTr = Ring("hT", [ar.alloc([8, CH], BF16) for _ in range(2)])
                evb = Ring("evb", [ar.alloc([CH], BF16) for _ in range(4)])
                evf = Ring("evf", [ar.alloc([CH], F32) for _ in range(3)])
                cqn_r = Ring("cqn", [ar.alloc([2, CH], BF16) for _ in range(2)])
                sq_r = Ring("sq", [ar.alloc([2, CH], BF16) for _ in range(2)])
                ckvn_r = Ring("ckvn", [ar.alloc([CH], BF16) for _ in range(2)])
                rq_r = Ring("rq", [ar.alloc([CH], F32) for _ in range(2)])
                cs_r = Ring("cs", [ar.alloc([2, CH], F32, parts=RM) for _ in range(2)])
                rope_r = Ring("rope", [ar.alloc([CH], F32, parts=RM) for _ in range(4)])
                ropeo_r = Ring("ropeo", [ar.alloc([CH], BF16, parts=RM) for _ in range(4)])
                vaugF_r = Ring("vaugF", [ar.alloc([4, NH, 65], BF16) for _ in range(2)])
                vS_r = Ring("vS", [ar.alloc([4, NH, 64], BF16) for _ in range(2)])
                vaugM_r = Ring("vaugM", [ar.alloc([4, NH, 65], BF16) for _ in range(2)])
                fl_r = Ring("fl", [ar.alloc([CH], F32, parts=NH) for _ in range(2)])
                for k_, t_ in zip(["vaugF0", "vaugF1"], vaugF_r.aps):
                    P_pool.memset(t_, 1.0, w=[k_])
                for k_, t_ in zip(["vaugM0", "vaugM1"], vaugM_r.aps):
                    P_pool.memset(t_, 1.0, w=[k_])

            nchunk = NCH if nchunk_lim is None else min(NCH, nchunk_lim)
            src_d = x_in if stage == 0 else h_d
            dst_d = out_d if stage == depth else h_d
            cctx = {}
            rload = {}
            gload = {}
            hkeep = {}
            ntile_all = nchunk * 4

            def load_R(gi):
                if gi in rload or gi >= ntile_all:
                    return
                kR, R = Rr.next()
                sch.dma(R, src_d[gi * 128:(gi + 1) * 128, :], r=["hrow%d" % gi], w=[kR])
                rload[gi] = (kR, R)

            def load_g(ci):
                if (not prev) or ci in gload or ci >= nchunk:
                    return
                kg, gtf = gtfr.next()
                sch.dma(gtf, gt_d[:, ci * CH:(ci + 1) * CH].rearrange("(c p) s -> p c s", p=128), w=[kg])
                gload[ci] = (kg, gtf)

            def pro_a(ci, ti):
                gi = ci * 4 + ti
                if ti == 0:
                    ctx = {}
                    if body:
                        ctx["khT"], ctx["hT"] = hTr.next()
                    cctx[ci] = ctx
                    load_g(ci)
                    load_g(ci + 1)
                load_R(gi)
                load_R(gi + 1)
                if not body:
                    load_R(gi + 2)
                    load_R(gi + 3)
                kR, R = rload.pop(gi)
                kH, H = Hr.next()
                if prev:
                    kg, gtf = gload[ci]
                    for half in range(2):
                        kb, bk = bank_ring.next()
                        for c in range(8):
                            P_pe.matmul(bk, lhsT=gtf[:, c, ti * 128:(ti + 1) * 128], rhs=wout[:, c, half * 512:(half + 1) * 512],
                                        start=(c == 0), stop=(c == 7), r=[kg], w=[kb])
                        P_dve.scalar_tensor_tensor(out=R[:, half * 512:(half + 1) * 512], in0=R[:, half * 512:(half + 1) * 512],
                                                   scalar=ALPHA, in1=bk, op0=ALU.mult, op1=ALU.add, r=[kb, kR], w=[kR])
                layer_norm(R, kR, H, kH, Gbc, Bbc, small_ring)
                sch.dma(dst_d[gi * 128:(gi + 1) * 128, :], H, r=[kH], w=["hrow%d" % gi])
                hkeep[gi] = (kH, H)

            def pro_b(ci, ti):
                kH, H = hkeep.pop(ci * 4 + ti)
                if body:
                    transposes(H, kH, cctx[ci]["hT"], cctx[ci]["khT"], ti * 128)

            for ci in range(nchunk):
                c0 = ci * CH
                if ci == 0 or not body:
                    for ti in range(4):
                        pro_a(ci, ti)
                        pro_b(ci, ti)
                if not body:
                    continue
                khT, hT = cctx[ci]["khT"], cctx[ci]["hT"]

                def nxt(k, ci=ci):
                    if ci + 1 >= nchunk:
                        return
                    if k >= 1:
                        pro_b(ci + 1, k - 1)
                    if k <= 3:
                        pro_a(ci + 1, k)

                nxt(0)
                hk = hT_keys(khT, CH)

                def fm_tile(col, m, hT=hT, hk=hk):
                    kb, bk = bank_ring.next()
                    for c in range(8):
                        P_pe.matmul(bk[0:m, :], lhsT=wbf[:, c, col:col + m], rhs=hT[:, c, :], start=(c == 0), stop=(c == 7), r=hk, w=[kb])
                    return kb, bk

                for col, dst, scl in ((FQ, fox_qT, 0.125), (FK, fox_kT, 1.0), (SQ, sb_qT, 0.125), (SK, sb_kT, 1.0), (MQ, mem_qT, 0.125)):
                    for t in range(NH // 2):
                        kb, bk = fm_tile(col + t * 128, 128)
                        ke, ev = evb.next()
                        P_act.activation(out=ev, in_=bk, func=AF.Copy, scale=scl, r=[kb], w=[ke])
                        for jj in range(2):
                            sch.dma(dst[2 * t + jj, 0:64, c0:c0 + CH], ev[jj * 64:(jj + 1) * 64, :], r=[ke], w=[])
                nxt(1)
                for t in range(MIX // 128):
                    kb, bk = fm_tile(GATE + t * 128, 128)
                    ke, ev = evf.next()
                    P_act.activation(out=ev, in_=bk, func=AF.Exp, scale=-1.0, r=[kb], w=[ke])
                    P_act.activation(out=ev, in_=ev, func=AF.Ln, bias=1.0, scale=1.0, r=[ke], w=[ke])
                    P_act.activation(out=ev, in_=ev, func=AF.Exp, scale=-1.0, r=[ke], w=[ke])
                    P_dve.tensor_tensor(out=ev, in0=bk, in1=ev, op=ALU.mult, r=[kb, ke], w=[ke])
                    sch.dma(gateT[t * 128:(t + 1) * 128, c0:c0 + CH], ev, r=[ke], w=[])
                nxt(2)
                kb, bk = fm_tile(FL, NH)
                kf, fl = fl_r.next()
                P_dve.tensor_copy(out=fl, in_=bk[0:NH, :], r=[kb], w=[kf])
                sch.dma(flogT[:, c0:c0 + CH], fl, r=[kf], w=[])
                kvf, vaugF = vaugF_r.next()
                kvs, vS = vS_r.next()
                for ti in range(4):
                    kb, bk = bank_ring.next()
                    for c in range(8):
                        P_pe.matmul(bk[:, 0:2 * QW], lhsT=hT[:, c, ti * 128:(ti + 1) * 128], rhs=wbf[:, c, FV:FV + 2 * QW],
                                    start=(c == 0), stop=(c == 7), r=hk, w=[kb])
                    P_act.activation(out=vaugF[:, ti, :, 0:64], in_=bk[:, 0:QW].rearrange("p (j d) -> p j d", j=NH), func=AF.Copy,
                                     r=[kb, kvf], w=[kvf + "_%d" % ti])
                    P_act.activation(out=vS[:, ti, :, :], in_=bk[:, QW:2 * QW].rearrange("p (j d) -> p j d", j=NH), func=AF.Copy,
                                     r=[kb], w=[kvs + "_%d" % ti])
                for j in range(NH):
                    sch.dma(fox_v[j, c0:c0 + CH, :].rearrange("(t p) c -> p t c", p=128), vaugF[:, :, j, :],
                            r=[kvf + "_%d" % ti for ti in range(4)], w=[])
                    sch.dma(sb_v[j, c0:c0 + CH, :].rearrange("(t p) c -> p t c", p=128), vS[:, :, j, :],
                            r=[kvs + "_%d" % ti for ti in range(4)], w=[])
                nxt(3)
                kcs, cs = cs_r.next()
                sch.dma(cs[:, 0, :], c_cos[:, c0:c0 + CH], w=[kcs + "c"])
                sch.dma(cs[:, 1, :], c_sin[:, c0:c0 + CH], w=[kcs + "s"])

                def rope(b1, kb1, b2, kb2, m, dsts, cs=cs, kcs=kcs):
                    ka, a = rope_r.next()
                    kb_, b = rope_r.next()
                    ko1, o1 = ropeo_r.next()
                    ko2, o2 = ropeo_r.next()
                    cosv, sinv = cs[0:m, 0, :], cs[0:m, 1, :]
                    P_dve.tensor_tensor(out=a[0:m, :], in0=b1, in1=cosv, op=ALU.mult, r=[kb1, kcs + "c"], w=[ka])
                    P_dve.tensor_tensor(out=b[0:m, :], in0=b2, in1=sinv, op=ALU.mult, r=[kb2, kcs + "s"], w=[kb_])
                    P_pool.tensor_tensor(out=o1[0:m, :], in0=a[0:m, :], in1=b[0:m, :], op=ALU.subtract, r=[ka, kb_], w=[ko1])
                    kc, c_ = rope_r.next()
                    kd, d_ = rope_r.next()
                    P_dve.tensor_tensor(out=c_[0:m, :], in0=b1, in1=sinv, op=ALU.mult, r=[kb1, kcs + "s"], w=[kc])
                    P_dve.tensor_tensor(out=d_[0:m, :], in0=b2, in1=cosv, op=ALU.mult, r=[kb2, kcs + "c"], w=[kd])
                    P_pool.tensor_tensor(out=o2[0:m, :], in0=c_[0:m, :], in1=d_[0:m, :], op=ALU.add, r=[kc, kd], w=[ko2])
                    for (dst_ap, lo, which) in dsts:
                        src = (o1 if which == 0 else o2)[lo:lo + 16, :]
                        sch.dma(dst_ap, src, r=[ko1 if which == 0 else ko2], w=[])

                def rms_bcast(pstiles, nt):
                    ksq, sq = sq_r.next()
                    for t, (kb, bk) in enumerate(pstiles):
                        ke, ev = evf.next()
                        P_act.activation(out=ev, in_=bk, func=AF.Copy, r=[kb], w=[ke])
                        P_dve.tensor_tensor(out=sq[:, t, :], in0=ev, in1=bk, op=ALU.mult, r=[kb, ke], w=[ksq + "_%d" % t])
                    kss, ss = bank_ring.next()
                    for t in range(nt):
                        P_pe.matmul(ss, lhsT=ones_bf, rhs=sq[:, t, :], start=(t == 0), stop=(t == nt - 1), r=[ksq + "_%d" % t], w=[kss])
                    krq, rq = rq_r.next()
                    P_act.activation(out=rq, in_=ss, func=AF.Ln, bias=epsln[:, 1:2], scale=1.0 / (128.0 * nt), r=[kss], w=[krq])
                    P_act.activation(out=rq, in_=rq, func=AF.Exp, scale=-0.5, r=[krq], w=[krq])
                    return krq, rq

                cqt = [fm_tile(CQ + t * 128, 128) for t in range(2)]
                krq, rq = rms_bcast(cqt, 2)
                kcq, cqn = cqn_r.next()
                for t, (kb, bk) in enumerate(cqt):
                    P_dve.tensor_tensor(out=cqn[:, t, :], in0=bk, in1=rq, op=ALU.mult, r=[kb, krq], w=[kcq + "_%d" % t])
                kcqs = [kcq + "_0", kcq + "_1"]
                for tt in range(NH // 2):
                    kb, bk = bank_ring.next()
                    for t in range(2):
                        P_pe.matmul(bk, lhsT=wqg[:, t, tt * 128:(tt + 1) * 128], rhs=cqn[:, t, :], start=(t == 0), stop=(t == 1), r=kcqs, w=[kb])
                    ke, ev = evb.next()
                    P_act.activation(out=ev, in_=bk, func=AF.Copy, r=[kb], w=[ke])
                    for jj in range(2):
                        sch.dma(mla_qT[2 * tt + jj, 0:64, c0:c0 + CH], ev[jj * 64:(jj + 1) * 64, :], r=[ke], w=[])
                kb1, bk1 = bank_ring.next()
                kb2, bk2 = bank_ring.next()
                for t in range(2):
                    P_pe.matmul(bk1[0:RM, :], lhsT=wqg[:, t, QW:QW + RM], rhs=cqn[:, t, :], start=(t == 0), stop=(t == 1), r=kcqs, w=[kb1])
                for t in range(2):
                    P_pe.matmul(bk2[0:RM, :], lhsT=wqg[:, t, QW + RM:QW + 2 * RM], rhs=cqn[:, t, :], start=(t == 0), stop=(t == 1), r=kcqs, w=[kb2])
                rope(bk1[0:RM, :], kb1, bk2[0:RM, :], kb2, RM,
                     [(mla_qT[j, 64 + 16 * w_:80 + 16 * w_, c0:c0 + CH], j * 16, w_) for j in range(NH) for w_ in range(2)])
                nxt(4)
                ckt = [fm_tile(CKV, 128)]
                krk, rk = rms_bcast(ckt, 1)
                kck, ckvn = ckvn_r.next()
                P_dve.tensor_tensor(out=ckvn, in0=ckt[0][1], in1=rk, op=ALU.mult, r=[ckt[0][0], krk], w=[kck])
                for tt in range(NH // 2):
                    kb, bk = bank_ring.next()
                    P_pe.matmul(bk, lhsT=wkvg[:, tt * 128:(tt + 1) * 128], rhs=ckvn, start=True, stop=True, r=[kck], w=[kb])
                    ke, ev = evb.next()
                    P_act.activation(out=ev, in_=bk, func=AF.Copy, r=[kb], w=[ke])
                    for jj in range(2):
                        sch.dma(mla_kT[2 * tt + jj, 0:64, c0:c0 + CH], ev[jj * 64:(jj + 1) * 64, :], r=[ke], w=[])
                kvm, vaugM = vaugM_r.next()
                per_bank = 512 // QW
                for tb in range(4 // per_bank):
                    kb, bk = bank_ring.next()
                    for tq in range(per_bank):
                        ti = tb * per_bank + tq
                        P_pe.matmul(bk[:, tq * QW:(tq + 1) * QW], lhsT=ckvn[:, ti * 128:(ti + 1) * 128], rhs=wkvg[:, QW:2 * QW],
                                    start=True, stop=True, r=[kck], w=[kb])
                    P_dve.tensor_copy(out=vaugM[:, tb * per_bank:(tb + 1) * per_bank, :, 0:64],
                                      in_=bk.rearrange("p (t j d) -> p t j d", t=per_bank, j=NH), r=[kb, kvm], w=[kvm + "_%d" % tb])
                for j in range(NH):
                    sch.dma(mla_v[j, c0:c0 + CH, :].rearrange("(t p) c -> p t c", p=128), vaugM[:, :, j, :],
                            r=[kvm + "_%d" % tb for tb in range(4 // per_bank)], w=[])
                kb1, bk1 = fm_tile(KR1, 16)
                kb2, bk2 = fm_tile(KR2, 16)
                rope(bk1[0:16, :], kb1, bk2[0:16, :], kb2, 16,
                     [(mla_kT[j, 64 + 16 * w_:80 + 16 * w_, c0:c0 + CH], 0, w_) for j in range(NH) for w_ in range(2)])

            sch.barrier()
            ar.release(mW)
            if not body:
                continue

            mF = ar.mark()
            SEG = 2048
            flr = Ring("flseg", [ar.alloc([SEG], F32, parts=NH) for _ in range(2)])
            Pr = Ring("P", [ar.alloc([SEG], F32, parts=NH) for _ in range(2)])
            r1r = Ring("r1", [ar.alloc([SEG], F32, parts=NH) for _ in range(2)])
            pcr = Ring("pc", [ar.alloc([3, SEG], BF16, parts=NH) for _ in range(2)])
            ncr = Ring("nc", [ar.alloc([3, SEG], BF16, parts=NH) for _ in range(2)])
            zer = ar.alloc([SEG], F32, parts=NH)
            onesr = ar.alloc([3, SEG], BF16, parts=NH)
            carry = ar.alloc([2], F32, parts=NH)
            P_pool.memset(zer, 0.0, w=["zer"])
            P_pool.memset(onesr, 1.0, w=["onesr"])
            P_pool.memset(carry, 0.0, w=["carry"])
            for sg in range(S // SEG):
                s0 = sg * SEG
                kfl, fl = flr.next()
                kP, P = Pr.next()
                kr1, r1 = r1r.next()
                kpc, pc = pcr.next()
                knc, ncp = ncr.next()
                sch.dma(fl, flogT[:, s0:s0 + SEG], w=[kfl])
                P_act.activation(out=fl, in_=fl, func=AF.Exp, bias=nbf[:, 0:1], scale=-1.0, r=[kfl, "nbf"], w=[kfl])
                P_act.activation(out=fl, in_=fl, func=AF.Ln, bias=1.0, scale=1.0, r=[kfl], w=[kfl])
                P_dve.tensor_tensor_scan(out=P, data0=fl, data1=zer, initial=carry[:, 0:1], op0=ALU.add, op1=ALU.add,
                                         r=[kfl, "zer", "carry"], w=[kP])
                P_dve.tensor_copy(out=carry[:, 0:1], in_=P[:, SEG - 1:SEG], r=[kP], w=["carry"])
                P_dve.tensor_copy(out=pc[:, 0, :], in_=P, r=[kP], w=[kpc + "0"])
                P_dve.tensor_tensor(out=r1, in0=P, in1=pc[:, 0, :], op=ALU.subtract, r=[kP, kpc + "0"], w=[kr1])
                P_dve.tensor_copy(out=pc[:, 1, :], in_=r1, r=[kr1], w=[kpc + "1"])
                P_dve.tensor_tensor(out=r1, in0=r1, in1=pc[:, 1, :], op=ALU.subtract, r=[kr1, kpc + "1"], w=[kr1])
                P_dve.tensor_copy(out=pc[:, 2, :], in_=r1, r=[kr1], w=[kpc + "2"])
                P_dve.tensor_scalar(out=ncp, in0=pc, scalar1=-1.0, scalar2=None, op0=ALU.mult,
                                    r=[kpc + "0", kpc + "1", kpc + "2"], w=[knc])
                sch.dma(fox_qT[:, 64:67, s0:s0 + SEG], ncp, r=[knc], w=[])
                sch.dma(fox_qT[:, 67:70, s0:s0 + SEG], onesr, r=["onesr"], w=[])
                sch.dma(fox_kT[:, 64:67, s0:s0 + SEG], onesr, r=["onesr"], w=[])
                sch.dma(fox_kT[:, 67:70, s0:s0 + SEG], pc, r=[kpc + "0", kpc + "1", kpc + "2"], w=[])
            sch.barrier()
            ar.release(mF)

            qTr = Ring("qT", [ar.alloc([S], BF16, parts=96) for _ in range(2)])
            kTr = Ring("kT", [ar.alloc([S], BF16, parts=96) for _ in range(2)])
            vr = Ring("v", [ar.alloc([64, 65], BF16) for _ in range(2)])
            PT2r = Ring("PT2", [ar.alloc([2 * CH], BF16) for _ in range(3)])
            pair_i = [0]
            pair_slots = [(["bank0", "bank1"], allb[:, 0:1024]), (["bank2", "bank3"], allb[:, 1024:2048])]
            dbkr = Ring("bank", [banks[4][:], banks[5][:]])
            dbkr.aps = [banks[4][:], banks[5][:]]
            Er = Ring("E", [ar.alloc([CH], F32) for _ in range(2)])
            Lr = Ring("Lp", [ar.alloc([CH], BF16) for _ in range(3)])
            Ar = Ring("A", [ar.alloc([CH], F32) for _ in range(2)])
            Wr = Ring("W", [ar.alloc([CH], BF16) for _ in range(3)])
            Cpr = Ring("Cp", [ar.alloc([CH], F32) for _ in range(2)])
            gater = Ring("gate", [ar.alloc([CH], F32, parts=64) for _ in range(3)])
            Osbr = Ring("Osb", [ar.alloc([CH], F32, parts=65) for _ in range(2)])
            rDr = Ring("rD", [ar.alloc([CH], F32, parts=64) for _ in range(2)])
            Gstr = Ring("Gst", [ar.alloc([CH], BF16, parts=64) for _ in range(3)])
            short = Ring("bank", [banks[i][:] for i in range(5 if (junk_fox or junk_sb) else 6)])
            oaccr = Ring("oacc", [banks[6][:], banks[7][:]])
            junkb = banks[5][:]
            jsrc = cbf.rearrange("p a b -> p (a b)")

            def junk(ncols):
                if ncols:
                    P_pe.matmul(junkb[:, 0:ncols], lhsT=ones_bf, rhs=jsrc[:, 0:ncols], start=True, stop=True, skip_group_check=True, r=[], w=[])

            heads = [(kind, j) for kind in ("fox", "sb", "mla", "mem") for j in range(NH)]
            if heads_lim is not None:
                heads = [heads[i] for i in heads_lim]
            gidx = {"fox": 0, "sb": 1, "mla": 2, "mem": 3}
            srcs = {"fox": (fox_qT, fox_kT, fox_v, 70, 65), "sb": (sb_qT, sb_kT, sb_v, 64, 64),
                    "mla": (mla_qT, mla_kT, mla_v, 96, 65), "mem": (mem_qT, mem_kT, mem_v, 64, 65)}

            def load_head(kind, j):
                qd, kd, vd, KQ, DV = srcs[kind]
                kq, qT = qTr.next()
                kk, kT = kTr.next()
                kv, v = vr.next()
                sk_ = MEM if kind == "mem" else S
                sch.dma(qT[0:KQ, :], qd[j, :, :], w=[kq])
                sch.dma(kT[0:KQ, 0:sk_], kd[j, :, :], w=[kk])
                sch.dma(v[:, 0:sk_ // 128, 0:DV], vd[j, :, :].rearrange("(b p) c -> p b c", p=128), w=[kv])
                return (kq, qT, kk, kT, kv, v)

            pend_epi = [None]
            dbk_i = [0]
            loaded = load_head(*heads[0]) if heads else None
            for hi, (kind, j) in enumerate(heads):
                kq, qT, kk, kT, kv, v = loaded
                if hi + 1 < len(heads):
                    loaded = load_head(*heads[hi + 1])
                _, _, _, KQ, DV = srcs[kind]
                row0 = gidx[kind] * QW + j * 64
                scale = MLA_SCALE if kind == "mla" else 1.0
                for c in range(nchunk):
                    c0 = c * CH
                    kg, gt = gater.next()
                    sch.dma(gt, gateT[row0:row0 + 64, c0:c0 + CH], w=[kg])
                    ko, oacc = oaccr.next()
                    if kind == "mem":
                        units = [(0, 0, False), (1, 0, False)]
                    elif kind == "sb":
                        units = [(kb_, max(0, kb_ - 4 * c), kb_ >= 4 * c) for kb_ in range(4 * c + 3, -1, -1)]
                    else:
                        units = [(kb_, max(0, kb_ - 4 * c), kb_ >= 4 * c) for kb_ in range(0, 4 * c + 4)]
                    nu = len(units)

                    def s_mm(u, addmask):
                        kb_, jj, diag = units[u]
                        kbk, bk = short.next()
                        lo = jj * 128
                        P_pe.matmul(bk[:, lo:CH], lhsT=kT[0:KQ, kb_ * 128:(kb_ + 1) * 128], rhs=qT[0:KQ, c0 + lo:c0 + CH],
                                    start=True, stop=not (diag and addmask), skip_group_check=True, r=[kk, kq], w=[kbk])
                        return kbk, bk, lo

                    if kind != "sb":
                        groups = []
                        i_ = 0
                        while i_ < nu:
                            if (not units[i_][2]) and i_ + 1 < nu and (not units[i_ + 1][2]):
                                groups.append([i_, i_ + 1])
                                i_ += 2
                            else:
                                groups.append([i_])
                                i_ += 1
                        ng = len(groups)

                        def s_grp(g):
                            keys, pb = pair_slots[pair_i[0] % 2]
                            pair_i[0] += 1
                            for idx, u in enumerate(groups[g]):
                                kb_, jj, diag = units[u]
                                lo = jj * 128
                                P_pe.matmul(pb[:, idx * 512 + lo:(idx + 1) * 512], lhsT=kT[0:KQ, kb_ * 128:(kb_ + 1) * 128],
                                            rhs=qT[0:KQ, c0 + lo:c0 + CH], start=True, stop=not diag, skip_group_check=True,
                                            r=[kk, kq], w=[keys[idx]])
                                if diag:
                                    P_pe.matmul(pb[:, idx * 512 + lo:idx * 512 + lo + 128], lhsT=ident_bf, rhs=mneg_incl, start=False, stop=True,
                                                skip_group_check=True, r=[], w=[keys[idx]])
                            return keys, pb

                        gg = {}
                        for g in range(min(2, ng)):
                            gg[g] = s_grp(g)
                        for g in range(ng):
                            keys, pb = gg.pop(g)
                            us = groups[g]
                            kp, PT2 = PT2r.next()
                            if len(us) == 2:
                                P_act.activation(out=PT2[:, 0:2 * CH], in_=pb[:, 0:2 * CH], func=AF.Exp, scale=scale, r=keys, w=[kp])
                            else:
                                lo = units[us[0]][1] * 128
                                P_act.activation(out=PT2[:, lo:CH], in_=pb[:, lo:CH], func=AF.Exp, scale=scale, r=[keys[0]], w=[kp])
                            if g + 2 < ng:
                                gg[g + 2] = s_grp(g + 2)
                            for idx, u in enumerate(us):
                                kb_, jj, diag = units[u]
                                lo = jj * 128
                                P_pe.matmul(oacc[0:DV, lo:CH], lhsT=v[:, kb_, 0:DV], rhs=PT2[:, idx * 512 + lo:(idx + 1) * 512], start=(u == 0),
                                            stop=(u == nu - 1), skip_group_check=True, r=[kp, kv], w=[ko])
                            if pend_epi[0] is not None and g == min(1, ng - 1):
                                pend_epi[0]()
                                pend_epi[0] = None
                        kos, Osb = Osbr.next()
                        P_dve.tensor_copy(out=Osb[0:65, :], in_=oacc[0:65, :], r=[ko], w=[kos])

                        def epi(Osb=Osb, kos=kos, gt=gt, kg=kg, row0=row0, c0=c0):
                            dbk = banks[4 + dbk_i[0] % 2][:]
                            kdb = "bank%d" % (4 + dbk_i[0] % 2)
                            dbk_i[0] += 1
                            P_pe.matmul(dbk[0:64, :], lhsT=sel[0:65, :], rhs=Osb[0:65, :], start=True, stop=True, r=[kos], w=[kdb])
                            krd, rD = rDr.next()
                            P_dve.reciprocal(out=rD, in_=dbk[0:64, :], r=[kdb], w=[krd])
                            P_dve.tensor_tensor(out=rD, in0=rD, in1=Osb[0:64, :], op=ALU.mult, r=[krd, kos], w=[krd])
                            kgs, Gst = Gstr.next()
                            P_pool.tensor_tensor(out=Gst, in0=rD, in1=gt, op=ALU.mult, r=[krd, kg], w=[kgs])
                            sch.dma(gt_d[row0:row0 + 64, c0:c0 + CH], Gst, r=[kgs], w=[])
                        pend_epi[0] = epi
                    else:
                        if pend_epi[0] is not None:
                            pend_epi[0]()
                            pend_epi[0] = None
                        kcp, Cp = Cpr.next()
                        P_pool.memset(Cp, 0.0, w=[kcp])
                        zz = {}
                        ll = {}

                        def act12(u):
                            kb_, jj, diag = units[u]
                            kbk, bk, lo = zz.pop(u)
                            ke, E = Er.next()
                            kl, Lp = Lr.next()
                            P_act.activation(out=E[:, lo:CH], in_=bk[:, lo:CH], func=AF.Exp, r=[kbk], w=[ke, kbk + "E"])
                            P_act.activation(out=Lp[:, lo:CH], in_=E[:, lo:CH], func=AF.Ln, bias=1.0, scale=1.0, r=[ke], w=[kl])
                            if diag:
                                P_dve.tensor_tensor(out=Lp[:, lo:lo + 128], in0=Lp[:, lo:lo + 128], in1=m01_strict, op=ALU.mult, r=[kl], w=[kl])
                            ll[u] = (kl, Lp, kbk, bk)

                        for u in range(min(2, nu)):
                            zz[u] = s_mm(u, False)
                        act12(0)
                        pend = None
                        for u in range(nu):
                            kb_, jj, diag = units[u]
                            lo = jj * 128
                            kl, Lp, kab, ab = ll.pop(u)
                            P_pe.matmul(ab[:, lo:CH], lhsT=negU, rhs=Lp[:, lo:CH], start=False, stop=not diag, skip_group_check=True, r=[kl, kab + "E"], w=[kab])
                            if diag:
                                P_pe.matmul(ab[:, lo:lo + 128], lhsT=ident_bf, rhs=mneg_strict, start=False, stop=True, skip_group_check=True,
                                            r=[], w=[kab])
                            kcs_, csb = short.next()
                            P_pe.matmul(csb[:, lo:CH], lhsT=ones_bf, rhs=Lp[:, lo:CH], start=True, stop=True, r=[kl], w=[kcs_])
                            for jn in junk_sb:
                                junk(jn)
                            if u + 2 < nu:
                                zz[u + 2] = s_mm(u + 2, False)
                            if u + 1 < nu:
                                act12(u + 1)
                            ka, A = Ar.next()
                            P_dve.tensor_tensor(out=A[:, lo:CH], in0=ab[:, lo:CH], in1=Cp[:, lo:CH], op=ALU.subtract, r=[kab, kcp], w=[ka])
                            P_dve.tensor_tensor(out=Cp[:, lo:CH], in0=csb[:, lo:CH], in1=Cp[:, lo:CH], op=ALU.add, r=[kcs_, kcp], w=[kcp])
                            if pend is not None:
                                pend()
                            kw, W = Wr.next()
                            P_act.activation(out=W[:, lo:CH], in_=A[:, lo:CH], func=AF.Exp, r=[ka], w=[kw])

                            def pv(W=W, kw=kw, lo=lo, kb_=kb_, u=u):
                                P_pe.matmul(oacc[0:64, lo:CH], lhsT=v[:, kb_, 0:64], rhs=W[:, lo:CH], start=(u == 0), stop=(u == nu - 1),
                                            skip_group_check=True, r=[kw, kv], w=[ko])
                            pend = pv
                        pend()
                        kgs, Gst = Gstr.next()
                        P_dve.tensor_tensor(out=Gst, in0=oacc[0:64, :], in1=gt, op=ALU.mult, r=[ko, kg], w=[kgs])
                        sch.dma(gt_d[row0:row0 + 64, c0:c0 + CH], Gst, r=[kgs], w=[])
            if pend_epi[0] is not None:
                pend_epi[0]()
                pend_epi[0] = None
            sch.barrier()
            ar.release(mW)

        sch.barrier()
        sch.emit(nc)
    if _os.environ.get('KPEAK'):
        print('arena peak words', ar.peak, 'of', NW)
    return nc, sch.nins


NHK = 4


def _consts():
    i = np.arange(128)
    ident = np.eye(128, dtype=np.float32)
    ones = np.ones((128, 128), np.float32)
    negU = -(i[:, None] >= i[None, :]).astype(np.float32)
    m_incl = np.where(i[:, None] <= i[None, :], 0.0, NEG).astype(np.float32)
    m_strict = np.where(i[:, None] < i[None, :], 0.0, NEG).astype(np.float32)
    m01 = (i[:, None] < i[None, :]).astype(np.float32)
    c_bf = np.concatenate([ident, ones, negU, m_incl, m_strict, m01], axis=1).astype(ml_dtypes.bfloat16)
    sel = np.zeros((65, 64), np.float32)
    sel[64, :] = 1.0
    half = 16
    inv_freq = (np.float32(10000.0) ** (-np.arange(half, dtype=np.float32) / np.float32(half))).astype(np.float32)
    ang = (np.arange(S, dtype=np.float32)[:, None] * inv_freq[None, :]).astype(np.float32)
    cos = np.cos(ang).astype(np.float32).T
    sin = np.sin(ang).astype(np.float32).T
    c_cos = np.ascontiguousarray(np.concatenate([cos] * NHK, axis=0))
    c_sin = np.ascontiguousarray(np.concatenate([sin] * NHK, axis=0))
    return dict(c_bf=np.ascontiguousarray(c_bf), c_id=ident, c_sel=sel, c_cos=c_cos, c_sin=c_sin)


def _rep(v):
    return np.ascontiguousarray(np.broadcast_to(np.asarray(v, np.float32)[None, :], (128, v.shape[0])))


def _w_in_cols():
    o_fq, o_fk, o_fv, o_fl = 0, 256, 512, 768
    o_sq, o_sk, o_sv = 772, 1028, 1284
    o_cq, o_ckv, o_kr = 1540, 1796, 1924
    o_mq, o_gate = 1956, 2212
    r = lambda a, n: list(range(a, a + n))
    cols = r(o_fq, 256) + r(o_fk, 256) + r(o_sq, 256) + r(o_sk, 256) + r(o_mq, 256) + r(o_gate, 1024)
    cols += r(o_cq, 256) + r(o_ckv, 128) + r(o_kr, 16) + r(o_kr + 16, 16) + r(o_fl, 4) + r(o_fv, 256) + r(o_sv, 256)
    assert len(cols) == 3236
    return np.array(cols)


def make_in_maps(x, mem, ln_in_g, ln_in_b, mem_ln_g, mem_ln_b, w_in, b_forget, mla_q_norm_g, w_mla_q_up,
                 mla_kv_norm_g, w_mla_kv_up, w_mem_kv, w_out, ln_g, ln_b):
    f = lambda a: np.ascontiguousarray(np.asarray(a, dtype=np.float32))
    cst = _consts()
    wcols = _w_in_cols()
    hs = range(4)
    qcols = [h * 96 + d for h in hs for d in range(64)] + [h * 96 + 64 + d for h in hs for d in range(16)] + \
            [h * 96 + 80 + d for h in hs for d in range(16)]
    kvcols = [h * 128 + d for h in hs for d in range(64)] + [h * 128 + 64 + d for h in hs for d in range(64)]
    shared = dict(mlng=_rep(f(mem_ln_g)), mlnb=_rep(f(mem_ln_b)), **cst)
    lgs = [f(ln_in_g)] + [f(ln_g)[l] for l in range(DEPTH)]
    lbs = [f(ln_in_b)] + [f(ln_b)[l] for l in range(DEPTH)]
    for i in range(DEPTH + 1):
        shared["lng%d" % i] = _rep(lgs[i])
        shared["lnb%d" % i] = _rep(lbs[i])
    for l in range(DEPTH):
        shared["w_in%d" % l] = f(f(w_in)[l][:, wcols])
        shared["b_f%d" % l] = f(f(b_forget)[l].reshape(4, 1))
        shared["gq%d" % l] = f(f(mla_q_norm_g)[l].reshape(2, 128).T)
        shared["gkv%d" % l] = f(f(mla_kv_norm_g)[l].reshape(128, 1))
        shared["w_qup%d" % l] = f(f(w_mla_q_up)[l][:, qcols])
        shared["w_kvup%d" % l] = f(f(w_mla_kv_up)[l][:, kvcols])
        shared["w_mem%d" % l] = f(f(w_mem_kv)[l])
        shared["w_out%d" % l] = f(f(w_out)[l])
    maps = []
    for b in range(4):
        m = dict(shared)
        m["x"] = f(x[b])
        m["mem"] = f(mem[b])
        maps.append(m)
    return maps


_PROG = {}


def kernel(x, mem, ln_in_g, ln_in_b, mem_ln_g, mem_ln_b, w_in, b_forget, mla_q_norm_g, w_mla_q_up,
           mla_kv_norm_g, w_mla_kv_up, w_mem_kv, w_out, ln_g, ln_b):
    if "fused" not in _PROG:
        _PROG["fused"] = build_fused()[0]
    nc = _PROG["fused"]
    maps = make_in_maps(x, mem, ln_in_g, ln_in_b, mem_ln_g, mem_ln_b, w_in, b_forget, mla_q_norm_g, w_mla_q_up,
                        mla_kv_norm_g, w_mla_kv_up, w_mem_kv, w_out, ln_g, ln_b)
    res = run_bass_kernel_spmd(nc, maps, core_ids=list(range(4))).results
    return np.stack([np.asarray(res[b]["out"]) for b in range(4)], axis=0).astype(np.float32)
```

```python
import math
import numpy as np
import ml_dtypes
import concourse.bass as bass
import concourse.mybir as mybir
from concourse.bass_utils import run_bass_kernel_spmd

F32 = mybir.dt.float32
BF16 = mybir.dt.bfloat16
AF = mybir.ActivationFunctionType
ALU = mybir.AluOpType

S = 8192
D = 1024
MEM = 256
NCH = 16
CH = 512
HPG = 2
DEPTH = 2
ALPHA = (2 * DEPTH) ** 0.25
LN_EPS = 1e-5
RMS_EPS = 1e-6
NEG = -30000.0
MLA_SCALE = 96 ** -0.5

FQ, FK, SQ, SK, MQ = 0, 128, 256, 384, 512
GATE = 640
CQ = 1152
CKV = 1408
KR1 = 1536
KR2 = 1552
FL = 1568
FV = 1570
SV = 1698
NCOL = 1826

ENGS = ["pe", "act", "dve", "pool", "sp"]
NDSEM = 24
import os as _os
SUB = int(_os.environ.get('SUB', '9'))


class Sched:
    def __init__(self):
        self.q = {e: [] for e in ENGS}
        self.cnt = {e: 0 for e in ENGS}
        self.seen = {e: {} for e in ENGS}
        self.rw = {}
        self.rr = {}
        self.dval = [0] * NDSEM
        self.drr = 0
        self.nins = 0

    def _need(self, eng, reads, writes):
        need = {}

        def add(d, skip_pe):
            for sk, v in d.items():
                if skip_pe and sk == "pe" and eng == "pe":
                    continue
                if need.get(sk, 0) < v:
                    need[sk] = v

        for k in reads:
            add(self.rw.get(k, {}), False)
        for k in writes:
            add(self.rw.get(k, {}), True)
            add(self.rr.get(k, {}), False)
        out = []
        for sk, v in need.items():
            if self.seen[eng].get(sk, 0) >= v:
                continue
            self.seen[eng][sk] = v
            out.append((sk, v))
        return out

    def _record(self, tok, reads, writes):
        sk, v = tok
        for k in reads:
            self.rr.setdefault(k, {})[sk] = v
        for k in writes:
            self.rw[k] = {sk: v}
            self.rr[k] = {}

    def op(self, eng, meth, args, kwargs, r=(), w=()):
        waits = self._need(eng, r, w)
        self.cnt[eng] += 1
        tok = (eng, self.cnt[eng])
        self.q[eng].append((waits, (meth, args, kwargs), tok))
        self._record(tok, r, w)
        self.nins += 1 + max(0, len(waits) - 1)

    def proxy(self, eng):
        sch = self

        class _P:
            def __getattr__(self, meth):
                def f(*args, r=(), w=(), **kwargs):
                    sch.op(eng, meth, args, kwargs, r, w)
                return f
        return _P()

    def dma(self, out, in_, r=(), w=()):
        eng = "sp"
        waits = self._need(eng, r, w)
        i = self.drr
        self.drr = (self.drr + 1) % NDSEM
        sk = ("d", i)
        if self.dval[i] > 0 and self.seen[eng].get(sk, 0) < self.dval[i]:
            self.seen[eng][sk] = self.dval[i]
            waits.append((sk, self.dval[i]))
        self.dval[i] += 16
        tok = (sk, self.dval[i])
        self.q[eng].append((waits, (out, in_), tok))
        self._record(tok, r, w)
        self.nins += 1 + max(0, len(waits) - 1)

    def barrier(self):
        for e in ENGS:
            waits = []
            for o in ENGS:
                if o == e or o == "sp":
                    continue
                if self.cnt[o] > self.seen[e].get(o, 0):
                    self.seen[e][o] = self.cnt[o]
                    waits.append((o, self.cnt[o]))
            for i in range(NDSEM):
                sk = ("d", i)
                if self.dval[i] > self.seen[e].get(sk, 0):
                    self.seen[e][sk] = self.dval[i]
                    waits.append((sk, self.dval[i]))
            if waits:
                self.q[e].append((waits, None, None))
                self.nins += len(waits)
        self.rw = {}
        self.rr = {}

    def emit(self, nc):
        import contextlib

        with contextlib.ExitStack() as st:
            sems = {}
            for e in ["pe", "act", "dve", "pool"]:
                sems[e] = st.enter_context(nc.semaphore("s_" + e))
            for i in range(NDSEM):
                sems[("d", i)] = st.enter_context(nc.semaphore("s_d%d" % i))
            block = st.enter_context(nc.Block())

            def run(eng, e):
                for waits, fn, tok in self.q[eng]:
                    if fn is None:
                        for sk, v in waits:
                            e.wait_ge(sems[sk], v)
                        continue
                    if eng == "pe":
                        for sk, v in waits:
                            e.wait_ge(sems[sk], v)
                        waits = []
                    for sk, v in waits[:-1]:
                        e.wait_ge(sems[sk], v)
                    if eng == "sp":
                        ins = e.dma_start(out=fn[0], in_=fn[1])
                    else:
                        ins = getattr(e, fn[0])(*fn[1], **fn[2])
                    if waits:
                        ins._wait_ge(sems[waits[-1][0]], waits[-1][1])
                    if eng == "sp":
                        ins.then_inc(sems[tok[0]], 16)
                    else:
                        ins.then_inc(sems[eng], 1)

            @block.tensor
            def _(e):
                run("pe", e)

            @block.scalar
            def _(e):
                run("act", e)

            @block.vector
            def _(e):
                run("dve", e)

            @block.gpsimd
            def _(e):
                run("pool", e)

            @block.sync
            def _(e):
                run("sp", e)


class Arena:
    def __init__(self, t, nwords):
        self.t = t
        self.n = nwords
        self.off = 0

    def mark(self):
        return self.off

    def release(self, m):
        self.off = m

    def alloc(self, free_shape, dtype, parts=128):
        n = int(np.prod(free_shape))
        words = n if dtype == F32 else (n + 1) // 2
        words = (words + 1) // 2 * 2
        assert self.off + words <= self.n, ("arena overflow", self.off, words, self.n)
        ap = self.t[0:parts, self.off:self.off + words]
        self.off += words
        self.peak = max(getattr(self, "peak", 0), self.off)
        if dtype != F32:
            ap = ap.bitcast(dtype)
        ap = ap[:, 0:n]
        if len(free_shape) == 2:
            ap = ap.rearrange("p (a b) -> p a b", a=free_shape[0])
        elif len(free_shape) == 3:
            ap = ap.rearrange("p (a b c) -> p a b c", a=free_shape[0], b=free_shape[1])
        return ap


class Ring:
    def __init__(self, name, aps):
        self.name = name
        self.aps = aps
        self.i = 0

    def next(self):
        k = self.i % len(self.aps)
        self.i += 1
        return "%s%d" % (self.name, k), self.aps[k]


def build_fused(NH=4, depth=DEPTH, nchunk_lim=None, heads_lim=None, stages_lim=None, junk_fox=0, junk_sb=()):
    nc = bass.Bass("TRN2", target_bir_lowering=False)
    QW = NH * 64
    FQ, FK, SQ, SK, MQ = 0, QW, 2 * QW, 3 * QW, 4 * QW
    GATE = 5 * QW
    CQ = 9 * QW
    CKV = CQ + 256
    KR1 = CKV + 128
    KR2 = KR1 + 16
    FL = KR2 + 16
    FV = FL + NH
    SV = FV + QW
    NCOL = SV + QW
    MIX = 4 * QW
    RM = NH * 16
    dbg = bool(_os.environ.get("KDBG"))

    def din(name, shape, dt=F32):
        return nc.dram_tensor(name, list(shape), dt, kind="ExternalInput").ap()

    def dscr(name, shape, dt=BF16):
        return nc.dram_tensor(name, list(shape), dt, kind=("ExternalOutput" if dbg else "Internal")).ap()

    x_in = din("x", [S, D])
    mem_in = din("mem", [MEM, D])
    lng = [din("lng%d" % i, [128, D]) for i in range(depth + 1)]
    lnb = [din("lnb%d" % i, [128, D]) for i in range(depth + 1)]
    mlng = din("mlng", [128, D])
    mlnb = din("mlnb", [128, D])
    c_bf = din("c_bf", [128, 6 * 128], BF16)
    c_id = din("c_id", [128, 128])
    c_sel = din("c_sel", [65, 64])
    c_cos = din("c_cos", [RM, S])
    c_sin = din("c_sin", [RM, S])
    w_in = [din("w_in%d" % l, [D, NCOL]) for l in range(depth)]
    b_f = [din("b_f%d" % l, [NH, 1]) for l in range(depth)]
    gq = [din("gq%d" % l, [128, 2]) for l in range(depth)]
    gkv = [din("gkv%d" % l, [128, 1]) for l in range(depth)]
    w_qup = [din("w_qup%d" % l, [256, NH * 96]) for l in range(depth)]
    w_kvup = [din("w_kvup%d" % l, [128, 2 * QW]) for l in range(depth)]
    w_mem = [din("w_mem%d" % l, [D, 2 * QW]) for l in range(depth)]
    w_out = [din("w_out%d" % l, [MIX, D]) for l in range(depth)]
    out_d = nc.dram_tensor("out", [S, D], F32, kind="ExternalOutput").ap()

    h_d = dscr("h_scr", [S, D], F32)
    gt_d = dscr("gt_scr", [MIX, S])
    fox_qT = dscr("fox_qT", [NH, 70, S])
    fox_kT = dscr("fox_kT", [NH, 70, S])
    fox_v = dscr("fox_v", [NH, S, 65])
    sb_qT = dscr("sb_qT", [NH, 64, S])
    sb_kT = dscr("sb_kT", [NH, 64, S])
    sb_v = dscr("sb_v", [NH, S, 64])
    mla_qT = dscr("mla_qT", [NH, 96, S])
    mla_kT = dscr("mla_kT", [NH, 96, S])
    mla_v = dscr("mla_v", [NH, S, 65])
    mem_qT = dscr("mem_qT", [NH, 64, S])
    mem_kT = dscr("mem_kT", [NH, 64, MEM])
    mem_v = dscr("mem_v", [NH, MEM, 65])
    gateT = dscr("gateT", [MIX, S], F32)
    flogT = dscr("flogT", [NH, S], F32)

    sch = Sched()
    P_pe, P_act, P_dve, P_pool = sch.proxy("pe"), sch.proxy("act"), sch.proxy("dve"), sch.proxy("pool")
    NW = 52400
    import contextlib

    with contextlib.ExitStack() as st:
        arena_t = st.enter_context(nc.sbuf_tensor("arena", [128, NW], F32))
        allb = st.enter_context(nc.psum_tensor("allb", [128, 4096], F32))
        banks = [allb[:, i * 512:(i + 1) * 512] for i in range(8)]
        ar = Arena(arena_t, NW)

        cbf = ar.alloc([6, 128], BF16)
        ident_bf, ones_bf, negU, mneg_incl, mneg_strict, m01_strict = [cbf[:, i, :] for i in range(6)]
        ident_f = ar.alloc([128], F32)
        sel = ar.alloc([64], F32, parts=65)
        Gbc = ar.alloc([D], F32)
        Bbc = ar.alloc([D], F32)
        mGbc = ar.alloc([D], F32)
        mBbc = ar.alloc([D], F32)
        epsln = ar.alloc([2], F32)
        nbf = ar.alloc([2], F32, parts=NH)
        small_ring = Ring("small", [ar.alloc([16], F32) for _ in range(4)])
        sch.dma(cbf, c_bf.rearrange("p (a b) -> p a b", a=6), w=["cbf"])
        sch.dma(ident_f, c_id, w=["ident_f"])
        sch.dma(sel, c_sel, w=["sel"])
        sch.dma(mGbc, mlng, w=["mGbc"])
        sch.dma(mBbc, mlnb, w=["mBbc"])
        P_pool.memset(epsln[:, 0:1], LN_EPS, w=["epsln"])
        P_pool.memset(epsln[:, 1:2], RMS_EPS, w=["epsln2"])
        sch.barrier()

        bank_ring = Ring("bank", [b[:] for b in banks])

        def layer_norm(R2, kR2, H, kH, G, B, small):
            kst, stt = small.next()
            st6 = stt[:, 0:12].rearrange("p (a b) -> p a b", a=2)
            mv = stt[:, 12:14]
            tmp = stt[:, 14:16]
            P_dve.bn_stats(out=st6[:, 0, :], in_=R2[:, 0:512], r=[kR2], w=[kst])
            P_dve.bn_stats(out=st6[:, 1, :], in_=R2[:, 512:1024], r=[kR2], w=[kst + "b"])
            P_dve.bn_aggr(out=mv, in_=stt[:, 0:12], r=[kst, kst + "b"], w=[kst + "mv"])
            P_act.activation(out=tmp[:, 0:1], in_=mv[:, 1:2], func=AF.Ln, bias=epsln[:, 0:1], scale=1.0,
                             r=[kst + "mv"], w=[kst + "t0"])
            P_act.activation(out=tmp[:, 1:2], in_=tmp[:, 0:1], func=AF.Exp, scale=-0.5, r=[kst + "t0"], w=[kst + "t1"])
            P_dve.tensor_scalar(out=H, in0=R2, scalar1=mv[:, 0:1], scalar2=tmp[:, 1:2], op0=ALU.subtract, op1=ALU.mult,
                                r=[kR2, kst + "mv", kst + "t1"], w=[kH])
            P_pool.tensor_tensor(out=H, in0=H, in1=G, op=ALU.mult, r=[kH], w=[kH])
            P_dve.tensor_tensor(out=H, in0=H, in1=B, op=ALU.add, r=[kH], w=[kH])

        def transposes(H, kH, hT, khT, col0):
            for half in range(2):
                kb, bk = bank_ring.next()
                for cc in range(4):
                    c = half * 4 + cc
                    P_pe.transpose(out=bk[:, cc * 128:(cc + 1) * 128], in_=H[:, c * 128:(c + 1) * 128], identity=ident_f,
                                   r=[kH], w=[kb])
                src = bk.rearrange("p (a b) -> p a b", a=4)
                dst = hT[:, half * 4:(half + 1) * 4, col0:col0 + 128]
                if half == 0:
                    P_act.activation(out=dst, in_=src, func=AF.Copy, r=[kb], w=[khT + "h%d_%d" % (half, col0)])
                else:
                    P_dve.tensor_copy(out=dst, in_=src, r=[kb], w=[khT + "h%d_%d" % (half, col0)])

        def hT_keys(khT, ncols):
            return [khT + "h%d_%d" % (half, c0) for half in range(2) for c0 in range(0, ncols, 128)]

        cast_i = [0]

        def cast(out, in_, r, w):
            k = cast_i[0] % 3
            cast_i[0] += 1
            if k == 0:
                P_dve.tensor_copy(out=out, in_=in_, r=r, w=w)
            elif k == 1:
                P_pool.tensor_copy(out=out, in_=in_, r=r, w=w)
            else:
                P_act.activation(out=out, in_=in_, func=AF.Copy, r=r, w=w)

        nstage = depth + 1 if stages_lim is None else stages_lim
        for stage in range(nstage):
            body = stage < depth
            prev = stage > 0
            l = stage
            mW = ar.mark()
            sch.dma(Gbc, lng[stage], w=["Gbc"])
            sch.dma(Bbc, lnb[stage], w=["Bbc"])
            if prev:
                wout = ar.alloc([8, D], BF16)
            if body:
                wbf = ar.alloc([8, NCOL], BF16)
                wmem = ar.alloc([8, 2 * QW], BF16)
                wqg = ar.alloc([2, NH * 96], BF16)
                wkvg = ar.alloc([2 * QW], BF16)
            m0 = ar.mark()
            stg = Ring("stg", [ar.alloc([NCOL], F32) for _ in range(3)])
            if prev:
                for c in range(8):
                    ks, sg_ = stg.next()
                    sch.dma(sg_[:, 0:D], w_out[l - 1][c * 128:(c + 1) * 128, :], w=[ks])
                    cast(wout[:, c, :], sg_[:, 0:D], [ks], ["wout%d" % c])
            if body:
                for c in range(8):
                    ks, sg_ = stg.next()
                    sch.dma(sg_, w_in[l][c * 128:(c + 1) * 128, :], w=[ks])
                    cast(wbf[:, c, :], sg_, [ks], ["wbf%d" % c])
                for c in range(8):
                    ks, sg_ = stg.next()
                    sch.dma(sg_[:, 0:2 * QW], w_mem[l][c * 128:(c + 1) * 128, :], w=[ks])
                    cast(wmem[:, c, :], sg_[:, 0:2 * QW], [ks], ["wmem%d" % c])
                gq_t = ar.alloc([2], F32)
                gkv_t = ar.alloc([2], F32)
                bf_t = ar.alloc([2], F32, parts=NH)
                sch.dma(gq_t, gq[l], w=["gq_t"])
                sch.dma(gkv_t[:, 0:1], gkv[l], w=["gkv_t"])
                sch.dma(bf_t[:, 0:1], b_f[l], w=["bf_t"])
                for c in range(2):
                    ks, sg_ = stg.next()
                    sch.dma(sg_[:, 0:NH * 96], w_qup[l][c * 128:(c + 1) * 128, :], w=[ks])
                    P_dve.tensor_scalar(out=wqg[:, c, :], in0=sg_[:, 0:NH * 96], scalar1=gq_t[:, c:c + 1], scalar2=None, op0=ALU.mult,
                                        r=[ks, "gq_t"], w=["wqg%d" % c])
                ks, sg_ = stg.next()
                sch.dma(sg_[:, 0:2 * QW], w_kvup[l], w=[ks])
                P_dve.tensor_scalar(out=wkvg, in0=sg_[:, 0:2 * QW], scalar1=gkv_t[:, 0:1], scalar2=None, op0=ALU.mult,
                                    r=[ks, "gkv_t"], w=["wkvg"])
                P_dve.tensor_scalar(out=nbf[:, 0:1], in0=bf_t[:, 0:1], scalar1=-1.0, scalar2=None, op0=ALU.mult, r=["bf_t"], w=["nbf"])
            sch.barrier()
            ar.release(m0)
            mA = ar.mark()

            if body:
                Rm = Ring("Rm", [ar.alloc([D], F32) for _ in range(2)])
                Hm = Ring("Hm", [ar.alloc([D], F32) for _ in range(2)])
                mT = ar.alloc([8, MEM], BF16)
                vaugm = ar.alloc([2, NH, 65], BF16)
                kmem_r = Ring("kmem", [ar.alloc([MEM], BF16) for _ in range(2)])
                P_pool.memset(vaugm, 1.0, w=["vaugm"])
                for i in range(2):
                    kR, R = Rm.next()
                    kH, H = Hm.next()
                    sch.dma(R, mem_in[i * 128:(i + 1) * 128, :], w=[kR])
                    layer_norm(R, kR, H, kH, mGbc, mBbc, small_ring)
                    transposes(H, kH, mT, "mT", i * 128)
                for t in range(NH // 2):
                    kb, bk = bank_ring.next()
                    for c in range(8):
                        P_pe.matmul(bk[:, 0:MEM], lhsT=wmem[:, c, t * 128:(t + 1) * 128], rhs=mT[:, c, :], start=(c == 0), stop=(c == 7),
                                    r=hT_keys("mT", MEM), w=[kb])
                    kkm, kmem_sb = kmem_r.next()
                    P_act.activation(out=kmem_sb, in_=bk[:, 0:MEM], func=AF.Copy, r=[kb], w=[kkm])
                    for jj in range(2):
                        sch.dma(mem_kT[2 * t + jj, :, :], kmem_sb[jj * 64:(jj + 1) * 64, :], r=[kkm], w=[])
                for i in range(2):
                    kb, bk = bank_ring.next()
                    for c in range(8):
                        P_pe.matmul(bk[:, 0:QW], lhsT=mT[:, c, i * 128:(i + 1) * 128], rhs=wmem[:, c, QW:2 * QW], start=(c == 0), stop=(c == 7),
                                    r=hT_keys("mT", MEM), w=[kb])
                    P_dve.tensor_copy(out=vaugm[:, i, :, 0:64], in_=bk[:, 0:QW].rearrange("p (j d) -> p j d", j=NH),
                                      r=[kb, "vaugm"], w=["vaugm%d" % i])
                    for j in range(NH):
                        sch.dma(mem_v[j, i * 128:(i + 1) * 128, :], vaugm[:, i, j, :], r=["vaugm%d" % i], w=[])
                sch.barrier()
                ar.release(mA)

            Rr = Ring("R", [ar.alloc([D], F32) for _ in range(2)])
            Hr = Ring("H", [ar.alloc([D], F32) for _ in range(2)])
            if prev:
                gtfr = Ring("gtf", [ar.alloc([8, CH], BF16) for _ in range(2)])
            if body:
                hTr = Ring("hT", [ar.alloc([8, CH], BF16) for _ in range(2)])
                evb = Ring("evb", [ar.alloc([CH], BF16) for _ in range(4)])
                evf = Ring("evf", [ar.alloc([CH], F32) for _ in range(3)])
                cqn_r = Ring("cqn", [ar.alloc([2, CH], BF16) for _ in range(2)])
                sq_r = Ring("sq", [ar.alloc([2, CH], BF16) for _ in range(2)])
                ckvn_r = Ring("ckvn", [ar.alloc([CH], BF16) for _ in range(2)])
                rq_r = Ring("rq", [ar.alloc([CH], F32) for _ in range(2)])
                cs_r = Ring("cs", [ar.alloc([2, CH], F32, parts=RM) for _ in range(2)])
                rope_r = Ring("rope", [ar.alloc([CH], F32, parts=RM) for _ in range(4)])
                ropeo_r = Ring("ropeo", [ar.alloc([CH], BF16, parts=RM) for _ in range(4)])
                vaugF_r = Ring("vaugF", [ar.alloc([4, NH, 65], BF16) for _ in range(2)])
                vS_r = Ring("vS", [ar.alloc([4, NH, 64], BF16) for _ in range(2)])
                vaugM_r = Ring("vaugM", [ar.alloc([4, NH, 65], BF16) for _ in range(2)])
                fl_r = Ring("fl", [ar.alloc([CH], F32, parts=NH) for _ in range(2)])
                for k_, t_ in zip(["vaugF0", "vaugF1"], vaugF_r.aps):
                    P_pool.memset(t_, 1.0, w=[k_])
                for k_, t_ in zip(["vaugM0", "vaugM1"], vaugM_r.aps):
                    P_pool.memset(t_, 1.0, w=[k_])

            nchunk = NCH if nchunk_lim is None else min(NCH, nchunk_lim)
            src_d = x_in if stage == 0 else h_d
            dst_d = out_d if stage == depth else h_d
            cctx = {}
            rload = {}
            gload = {}
            hkeep = {}
            ntile_all = nchunk * 4

            def load_R(gi):
                if gi in rload or gi >= ntile_all:
                    return
                kR, R = Rr.next()
                sch.dma(R, src_d[gi * 128:(gi + 1) * 128, :], r=["hrow%d" % gi], w=[kR])
                rload[gi] = (kR, R)

            def load_g(ci):
                if (not prev) or ci in gload or ci >= nchunk:
                    return
                kg, gtf = gtfr.next()
                sch.dma(gtf, gt_d[:, ci * CH:(ci + 1) * CH].rearrange("(c p) s -> p c s", p=128), w=[kg])
                gload[ci] = (kg, gtf)

            def pro_a(ci, ti):
                gi = ci * 4 + ti
                if ti == 0:
                    ctx = {}
                    if body:
                        ctx["khT"], ctx["hT"] = hTr.next()
                    cctx[ci] = ctx
                    load_g(ci)
                    load_g(ci + 1)
                load_R(gi)
                load_R(gi + 1)
                kR, R = rload.pop(gi)
                kH, H = Hr.next()
                if prev:
                    kg, gtf = gload[ci]
                    for half in range(2):
                        kb, bk = bank_ring.next()
                        for c in range(8):
                            P_pe.matmul(bk, lhsT=gtf[:, c, ti * 128:(ti + 1) * 128], rhs=wout[:, c, half * 512:(half + 1) * 512],
                                        start=(c == 0), stop=(c == 7), r=[kg], w=[kb])
                        P_dve.scalar_tensor_tensor(out=R[:, half * 512:(half + 1) * 512], in0=R[:, half * 512:(half + 1) * 512],
                                                   scalar=ALPHA, in1=bk, op0=ALU.mult, op1=ALU.add, r=[kb, kR], w=[kR])
                layer_norm(R, kR, H, kH, Gbc, Bbc, small_ring)
                sch.dma(dst_d[gi * 128:(gi + 1) * 128, :], H, r=[kH], w=["hrow%d" % gi])
                hkeep[gi] = (kH, H)

            def pro_b(ci, ti):
                kH, H = hkeep.pop(ci * 4 + ti)
                if body:
                    transposes(H, kH, cctx[ci]["hT"], cctx[ci]["khT"], ti * 128)

            for ci in range(nchunk):
                c0 = ci * CH
                if ci == 0 or not body:
                    for ti in range(4):
                        pro_a(ci, ti)
                        pro_b(ci, ti)
                if not body:
                    continue
                khT, hT = cctx[ci]["khT"], cctx[ci]["hT"]

                def nxt(k, ci=ci):
                    if ci + 1 >= nchunk:
                        return
                    if k >= 1:
                        pro_b(ci + 1, k - 1)
                    if k <= 3:
                        pro_a(ci + 1, k)

                nxt(0)
                hk = hT_keys(khT, CH)

                def fm_tile(col, m, hT=hT, hk=hk):
                    kb, bk = bank_ring.next()
                    for c in range(8):
                        P_pe.matmul(bk[0:m, :], lhsT=wbf[:, c, col:col + m], rhs=hT[:, c, :], start=(c == 0), stop=(c == 7), r=hk, w=[kb])
                    return kb, bk

                for col, dst, scl in ((FQ, fox_qT, 0.125), (FK, fox_kT, 1.0), (SQ, sb_qT, 0.125), (SK, sb_kT, 1.0), (MQ, mem_qT, 0.125)):
                    for t in range(NH // 2):
                        kb, bk = fm_tile(col + t * 128, 128)
                        ke, ev = evb.next()
                        P_act.activation(out=ev, in_=bk, func=AF.Copy, scale=scl, r=[kb], w=[ke])
                        for jj in range(2):
                            sch.dma(dst[2 * t + jj, 0:64, c0:c0 + CH], ev[jj * 64:(jj + 1) * 64, :], r=[ke], w=[])
                nxt(1)
                for t in range(MIX // 128):
                    kb, bk = fm_tile(GATE + t * 128, 128)
                    ke, ev = evf.next()
                    P_act.activation(out=ev, in_=bk, func=AF.Exp, scale=-1.0, r=[kb], w=[ke])
                    P_act.activation(out=ev, in_=ev, func=AF.Ln, bias=1.0, scale=1.0, r=[ke], w=[ke])
                    P_act.activation(out=ev, in_=ev, func=AF.Exp, scale=-1.0, r=[ke], w=[ke])
                    P_dve.tensor_tensor(out=ev, in0=bk, in1=ev, op=ALU.mult, r=[kb, ke], w=[ke])
                    sch.dma(gateT[t * 128:(t + 1) * 128, c0:c0 + CH], ev, r=[ke], w=[])
                nxt(2)
                kb, bk = fm_tile(FL, NH)
                kf, fl = fl_r.next()
                P_dve.tensor_copy(out=fl, in_=bk[0:NH, :], r=[kb], w=[kf])
                sch.dma(flogT[:, c0:c0 + CH], fl, r=[kf], w=[])
                kvf, vaugF = vaugF_r.next()
                kvs, vS = vS_r.next()
                for ti in range(4):
                    kb, bk = bank_ring.next()
                    for c in range(8):
                        P_pe.matmul(bk[:, 0:2 * QW], lhsT=hT[:, c, ti * 128:(ti + 1) * 128], rhs=wbf[:, c, FV:FV + 2 * QW],
                                    start=(c == 0), stop=(c == 7), r=hk, w=[kb])
                    P_act.activation(out=vaugF[:, ti, :, 0:64], in_=bk[:, 0:QW].rearrange("p (j d) -> p j d", j=NH), func=AF.Copy,
                                     r=[kb, kvf], w=[kvf + "_%d" % ti])
                    P_act.activation(out=vS[:, ti, :, :], in_=bk[:, QW:2 * QW].rearrange("p (j d) -> p j d", j=NH), func=AF.Copy,
                                     r=[kb], w=[kvs + "_%d" % ti])
                for j in range(NH):
                    sch.dma(fox_v[j, c0:c0 + CH, :].rearrange("(t p) c -> p t c", p=128), vaugF[:, :, j, :],
                            r=[kvf + "_%d" % ti for ti in range(4)], w=[])
                    sch.dma(sb_v[j, c0:c0 + CH, :].rearrange("(t p) c -> p t c", p=128), vS[:, :, j, :],
                            r=[kvs + "_%d" % ti for ti in range(4)], w=[])
                nxt(3)
                kcs, cs = cs_r.next()
                sch.dma(cs[:, 0, :], c_cos[:, c0:c0 + CH], w=[kcs + "c"])
                sch.dma(cs[:, 1, :], c_sin[:, c0:c0 + CH], w=[kcs + "s"])

                def rope(b1, kb1, b2, kb2, m, dsts, cs=cs, kcs=kcs):
                    ka, a = rope_r.next()
                    kb_, b = rope_r.next()
                    ko1, o1 = ropeo_r.next()
                    ko2, o2 = ropeo_r.next()
                    cosv, sinv = cs[0:m, 0, :], cs[0:m, 1, :]
                    P_dve.tensor_tensor(out=a[0:m, :], in0=b1, in1=cosv, op=ALU.mult, r=[kb1, kcs + "c"], w=[ka])
                    P_dve.tensor_tensor(out=b[0:m, :], in0=b2, in1=sinv, op=ALU.mult, r=[kb2, kcs + "s"], w=[kb_])
                    P_pool.tensor_tensor(out=o1[0:m, :], in0=a[0:m, :], in1=b[0:m, :], op=ALU.subtract, r=[ka, kb_], w=[ko1])
                    kc, c_ = rope_r.next()
                    kd, d_ = rope_r.next()
                    P_dve.tensor_tensor(out=c_[0:m, :], in0=b1, in1=sinv, op=ALU.mult, r=[kb1, kcs + "s"], w=[kc])
                    P_dve.tensor_tensor(out=d_[0:m, :], in0=b2, in1=cosv, op=ALU.mult, r=[kb2, kcs + "c"], w=[kd])
                    P_pool.tensor_tensor(out=o2[0:m, :], in0=c_[0:m, :], in1=d_[0:m, :], op=ALU.add, r=[kc, kd], w=[ko2])
                    for (dst_ap, lo, which) in dsts:
                        src = (o1 if which == 0 else o2)[lo:lo + 16, :]
                        sch.dma(dst_ap, src, r=[ko1 if which == 0 else ko2], w=[])

                def rms_bcast(pstiles, nt):
                    ksq, sq = sq_r.next()
                    for t, (kb, bk) in enumerate(pstiles):
                        ke, ev = evf.next()
                        P_act.activation(out=ev, in_=bk, func=AF.Copy, r=[kb], w=[ke])
                        P_dve.tensor_tensor(out=sq[:, t, :], in0=ev, in1=bk, op=ALU.mult, r=[kb, ke], w=[ksq + "_%d" % t])
                    kss, ss = bank_ring.next()
                    for t in range(nt):
                        P_pe.matmul(ss, lhsT=ones_bf, rhs=sq[:, t, :], start=(t == 0), stop=(t == nt - 1), r=[ksq + "_%d" % t], w=[kss])
                    krq, rq = rq_r.next()
                    P_act.activation(out=rq, in_=ss, func=AF.Ln, bias=epsln[:, 1:2], scale=1.0 / (128.0 * nt), r=[kss], w=[krq])
                    P_act.activation(out=rq, in_=rq, func=AF.Exp, scale=-0.5, r=[krq], w=[krq])
                    return krq, rq

                cqt = [fm_tile(CQ + t * 128, 128) for t in range(2)]
                krq, rq = rms_bcast(cqt, 2)
                kcq, cqn = cqn_r.next()
                for t, (kb, bk) in enumerate(cqt):
                    P_dve.tensor_tensor(out=cqn[:, t, :], in0=bk, in1=rq, op=ALU.mult, r=[kb, krq], w=[kcq + "_%d" % t])
                kcqs = [kcq + "_0", kcq + "_1"]
                for tt in range(NH // 2):
                    kb, bk = bank_ring.next()
                    for t in range(2):
                        P_pe.matmul(bk, lhsT=wqg[:, t, tt * 128:(tt + 1) * 128], rhs=cqn[:, t, :], start=(t == 0), stop=(t == 1), r=kcqs, w=[kb])
                    ke, ev = evb.next()
                    P_act.activation(out=ev, in_=bk, func=AF.Copy, r=[kb], w=[ke])
                    for jj in range(2):
                        sch.dma(mla_qT[2 * tt + jj, 0:64, c0:c0 + CH], ev[jj * 64:(jj + 1) * 64, :], r=[ke], w=[])
                kb1, bk1 = bank_ring.next()
                kb2, bk2 = bank_ring.next()
                for t in range(2):
                    P_pe.matmul(bk1[0:RM, :], lhsT=wqg[:, t, QW:QW + RM], rhs=cqn[:, t, :], start=(t == 0), stop=(t == 1), r=kcqs, w=[kb1])
                for t in range(2):
                    P_pe.matmul(bk2[0:RM, :], lhsT=wqg[:, t, QW + RM:QW + 2 * RM], rhs=cqn[:, t, :], start=(t == 0), stop=(t == 1), r=kcqs, w=[kb2])
                rope(bk1[0:RM, :], kb1, bk2[0:RM, :], kb2, RM,
                     [(mla_qT[j, 64 + 16 * w_:80 + 16 * w_, c0:c0 + CH], j * 16, w_) for j in range(NH) for w_ in range(2)])
                nxt(4)
                ckt = [fm_tile(CKV, 128)]
                krk, rk = rms_bcast(ckt, 1)
                kck, ckvn = ckvn_r.next()
                P_dve.tensor_tensor(out=ckvn, in0=ckt[0][1], in1=rk, op=ALU.mult, r=[ckt[0][0], krk], w=[kck])
                for tt in range(NH // 2):
                    kb, bk = bank_ring.next()
                    P_pe.matmul(bk, lhsT=wkvg[:, tt * 128:(tt + 1) * 128], rhs=ckvn, start=True, stop=True, r=[kck], w=[kb])
                    ke, ev = evb.next()
                    P_act.activation(out=ev, in_=bk, func=AF.Copy, r=[kb], w=[ke])
                    for jj in range(2):
                        sch.dma(mla_kT[2 * tt + jj, 0:64, c0:c0 + CH], ev[jj * 64:(jj + 1) * 64, :], r=[ke], w=[])
                kvm, vaugM = vaugM_r.next()
                per_bank = 512 // QW
                for tb in range(4 // per_bank):
                    kb, bk = bank_ring.next()
                    for tq in range(per_bank):
                        ti = tb * per_bank + tq
                        P_pe.matmul(bk[:, tq * QW:(tq + 1) * QW], lhsT=ckvn[:, ti * 128:(ti + 1) * 128], rhs=wkvg[:, QW:2 * QW],
                                    start=True, stop=True, r=[kck], w=[kb])
                    P_dve.tensor_copy(out=vaugM[:, tb * per_bank:(tb + 1) * per_bank, :, 0:64],
                                      in_=bk.rearrange("p (t j d) -> p t j d", t=per_bank, j=NH), r=[kb, kvm], w=[kvm + "_%d" % tb])
                for j in range(NH):
                    sch.dma(mla_v[j, c0:c0 + CH, :].rearrange("(t p) c -> p t c", p=128), vaugM[:, :, j, :],
                            r=[kvm + "_%d" % tb for tb in range(4 // per_bank)], w=[])
                kb1, bk1 = fm_tile(KR1, 16)
                kb2, bk2 = fm_tile(KR2, 16)
                rope(bk1[0:16, :], kb1, bk2[0:16, :], kb2, 16,
                     [(mla_kT[j, 64 + 16 * w_:80 + 16 * w_, c0:c0 + CH], 0, w_) for j in range(NH) for w_ in range(2)])

            sch.barrier()
            ar.release(mW)
            if not body:
                continue

            mF = ar.mark()
            SEG = 2048
            flr = Ring("flseg", [ar.alloc([SEG], F32, parts=NH) for _ in range(2)])
            Pr = Ring("P", [ar.alloc([SEG], F32, parts=NH) for _ in range(2)])
            r1r = Ring("r1", [ar.alloc([SEG], F32, parts=NH) for _ in range(2)])
            pcr = Ring("pc", [ar.alloc([3, SEG], BF16, parts=NH) for _ in range(2)])
            ncr = Ring("nc", [ar.alloc([3, SEG], BF16, parts=NH) for _ in range(2)])
            zer = ar.alloc([SEG], F32, parts=NH)
            onesr = ar.alloc([3, SEG], BF16, parts=NH)
            carry = ar.alloc([2], F32, parts=NH)
            P_pool.memset(zer, 0.0, w=["zer"])
            P_pool.memset(onesr, 1.0, w=["onesr"])
            P_pool.memset(carry, 0.0, w=["carry"])
            for sg in range(S // SEG):
                s0 = sg * SEG
                kfl, fl = flr.next()
                kP, P = Pr.next()
                kr1, r1 = r1r.next()
                kpc, pc = pcr.next()
                knc, ncp = ncr.next()
                sch.dma(fl, flogT[:, s0:s0 + SEG], w=[kfl])
                P_act.activation(out=fl, in_=fl, func=AF.Exp, bias=nbf[:, 0:1], scale=-1.0, r=[kfl, "nbf"], w=[kfl])
                P_act.activation(out=fl, in_=fl, func=AF.Ln, bias=1.0, scale=1.0, r=[kfl], w=[kfl])
                P_dve.tensor_tensor_scan(out=P, data0=fl, data1=zer, initial=carry[:, 0:1], op0=ALU.add, op1=ALU.add,
                                         r=[kfl, "zer", "carry"], w=[kP])
                P_dve.tensor_copy(out=carry[:, 0:1], in_=P[:, SEG - 1:SEG], r=[kP], w=["carry"])
                P_dve.tensor_copy(out=pc[:, 0, :], in_=P, r=[kP], w=[kpc + "0"])
                P_dve.tensor_tensor(out=r1, in0=P, in1=pc[:, 0, :], op=ALU.subtract, r=[kP, kpc + "0"], w=[kr1])
                P_dve.tensor_copy(out=pc[:, 1, :], in_=r1, r=[kr1], w=[kpc + "1"])
                P_dve.tensor_tensor(out=r1, in0=r1, in1=pc[:, 1, :], op=ALU.subtract, r=[kr1, kpc + "1"], w=[kr1])
                P_dve.tensor_copy(out=pc[:, 2, :], in_=r1, r=[kr1], w=[kpc + "2"])
                P_dve.tensor_scalar(out=ncp, in0=pc, scalar1=-1.0, scalar2=None, op0=ALU.mult,
                                    r=[kpc + "0", kpc + "1", kpc + "2"], w=[knc])
                sch.dma(fox_qT[:, 64:67, s0:s0 + SEG], ncp, r=[knc], w=[])
                sch.dma(fox_qT[:, 67:70, s0:s0 + SEG], onesr, r=["onesr"], w=[])
                sch.dma(fox_kT[:, 64:67, s0:s0 + SEG], onesr, r=["onesr"], w=[])
                sch.dma(fox_kT[:, 67:70, s0:s0 + SEG], pc, r=[kpc + "0", kpc + "1", kpc + "2"], w=[])
            sch.barrier()
            ar.release(mF)

            qTr = Ring("qT", [ar.alloc([S], BF16, parts=96) for _ in range(2)])
            kTr = Ring("kT", [ar.alloc([S], BF16, parts=96) for _ in range(2)])
            vr = Ring("v", [ar.alloc([64, 65], BF16) for _ in range(2)])
            PT2r = Ring("PT2", [ar.alloc([2 * CH], BF16) for _ in range(3)])
            pair_i = [0]
            pair_slots = [(["bank0", "bank1"], allb[:, 0:1024]), (["bank2", "bank3"], allb[:, 1024:2048])]
            dbkr = Ring("bank", [banks[4][:], banks[5][:]])
            dbkr.aps = [banks[4][:], banks[5][:]]
            Er = Ring("E", [ar.alloc([CH], F32) for _ in range(2)])
            Lr = Ring("Lp", [ar.alloc([CH], BF16) for _ in range(3)])
            Ar = Ring("A", [ar.alloc([CH], F32) for _ in range(2)])
            Wr = Ring("W", [ar.alloc([CH], BF16) for _ in range(3)])
            Cpr = Ring("Cp", [ar.alloc([CH], F32) for _ in range(2)])
            gater = Ring("gate", [ar.alloc([CH], F32, parts=64) for _ in range(3)])
            Osbr = Ring("Osb", [ar.alloc([CH], F32, parts=65) for _ in range(2)])
            rDr = Ring("rD", [ar.alloc([CH], F32, parts=64) for _ in range(2)])
            Gstr = Ring("Gst", [ar.alloc([CH], BF16, parts=64) for _ in range(3)])
            short = Ring("bank", [banks[i][:] for i in range(5 if (junk_fox or junk_sb) else 6)])
            oaccr = Ring("oacc", [banks[6][:], banks[7][:]])
            junkb = banks[5][:]
            jsrc = cbf.rearrange("p a b -> p (a b)")

            def junk(ncols):
                if ncols:
                    P_pe.matmul(junkb[:, 0:ncols], lhsT=ones_bf, rhs=jsrc[:, 0:ncols], start=True, stop=True, skip_group_check=True, r=[], w=[])

            heads = [(kind, j) for kind in ("fox", "sb", "mla", "mem") for j in range(NH)]
            if heads_lim is not None:
                heads = [heads[i] for i in heads_lim]
            gidx = {"fox": 0, "sb": 1, "mla": 2, "mem": 3}
            srcs = {"fox": (fox_qT, fox_kT, fox_v, 70, 65), "sb": (sb_qT, sb_kT, sb_v, 64, 64),
                    "mla": (mla_qT, mla_kT, mla_v, 96, 65), "mem": (mem_qT, mem_kT, mem_v, 64, 65)}

            def load_head(kind, j):
                qd, kd, vd, KQ, DV = srcs[kind]
                kq, qT = qTr.next()
                kk, kT = kTr.next()
                kv, v = vr.next()
                sk_ = MEM if kind == "mem" else S
                sch.dma(qT[0:KQ, :], qd[j, :, :], w=[kq])
                sch.dma(kT[0:KQ, 0:sk_], kd[j, :, :], w=[kk])
                sch.dma(v[:, 0:sk_ // 128, 0:DV], vd[j, :, :].rearrange("(b p) c -> p b c", p=128), w=[kv])
                return (kq, qT, kk, kT, kv, v)

            pend_epi = [None]
            dbk_i = [0]
            loaded = load_head(*heads[0]) if heads else None
            for hi, (kind, j) in enumerate(heads):
                kq, qT, kk, kT, kv, v = loaded
                if hi + 1 < len(heads):
                    loaded = load_head(*heads[hi + 1])
                _, _, _, KQ, DV = srcs[kind]
                row0 = gidx[kind] * QW + j * 64
                scale = MLA_SCALE if kind == "mla" else 1.0
                pre_s = {}
                for c in range(nchunk):
                    c0 = c * CH
                    kg, gt = gater.next()
                    sch.dma(gt, gateT[row0:row0 + 64, c0:c0 + CH], w=[kg])
                    ko, oacc = oaccr.next()
                    if kind == "mem":
                        units = [(0, 0, False), (1, 0, False)]
                    elif kind == "sb":
                        units = [(kb_, max(0, kb_ - 4 * c), kb_ >= 4 * c) for kb_ in range(4 * c + 3, -1, -1)]
                    else:
                        units = [(kb_, max(0, kb_ - 4 * c), kb_ >= 4 * c) for kb_ in range(0, 4 * c + 4)]
                    nu = len(units)

                    def s_mm(u, addmask):
                        kb_, jj, diag = units[u]
                        kbk, bk = short.next()
                        lo = jj * 128
                        P_pe.matmul(bk[:, lo:CH], lhsT=kT[0:KQ, kb_ * 128:(kb_ + 1) * 128], rhs=qT[0:KQ, c0 + lo:c0 + CH],
                                    start=True, stop=not (diag and addmask), skip_group_check=True, r=[kk, kq], w=[kbk])
                        return kbk, bk, lo

                    if kind != "sb":
                        groups = []
                        i_ = 0
                        while i_ < nu:
                            if (not units[i_][2]) and i_ + 1 < nu and (not units[i_ + 1][2]):
                                groups.append([i_, i_ + 1])
                                i_ += 2
                            else:
                                groups.append([i_])
                                i_ += 1
                        ng = len(groups)

                        def mk_next(cc):
                            if kind == "mem":
                                un = [(0, 0, False), (1, 0, False)]
                            else:
                                un = [(kb_, max(0, kb_ - 4 * cc), kb_ >= 4 * cc) for kb_ in range(0, 4 * cc + 4)]
                            gn = []
                            i2 = 0
                            while i2 < len(un):
                                if (not un[i2][2]) and i2 + 1 < len(un) and (not un[i2 + 1][2]):
                                    gn.append([i2, i2 + 1])
                                    i2 += 2
                                else:
                                    gn.append([i2])
                                    i2 += 1
                            return un, gn

                        def s_grp(g, units_=None, groups_=None, c0_=None):
                            units_ = units if units_ is None else units_
                            groups_ = groups if groups_ is None else groups_
                            c0_ = c0 if c0_ is None else c0_
                            keys, pb = pair_slots[pair_i[0] % 2]
                            pair_i[0] += 1
                            for idx, u in enumerate(groups_[g]):
                                kb_, jj, diag = units_[u]
                                lo = jj * 128
                                P_pe.matmul(pb[:, idx * 512 + lo:(idx + 1) * 512], lhsT=kT[0:KQ, kb_ * 128:(kb_ + 1) * 128],
                                            rhs=qT[0:KQ, c0_ + lo:c0_ + CH], start=True, stop=not diag, skip_group_check=True,
                                            r=[kk, kq], w=[keys[idx]])
                                if diag:
                                    P_pe.matmul(pb[:, idx * 512 + lo:idx * 512 + lo + 128], lhsT=ident_bf, rhs=mneg_incl, start=False, stop=True,
                                                skip_group_check=True, r=[], w=[keys[idx]])
                            return keys, pb

                        gg = {}
                        for g in range(min(2, ng)):
                            gg[g] = pre_s.pop(g) if g in pre_s else s_grp(g)
                        for g in range(ng):
                            keys, pb = gg.pop(g)
                            us = groups[g]
                            kp, PT2 = PT2r.next()
                            if len(us) == 2:
                                P_act.activation(out=PT2[:, 0:2 * CH], in_=pb[:, 0:2 * CH], func=AF.Exp, scale=scale, r=keys, w=[kp])
                            else:
                                lo = units[us[0]][1] * 128
                                P_act.activation(out=PT2[:, lo:CH], in_=pb[:, lo:CH], func=AF.Exp, scale=scale, r=[keys[0]], w=[kp])
                            if g + 2 < ng:
                                gg[g + 2] = s_grp(g + 2)
                            elif c + 1 < nchunk and ng >= 2:
                                un_, gn_ = mk_next(c + 1)
                                g2 = g + 2 - ng
                                if g2 < len(gn_):
                                    pre_s[g2] = s_grp(g2, un_, gn_, (c + 1) * CH)
                            for idx, u in enumerate(us):
                                kb_, jj, diag = units[u]
                                lo = jj * 128
                                P_pe.matmul(oacc[0:DV, lo:CH], lhsT=v[:, kb_, 0:DV], rhs=PT2[:, idx * 512 + lo:(idx + 1) * 512], start=(u == 0),
                                            stop=(u == nu - 1), skip_group_check=True, r=[kp, kv], w=[ko])
                            if pend_epi[0] is not None and g == min(1, ng - 1):
                                pend_epi[0]()
                                pend_epi[0] = None
                        kos, Osb = Osbr.next()
                        P_dve.tensor_copy(out=Osb[0:65, :], in_=oacc[0:65, :], r=[ko], w=[kos])

                        def epi(Osb=Osb, kos=kos, gt=gt, kg=kg, row0=row0, c0=c0):
                            dbk = banks[4 + dbk_i[0] % 2][:]
                            kdb = "bank%d" % (4 + dbk_i[0] % 2)
                            dbk_i[0] += 1
                            P_pe.matmul(dbk[0:64, :], lhsT=sel[0:65, :], rhs=Osb[0:65, :], start=True, stop=True, r=[kos], w=[kdb])
                            krd, rD = rDr.next()
                            P_dve.reciprocal(out=rD, in_=dbk[0:64, :], r=[kdb], w=[krd])
                            P_dve.tensor_tensor(out=rD, in0=rD, in1=Osb[0:64, :], op=ALU.mult, r=[krd, kos], w=[krd])
                            kgs, Gst = Gstr.next()
                            P_pool.tensor_tensor(out=Gst, in0=rD, in1=gt, op=ALU.mult, r=[krd, kg], w=[kgs])
                            sch.dma(gt_d[row0:row0 + 64, c0:c0 + CH], Gst, r=[kgs], w=[])
                        pend_epi[0] = epi
                    else:
                        if pend_epi[0] is not None:
                            pend_epi[0]()
                            pend_epi[0] = None
                        kcp, Cp = Cpr.next()
                        P_pool.memset(Cp, 0.0, w=[kcp])
                        zz = {}
                        ll = {}

                        def act12(u):
                            kb_, jj, diag = units[u]
                            kbk, bk, lo = zz.pop(u)
                            ke, E = Er.next()
                            kl, Lp = Lr.next()
                            P_act.activation(out=E[:, lo:CH], in_=bk[:, lo:CH], func=AF.Exp, r=[kbk], w=[ke, kbk + "E"])
                            P_act.activation(out=Lp[:, lo:CH], in_=E[:, lo:CH], func=AF.Ln, bias=1.0, scale=1.0, r=[ke], w=[kl])
                            if diag:
                                P_dve.tensor_tensor(out=Lp[:, lo:lo + 128], in0=Lp[:, lo:lo + 128], in1=m01_strict, op=ALU.mult, r=[kl], w=[kl])
                            ll[u] = (kl, Lp, kbk, bk)

                        for u in range(min(2, nu)):
                            zz[u] = s_mm(u, False)
                        act12(0)
                        pend = None
                        for u in range(nu):
                            kb_, jj, diag = units[u]
                            lo = jj * 128
                            kl, Lp, kab, ab = ll.pop(u)
                            P_pe.matmul(ab[:, lo:CH], lhsT=negU, rhs=Lp[:, lo:CH], start=False, stop=not diag, skip_group_check=True, r=[kl, kab + "E"], w=[kab])
                            if diag:
                                P_pe.matmul(ab[:, lo:lo + 128], lhsT=ident_bf, rhs=mneg_strict, start=False, stop=True, skip_group_check=True,
                                            r=[], w=[kab])
                            kcs_, csb = short.next()
                            P_pe.matmul(csb[:, lo:CH], lhsT=ones_bf, rhs=Lp[:, lo:CH], start=True, stop=True, r=[kl], w=[kcs_])
                            for jn in junk_sb:
                                junk(jn)
                            if u + 2 < nu:
                                zz[u + 2] = s_mm(u + 2, False)
                            if u + 1 < nu:
                                act12(u + 1)
                            ka, A = Ar.next()
                            P_dve.tensor_tensor(out=A[:, lo:CH], in0=ab[:, lo:CH], in1=Cp[:, lo:CH], op=ALU.subtract, r=[kab, kcp], w=[ka])
                            P_dve.tensor_tensor(out=Cp[:, lo:CH], in0=csb[:, lo:CH], in1=Cp[:, lo:CH], op=ALU.add, r=[kcs_, kcp], w=[kcp])
                            if pend is not None:
                                pend()
                            kw, W = Wr.next()
                            P_act.activation(out=W[:, lo:CH], in_=A[:, lo:CH], func=AF.Exp, r=[ka], w=[kw])

                            def pv(W=W, kw=kw, lo=lo, kb_=kb_, u=u):
                                P_pe.matmul(oacc[0:64, lo:CH], lhsT=v[:, kb_, 0:64], rhs=W[:, lo:CH], start=(u == 0), stop=(u == nu - 1),
                                            skip_group_check=True, r=[kw, kv], w=[ko])
                            pend = pv
                        pend()
                        kgs, Gst = Gstr.next()
                        P_dve.tensor_tensor(out=Gst, in0=oacc[0:64, :], in1=gt, op=ALU.mult, r=[ko, kg], w=[kgs])
                        sch.dma(gt_d[row0:row0 + 64, c0:c0 + CH], Gst, r=[kgs], w=[])
            if pend_epi[0] is not None:
                pend_epi[0]()
                pend_epi[0] = None
            sch.barrier()
            ar.release(mW)

        sch.barrier()
        sch.emit(nc)
    if _os.environ.get('KPEAK'):
        print('arena peak words', ar.peak, 'of', NW)
    return nc, sch.nins


NHK = 4


def _consts():
    i = np.arange(128)
    ident = np.eye(128, dtype=np.float32)
    ones = np.ones((128, 128), np.float32)
    negU = -(i[:, None] >= i[None, :]).astype(np.float32)
    m_incl = np.where(i[:, None] <= i[None, :], 0.0, NEG).astype(np.float32)
    m_strict = np.where(i[:, None] < i[None, :], 0.0, NEG).astype(np.float32)
    m01 = (i[:, None] < i[None, :]).astype(np.float32)
    c_bf = np.concatenate([ident, ones, negU, m_incl, m_strict, m01], axis=1).astype(ml_dtypes.bfloat16)
    sel = np.zeros((65, 64), np.float32)
    sel[64, :] = 1.0
    half = 16
    inv_freq = (np.float32(10000.0) ** (-np.arange(half, dtype=np.float32) / np.float32(half))).astype(np.float32)
    ang = (np.arange(S, dtype=np.float32)[:, None] * inv_freq[None, :]).astype(np.float32)
    cos = np.cos(ang).astype(np.float32).T
    sin = np.sin(ang).astype(np.float32).T
    c_cos = np.ascontiguousarray(np.concatenate([cos] * NHK, axis=0))
    c_sin = np.ascontiguousarray(np.concatenate([sin] * NHK, axis=0))
    return dict(c_bf=np.ascontiguousarray(c_bf), c_id=ident, c_sel=sel, c_cos=c_cos, c_sin=c_sin)


def _rep(v):
    return np.ascontiguousarray(np.broadcast_to(np.asarray(v, np.float32)[None, :], (128, v.shape[0])))


def _w_in_cols():
    o_fq, o_fk, o_fv, o_fl = 0, 256, 512, 768
    o_sq, o_sk, o_sv = 772, 1028, 1284
    o_cq, o_ckv, o_kr = 1540, 1796, 1924
    o_mq, o_gate = 1956, 2212
    r = lambda a, n: list(range(a, a + n))
    cols = r(o_fq, 256) + r(o_fk, 256) + r(o_sq, 256) + r(o_sk, 256) + r(o_mq, 256) + r(o_gate, 1024)
    cols += r(o_cq, 256) + r(o_ckv, 128) + r(o_kr, 16) + r(o_kr + 16, 16) + r(o_fl, 4) + r(o_fv, 256) + r(o_sv, 256)
    assert len(cols) == 3236
    return np.array(cols)


def make_in_maps(x, mem, ln_in_g, ln_in_b, mem_ln_g, mem_ln_b, w_in, b_forget, mla_q_norm_g, w_mla_q_up,
                 mla_kv_norm_g, w_mla_kv_up, w_mem_kv, w_out, ln_g, ln_b):
    f = lambda a: np.ascontiguousarray(np.asarray(a, dtype=np.float32))
    cst = _consts()
    wcols = _w_in_cols()
    hs = range(4)
    qcols = [h * 96 + d for h in hs for d in range(64)] + [h * 96 + 64 + d for h in hs for d in range(16)] + \
            [h * 96 + 80 + d for h in hs for d in range(16)]
    kvcols = [h * 128 + d for h in hs for d in range(64)] + [h * 128 + 64 + d for h in hs for d in range(64)]
    shared = dict(mlng=_rep(f(mem_ln_g)), mlnb=_rep(f(mem_ln_b)), **cst)
    lgs = [f(ln_in_g)] + [f(ln_g)[l] for l in range(DEPTH)]
    lbs = [f(ln_in_b)] + [f(ln_b)[l] for l in range(DEPTH)]
    for i in range(DEPTH + 1):
        shared["lng%d" % i] = _rep(lgs[i])
        shared["lnb%d" % i] = _rep(lbs[i])
    for l in range(DEPTH):
        shared["w_in%d" % l] = f(f(w_in)[l][:, wcols])
        shared["b_f%d" % l] = f(f(b_forget)[l].reshape(4, 1))
        shared["gq%d" % l] = f(f(mla_q_norm_g)[l].reshape(2, 128).T)
        shared["gkv%d" % l] = f(f(mla_kv_norm_g)[l].reshape(128, 1))
        shared["w_qup%d" % l] = f(f(w_mla_q_up)[l][:, qcols])
        shared["w_kvup%d" % l] = f(f(w_mla_kv_up)[l][:, kvcols])
        shared["w_mem%d" % l] = f(f(w_mem_kv)[l])
        shared["w_out%d" % l] = f(f(w_out)[l])
    maps = []
    for b in range(4):
        m = dict(shared)
        m["x"] = f(x[b])
        m["mem"] = f(mem[b])
        maps.append(m)
    return maps


_PROG = {}


def kernel(x, mem, ln_in_g, ln_in_b, mem_ln_g, mem_ln_b, w_in, b_forget, mla_q_norm_g, w_mla_q_up,
           mla_kv_norm_g, w_mla_kv_up, w_mem_kv, w_out, ln_g, ln_b):
    if "fused" not in _PROG:
        _PROG["fused"] = build_fused()[0]
    nc = _PROG["fused"]
    maps = make_in_maps(x, mem, ln_in_g, ln_in_b, mem_ln_g, mem_ln_b, w_in, b_forget, mla_q_norm_g, w_mla_q_up,
                        mla_kv_norm_g, w_mla_kv_up, w_mem_kv, w_out, ln_g, ln_b)
    res = run_bass_kernel_spmd(nc, maps, core_ids=list(range(4))).results
    return np.stack([np.asarray(res[b]["out"]) for b in range(4)], axis=0).astype(np.float32)
```
